# Optimizing a Trainium2 kernel written in Bass

```python
import math
import jax, jax.numpy as jnp
from jax import lax
import numpy as np

D_MODEL = 1024
BATCH = 4
SEQ = 4096
DEPTH = 2

N_EVEN = (DEPTH + 1) // 2
N_ODD = DEPTH // 2
MEM_LEN = 256
CONV_DIM = D_MODEL
CONV_WIDTH = 31
CONV_PAD = (CONV_WIDTH - 1) // 2
MLA_HEADS = 8
QK_NOPE = 128
QK_ROPE = 64
V_DIM = 128
Q_LORA = 768
KV_LORA = 256
ROPE_THETA = 10000.0
Q_BLOCK = 128
MEM_HEADS = 4
MEM_HEAD_DIM = 128
MEM_DIM = MEM_HEADS * MEM_HEAD_DIM
S5_DIM = D_MODEL
S5_GROUP = 16
S5_GROUPS = S5_DIM // S5_GROUP
S5_STATE = 64
DT_MIN = 0.001
DT_MAX = 0.1
LN_EPS = 1e-5
RMS_EPS = 1e-6
ALPHA = (2 * DEPTH) ** 0.25
BETA = (8 * DEPTH) ** -0.25

EVEN_SPLITS = (2 * CONV_DIM, CONV_DIM, Q_LORA, KV_LORA, QK_ROPE, MLA_HEADS * V_DIM, MEM_DIM, MEM_DIM)
ODD_SPLITS = (S5_DIM, S5_DIM, MEM_DIM, MEM_DIM)
EVEN_IN = sum(EVEN_SPLITS)
ODD_IN = sum(ODD_SPLITS)
EVEN_MIX = CONV_DIM + MLA_HEADS * V_DIM + MEM_DIM
ODD_MIX = S5_DIM + MEM_DIM

kernel_name = "hybrid_conv_mla_s5_deepnorm_encoder"


def _split(h, sizes):
    idx = np.cumsum(sizes)[:-1].tolist()
    return jnp.split(h, idx, axis=-1)


def layer_norm(x, g, b):
    xf = x.astype(jnp.float32)
    mu = jnp.mean(xf, axis=-1, keepdims=True)
    xc = xf - mu
    var = jnp.mean(xc * xc, axis=-1, keepdims=True)
    return (xc * lax.rsqrt(var + LN_EPS) * g + b).astype(x.dtype)


def rms_norm(x, g):
    xf = x.astype(jnp.float32)
    ms = jnp.mean(xf * xf, axis=-1, keepdims=True)
    return (xf * lax.rsqrt(ms + RMS_EPS) * g).astype(x.dtype)


def rope_tables(positions):
    inv_freq = ROPE_THETA ** (-jnp.arange(0, QK_ROPE, 2, dtype=jnp.float32) / QK_ROPE)
    ang = positions.astype(jnp.float32)[..., None] * inv_freq
    return jnp.cos(ang), jnp.sin(ang)


def apply_rope(x, cos, sin):
    x1, x2 = jnp.split(x, 2, axis=-1)
    out = jnp.concatenate([x1 * cos - x2 * sin, x2 * cos + x1 * sin], axis=-1)
    return out.astype(x.dtype)


def mla_block_attention(q_nope, q_rope, k_nope, k_rope, v):
    bsz, seq, heads, _ = q_nope.shape
    nb = seq // Q_BLOCK
    scale = (QK_NOPE + QK_ROPE) ** -0.5
    qn = q_nope.reshape(bsz, nb, Q_BLOCK, heads, QK_NOPE).transpose(1, 0, 2, 3, 4)
    qr = q_rope.reshape(bsz, nb, Q_BLOCK, heads, QK_ROPE).transpose(1, 0, 2, 3, 4)

    def one_block(args):
        qn_b, qr_b = args
        s = (jnp.einsum('bqhd,bkhd->bhqk', qn_b, k_nope).astype(jnp.float32)
             + jnp.einsum('bqhr,bkr->bhqk', qr_b, k_rope).astype(jnp.float32)) * scale
        p = jax.nn.softmax(s, axis=-1).astype(v.dtype)
        return jnp.einsum('bhqk,bkhd->bqhd', p, v)

    o = lax.map(one_block, (qn, qr))
    return o.transpose(1, 0, 2, 3, 4).reshape(bsz, seq, heads * V_DIM)


def memory_attention(q, mem, w_mem_kv):
    bsz, seq, _ = q.shape
    kv = mem @ w_mem_kv
    k, v = jnp.split(kv, 2, axis=-1)
    k = k.reshape(bsz, -1, MEM_HEADS, MEM_HEAD_DIM)
    v = v.reshape(bsz, -1, MEM_HEADS, MEM_HEAD_DIM)
    qh = q.reshape(bsz, seq, MEM_HEADS, MEM_HEAD_DIM)
    s = jnp.einsum('bshd,bmhd->bhsm', qh, k).astype(jnp.float32) * (MEM_HEAD_DIM ** -0.5)
    p = jax.nn.softmax(s, axis=-1).astype(v.dtype)
    return jnp.einsum('bhsm,bmhd->bshd', p, v).reshape(bsz, seq, MEM_DIM)


def _cplx_combine(e1, e2):
    ar1, ai1, br1, bi1 = e1
    ar2, ai2, br2, bi2 = e2
    return (ar2 * ar1 - ai2 * ai1,
            ar2 * ai1 + ai2 * ar1,
            ar2 * br1 - ai2 * bi1 + br2,
            ar2 * bi1 + ai2 * br1 + bi2)


def s5_direction(u, a_re, a_im, log_dt, b_re, b_im, c_re, c_im, reverse):
    f32 = jnp.float32
    lam_re = jnp.minimum(a_re.astype(f32), -1e-4)
    lam_im = a_im.astype(f32)
    dt = jnp.exp(log_dt.astype(f32))[:, None]
    mag = jnp.exp(lam_re * dt)
    lb_re = mag * jnp.cos(lam_im * dt)
    lb_im = mag * jnp.sin(lam_im * dt)
    den = lam_re * lam_re + lam_im * lam_im
    nr = lb_re - 1.0
    f_re = (nr * lam_re + lb_im * lam_im) / den
    f_im = (lb_im * lam_re - nr * lam_im) / den
    br = b_re.astype(f32)
    bi = b_im.astype(f32)
    bb_re = f_re[..., None] * br - f_im[..., None] * bi
    bb_im = f_re[..., None] * bi + f_im[..., None] * br
    bu_re = jnp.einsum('bsgc,gpc->bsgp', u, bb_re)
    bu_im = jnp.einsum('bsgc,gpc->bsgp', u, bb_im)
    seq = u.shape[1]
    a_re_seq = jnp.broadcast_to(lb_re, (1, seq) + lb_re.shape)
    a_im_seq = jnp.broadcast_to(lb_im, (1, seq) + lb_im.shape)
    _, _, x_re, x_im = lax.associative_scan(
        _cplx_combine, (a_re_seq, a_im_seq, bu_re, bu_im), reverse=reverse, axis=1)
    return (jnp.einsum('bsgp,gcp->bsgc', x_re, c_re.astype(f32))
            - jnp.einsum('bsgp,gcp->bsgc', x_im, c_im.astype(f32)))


def even_layer(x, mem, cos, sin, w_in, conv_w, conv_b, conv_ln_g, conv_ln_b,
               q_norm, w_uq, kv_norm, w_ukv, w_mem_kv, w_out, ln_g, ln_b):
    bsz, seq, _ = x.shape
    h = x @ w_in
    conv_in, conv_gate, c_q, c_kv, k_rope, mla_gate, mem_q, mem_gate = _split(h, EVEN_SPLITS)

    glu = conv_in[..., :CONV_DIM] * jax.nn.sigmoid(conv_in[..., CONV_DIM:])
    dw = lax.conv_general_dilated(
        glu, conv_w, window_strides=(1,), padding=[(CONV_PAD, CONV_PAD)],
        dimension_numbers=('NWC', 'WIO', 'NWC'), feature_group_count=CONV_DIM) + conv_b
    a_out = jax.nn.silu(layer_norm(dw, conv_ln_g, conv_ln_b)) * jax.nn.silu(conv_gate)

    q = (rms_norm(c_q, q_norm) @ w_uq).reshape(bsz, seq, MLA_HEADS, QK_NOPE + QK_ROPE)
    q_nope, q_rope = q[..., :QK_NOPE], q[..., QK_NOPE:]
    q_rope = apply_rope(q_rope, cos[:, :, None, :], sin[:, :, None, :])
    kv = (rms_norm(c_kv, kv_norm) @ w_ukv).reshape(bsz, seq, MLA_HEADS, QK_NOPE + V_DIM)
    k_nope, v = kv[..., :QK_NOPE], kv[..., QK_NOPE:]
    k_rope = apply_rope(k_rope, cos, sin)
    b_out = mla_block_attention(q_nope, q_rope, k_nope, k_rope, v) * jax.nn.silu(mla_gate)

    m_out = memory_attention(mem_q, mem, w_mem_kv) * jax.nn.silu(mem_gate)

    y = jnp.concatenate([a_out, b_out, m_out], axis=-1) @ w_out
    return layer_norm(ALPHA * x + y, ln_g, ln_b)


def odd_layer(x, mem, w_in, s5_fwd, s5_bwd, s5_d, w_glu, w_mem_kv, w_out, ln_g, ln_b):
    bsz, seq, _ = x.shape
    h = x @ w_in
    s5_u, s5_gate, mem_q, mem_gate = _split(h, ODD_SPLITS)

    u = s5_u.astype(jnp.float32).reshape(bsz, seq, S5_GROUPS, S5_GROUP)
    y = s5_direction(u, *s5_fwd, reverse=False) + s5_direction(u, *s5_bwd, reverse=True)
    y = y.reshape(bsz, seq, S5_DIM) + s5_d.astype(jnp.float32) * s5_u.astype(jnp.float32)
    z = jax.nn.gelu(y).astype(x.dtype) @ w_glu
    c_out = (z[..., :S5_DIM] * jax.nn.sigmoid(z[..., S5_DIM:])) * jax.nn.silu(s5_gate)

    m_out = memory_attention(mem_q, mem, w_mem_kv) * jax.nn.silu(mem_gate)

    out = jnp.concatenate([c_out, m_out], axis=-1) @ w_out
    return layer_norm(ALPHA * x + out, ln_g, ln_b)


def setup_inputs(seed: int = 0) -> dict:
    key = jax.random.key(seed)
    ks = iter(jax.random.split(key, 48))
    f32 = jnp.float32

    def nrm(shape, std):
        return jax.random.normal(next(ks), shape, f32) * std

    def gain(shape):
        return 1.0 + nrm(shape, 0.01)

    x = nrm((BATCH, SEQ, D_MODEL), 1.0)
    mem = nrm((BATCH, MEM_LEN, D_MODEL), 1.0)
    offset = jax.random.randint(next(ks), (BATCH, 1), 0, SEQ, dtype=jnp.int32)
    positions = (jnp.arange(SEQ, dtype=jnp.int32)[None, :] + offset).astype(jnp.int32)

    E, O, G, P, C = N_EVEN, N_ODD, S5_GROUPS, S5_STATE, S5_GROUP
    inp = {
        "x": x, "mem": mem, "positions": positions,
        "e_w_in": nrm((E, D_MODEL, EVEN_IN), D_MODEL ** -0.5),
        "e_conv_w": nrm((E, CONV_WIDTH, 1, CONV_DIM), CONV_WIDTH ** -0.5),
        "e_conv_b": nrm((E, CONV_DIM), 0.01),
        "e_conv_ln_g": gain((E, CONV_DIM)),
        "e_conv_ln_b": nrm((E, CONV_DIM), 0.01),
        "e_q_norm": gain((E, Q_LORA)),
        "e_w_uq": nrm((E, Q_LORA, MLA_HEADS * (QK_NOPE + QK_ROPE)), Q_LORA ** -0.5),
        "e_kv_norm": gain((E, KV_LORA)),
        "e_w_ukv": nrm((E, KV_LORA, MLA_HEADS * (QK_NOPE + V_DIM)), KV_LORA ** -0.5),
        "e_mem_kv": nrm((E, D_MODEL, 2 * MEM_DIM), D_MODEL ** -0.5),
        "e_w_out": nrm((E, EVEN_MIX, D_MODEL), BETA * EVEN_MIX ** -0.5),
        "e_ln_g": gain((E, D_MODEL)),
        "e_ln_b": nrm((E, D_MODEL), 0.01),
        "o_w_in": nrm((O, D_MODEL, ODD_IN), D_MODEL ** -0.5),
    }
    a_im_init = jnp.broadcast_to(math.pi * jnp.arange(P, dtype=f32), (O, G, P))
    for d in ("f", "b"):
        inp["o_a_re_" + d] = -0.5 + nrm((O, G, P), 0.01)
        inp["o_a_im_" + d] = a_im_init + nrm((O, G, P), 0.01)
        inp["o_log_dt_" + d] = jax.random.uniform(next(ks), (O, G), f32,
                                                  math.log(DT_MIN), math.log(DT_MAX))
        inp["o_b_re_" + d] = nrm((O, G, P, C), (2.0 * C) ** -0.5)
        inp["o_b_im_" + d] = nrm((O, G, P, C), (2.0 * C) ** -0.5)
        inp["o_c_re_" + d] = nrm((O, G, C, P), (2.0 * P) ** -0.5)
        inp["o_c_im_" + d] = nrm((O, G, C, P), (2.0 * P) ** -0.5)
    inp["o_d"] = nrm((O, S5_DIM), 1.0)
    inp["o_w_glu"] = nrm((O, S5_DIM, 2 * S5_DIM), S5_DIM ** -0.5)
    inp["o_mem_kv"] = nrm((O, D_MODEL, 2 * MEM_DIM), D_MODEL ** -0.5)
    inp["o_w_out"] = nrm((O, ODD_MIX, D_MODEL), BETA * ODD_MIX ** -0.5)
    inp["o_ln_g"] = gain((O, D_MODEL))
    inp["o_ln_b"] = nrm((O, D_MODEL), 0.01)
    return inp


def reference(x, mem, positions,
              e_w_in, e_conv_w, e_conv_b, e_conv_ln_g, e_conv_ln_b, e_q_norm, e_w_uq,
              e_kv_norm, e_w_ukv, e_mem_kv, e_w_out, e_ln_g, e_ln_b,
              o_w_in,
              o_a_re_f, o_a_im_f, o_log_dt_f, o_b_re_f, o_b_im_f, o_c_re_f, o_c_im_f,
              o_a_re_b, o_a_im_b, o_log_dt_b, o_b_re_b, o_b_im_b, o_c_re_b, o_c_im_b,
              o_d, o_w_glu, o_mem_kv, o_w_out, o_ln_g, o_ln_b):
    cos, sin = rope_tables(positions)
    h = x
    for layer in range(DEPTH):
        i = layer // 2
        if layer % 2 == 0:
            h = even_layer(h, mem, cos, sin, e_w_in[i], e_conv_w[i], e_conv_b[i],
                           e_conv_ln_g[i], e_conv_ln_b[i], e_q_norm[i], e_w_uq[i],
                           e_kv_norm[i], e_w_ukv[i], e_mem_kv[i], e_w_out[i],
                           e_ln_g[i], e_ln_b[i])
        else:
            s5_fwd = (o_a_re_f[i], o_a_im_f[i], o_log_dt_f[i], o_b_re_f[i], o_b_im_f[i],
                      o_c_re_f[i], o_c_im_f[i])
            s5_bwd = (o_a_re_b[i], o_a_im_b[i], o_log_dt_b[i], o_b_re_b[i], o_b_im_b[i],
                      o_c_re_b[i], o_c_im_b[i])
            h = odd_layer(h, mem, o_w_in[i], s5_fwd, s5_bwd, o_d[i], o_w_glu[i],
                          o_mem_kv[i], o_w_out[i], o_ln_g[i], o_ln_b[i])
    return h
```

```python
import contextlib
import math
import numpy as np
import concourse.bass as bass
import concourse.mybir as mybir
from concourse.bass_utils import run_bass_kernel_spmd

F32 = mybir.dt.float32
BF16 = mybir.dt.bfloat16
I32 = mybir.dt.int32
AF = mybir.ActivationFunctionType
ALU = mybir.AluOpType

D_MODEL = 1024
SEQ = 4096
BATCH = 4
NT = 2048
ALPHA = (2 * 2) ** 0.25
LN_EPS = 1e-5
RMS_EPS = 1e-6
TWO_PI = 2.0 * math.pi


class Buf:
    __slots__ = ("name", "lw", "rd", "excl")

    def __init__(self, name, excl=False):
        self.name = name
        self.excl = excl
        self.lw = None
        self.rd = {}


class Sched:
    COMPUTE = ("pe", "act", "dve", "pool")

    def __init__(self, nc, stack, n_dma_sems=12, same_engine_sync=True):
        self.nc = nc
        self.eng = {"pe": nc.tensor, "act": nc.scalar, "dve": nc.vector,
                    "pool": nc.gpsimd, "sp": nc.sync}
        self.sems = {}
        for e in self.COMPUTE:
            self.sems[e] = stack.enter_context(nc.semaphore("sem_" + e))
        self.cnt = {e: 0 for e in self.COMPUTE}
        self.dma_ring = {}
        for q in ("sp", "pool", "act"):
            n = n_dma_sems if q != "act" else 4
            ring = []
            for i in range(n):
                key = "dma_%s_%d" % (q, i)
                self.sems[key] = stack.enter_context(nc.semaphore(key))
                ring.append(key)
            self.dma_ring[q] = {"keys": ring, "cnt": [0] * n, "next": 0}
        self.sems["cc"] = stack.enter_context(nc.semaphore("cc_sem"))
        self.cc_cnt = 0
        self.cc_dummy = stack.enter_context(nc.sbuf_tensor("cc_dummy", [128, 8], F32))[:]
        self.seen = {e: {} for e in ("pe", "act", "dve", "pool", "sp")}
        self.prog = {e: [] for e in ("pe", "act", "dve", "pool", "sp")}
        self.same_engine_sync = same_engine_sync
        self.ninst = 0

    def _wait(self, e, semkey, val):
        if self.seen[e].get(semkey, 0) >= val:
            return
        self.seen[e][semkey] = val
        self.prog[e].append(("wait", semkey, val))

    def _deps(self, e, reads, writes):
        deps = {}

        def add(ev):
            if ev is None:
                return
            k, v = ev
            if deps.get(k, 0) < v:
                deps[k] = v
        for b in reads:
            add(b.lw)
            if b.excl:
                for k, v in b.rd.items():
                    if k != e:
                        add((k, v))
        for b in writes:
            add(b.lw)
            for k, v in b.rd.items():
                add((k, v))
        return deps

    def op(self, e, fn, reads=(), writes=()):
        deps = self._deps(e, reads, writes)
        for k, v in deps.items():
            if k == e:
                if e == "pe" or not self.same_engine_sync:
                    continue
            self._wait(e, k, v)
        self.cnt[e] += 1
        ev = (e, self.cnt[e])
        self.prog[e].append(("inst", fn, e, 1))
        for b in writes:
            b.lw = ev
            b.rd = {}
        for b in reads:
            if b.rd.get(e, 0) < ev[1]:
                b.rd[e] = ev[1]
        self.ninst += 1
        return ev

    def dma(self, q, out, in_, reads=(), writes=(), **kw):
        ring = self.dma_ring[q]
        s = ring["next"]
        ring["next"] = (s + 1) % len(ring["keys"])
        key = ring["keys"][s]
        deps = self._deps(q, reads, writes)
        for k, v in deps.items():
            self._wait(q, k, v)
        if ring["cnt"][s] > 0:
            self._wait(q, key, 16 * ring["cnt"][s])
        ring["cnt"][s] += 1
        ev = (key, 16 * ring["cnt"][s])

        def fn(eng, out=out, in_=in_, kw=kw):
            return eng.dma_start(out=out, in_=in_, **kw)
        self.prog[q].append(("inst", fn, key, 16))
        if q in self.COMPUTE:
            pass
        for b in writes:
            b.lw = ev
            b.rd = {}
        for b in reads:
            if b.rd.get(key, 0) < ev[1]:
                b.rd[key] = ev[1]
        self.ninst += 1
        return ev

    def collective(self, kind, op, groups, in_ap, out_ap, reads=(), writes=()):
        if "cc" not in self.sems:
            raise RuntimeError("no cc semaphore")
        deps = self._deps("pool", reads, writes)
        for k, v in deps.items():
            self._wait("pool", k, v)
        self.cc_cnt += 1
        ev = ("cc", self.cc_cnt)

        def fn(eng):
            return eng.collective_compute(kind, op, replica_groups=groups, ins=[in_ap.opt()], outs=[out_ap.opt()])
        self.prog["pool"].append(("inst", fn, "cc", 1))
        self._wait("pool", "cc", self.cc_cnt)
        dummy = self.cc_dummy
        return self.op("pool", lambda e: e.memset(dummy, 0.0), reads=(), writes=list(writes))

    def mm(self, out, lhsT, rhs, start, stop, R=(), W=()):
        return self.op("pe", lambda e: e.matmul(out, lhsT, rhs, start=start, stop=stop), R, W)

    def act(self, out, in_, func, R=(), W=(), bias=None, scale=None, eng="act"):
        kw = {}
        if bias is not None:
            kw["bias"] = bias
        if scale is not None:
            kw["scale"] = scale
        return self.op("act", lambda e: e.activation(out, in_, func, **kw), R, W)

    def tt(self, eng, out, in0, in1, op, R=(), W=()):
        return self.op(eng, lambda e: e.tensor_tensor(out, in0, in1, op), R, W)

    def ts(self, eng, out, in0, s1, s2, op0, op1=None, R=(), W=()):
        if op1 is None:
            return self.op(eng, lambda e: e.tensor_scalar(out, in0, s1, None, op0), R, W)
        return self.op(eng, lambda e: e.tensor_scalar(out, in0, s1, s2, op0, op1), R, W)

    def stt(self, out, in0, scalar, in1, op0, op1, R=(), W=()):
        return self.op("dve", lambda e: e.scalar_tensor_tensor(out, in0, scalar, in1, op0, op1), R, W)

    def copy(self, eng, out, in_, R=(), W=()):
        if eng == "act":
            return self.op("act", lambda e: e.copy(out, in_), R, W)
        return self.op(eng, lambda e: e.tensor_copy(out, in_), R, W)

    def recip(self, out, in_, R=(), W=()):
        return self.op("dve", lambda e: e.reciprocal(out, in_), R, W)

    def barrier(self):
        evs = [(e, self.cnt[e]) for e in self.COMPUTE if self.cnt[e] > 0]
        for q, ring in self.dma_ring.items():
            for key, c in zip(ring["keys"], ring["cnt"]):
                if c > 0:
                    evs.append((key, 16 * c))
        for e in ("pe", "act", "dve", "pool", "sp"):
            for k, v in evs:
                if k == e:
                    continue
                self._wait(e, k, v)

    def wait_all_on(self, e, bufs):
        for b in bufs:
            if b.lw is not None:
                self._wait(e, b.lw[0], b.lw[1])

    def emit(self):
        nc = self.nc
        sems = self.sems
        prog = self.prog
        with nc.Block() as block:
            def make(e):
                def body(eng):
                    for item in prog[e]:
                        if item[0] == "wait":
                            eng.wait_ge(sems[item[1]], item[2])
                        else:
                            _, fn, key, inc = item
                            fn(eng).then_inc(sems[key], inc)
                return body
            block.tensor(make("pe"))
            block.scalar(make("act"))
            block.vector(make("dve"))
            block.gpsimd(make("pool"))
            block.sync(make("sp"))


STOP = [99]


class StopBuild(Exception):
    pass


def stop_at(level):
    if STOP[0] <= level:
        raise StopBuild()


class Ctx:
    def __init__(self, nc, stack):
        self.nc = nc
        self.stack = stack
        self.S = Sched(nc, stack)
        self.banks = []
        for i in range(8):
            t = stack.enter_context(nc.psum_tensor("bank%d" % i, [128, 512], F32))
            self.banks.append((t, Buf("bank%d" % i, excl=True)))
        self.rr = {}
        self.uid = 0

    def sb(self, stack, name, shape, dt):
        self.uid += 1
        return stack.enter_context(self.nc.sbuf_tensor("%s_%d" % (name, self.uid), shape, dt))

    def bank(self, pool):
        i = self.rr.get(pool, 0)
        self.rr[pool] = (i + 1) % len(pool)
        return self.banks[pool[i]]


class Ring:
    def __init__(self, cx, stack, name, shape, dt, n):
        self.tiles = [(cx.sb(stack, name, shape, dt), Buf(name + str(i))) for i in range(n)]
        self.i = 0

    def next(self):
        t = self.tiles[self.i]
        self.i = (self.i + 1) % len(self.tiles)
        return t


PA = (0, 1, 2, 3)
PB = (4, 5)
PC = (6, 7)

E_CONV_IN, E_CONV_GATE, E_CQ, E_CKV, E_KR, E_MLAG, E_MEMQ, E_MEMG = 0, 2048, 3072, 3840, 4096, 4160, 5184, 5696


def wview(w, r0, nrows, c0, ncols):
    return w[r0:r0 + nrows, c0:c0 + ncols].rearrange("(kc p) n -> p kc n", p=128)


def load_consts(cx, st, d):
    S = cx.S
    c = {}
    c["idf"] = cx.sb(st, "idf", [128, 128], F32)
    c["idb"] = cx.sb(st, "idb", [128, 128], BF16)
    c["onef"] = cx.sb(st, "onef", [128, 128], F32)
    c["oneb"] = cx.sb(st, "oneb", [128, 128], BF16)
    c["B"] = Buf("consts")
    S.dma("sp", c["idf"][:], d["ident"], reads=[], writes=[c["B"]])
    S.dma("pool", c["idb"][:], d["ident"], reads=[], writes=[c["B"]])
    if "rident" in d:
        c["ridb"] = cx.sb(st, "ridb", [128, 128], BF16)
        S.dma("pool", c["ridb"][:], d["rident"], reads=[], writes=[c["B"]])
    S.op("dve", lambda e: e.memset(c["onef"][:], 1.0), writes=[c["B"]])
    S.op("dve", lambda e: e.memset(c["oneb"][:], 1.0), writes=[c["B"]])
    c["eps"] = {}
    for v in (RMS_EPS, LN_EPS):
        t = cx.sb(st, "eps", [128, 1], F32)
        S.op("dve", lambda e, t=t, v=v: e.memset(t[:], float(v)), writes=[c["B"]])
        c["eps"][v] = t[:]
    return c


def transpose_rows(cx, xr, xrB, nrows_last, ntiles, xT, xTB, consts, col0=0, rev=False):
    S = cx.S
    idb = consts["ridb"] if rev else consts["idb"]
    full = ntiles if nrows_last == 128 else ntiles - 1
    k = 0
    for kc in range(8):
        for t0 in range(0, full, 4):
            nt = min(4, full - t0)
            bk, bB = cx.bank(PB)
            for t in range(nt):
                srct = (ntiles - 1 - (t0 + t)) if rev else (t0 + t)
                S.mm(bk[:, t * 128:(t + 1) * 128], xr[:, srct, kc * 128:(kc + 1) * 128], idb[:],
                     True, True, R=[xrB, consts["B"]], W=[bB])
            eng = "act" if k % 2 == 0 else "dve"
            k += 1
            S.copy(eng, xT[:, kc, col0 + t0 * 128: col0 + (t0 + nt) * 128], bk[:, 0:nt * 128], R=[bB], W=[xTB])
        if nrows_last != 128:
            bk, bB = cx.bank(PB)
            n = nrows_last
            S.mm(bk[:, 0:n], xr[0:n, ntiles - 1, kc * 128:(kc + 1) * 128], idb[0:n, 0:n], True, True,
                 R=[xrB, consts["B"]], W=[bB])
            S.copy("dve", xT[:, kc, col0 + full * 128: col0 + full * 128 + n], bk[:, 0:n], R=[bB], W=[xTB])


def attention_block(cx, kts, qparts, v_of, nk, scale, pring, consts, out_cb, acc=None, q_alt=None):
    S = cx.S
    bo, bO = cx.banks[6]
    bs, bS = cx.banks[7]
    oneb = consts["oneb"]

    def scores(kc):
        bk, bB = cx.bank(PA)
        parts = kts(kc)
        for i, ((l, lb), (r, rb)) in enumerate(zip(parts, qparts)):
            S.mm(bk[:], l, r, i == 0, i == len(parts) - 1, R=list(lb) + list(rb), W=[bB])
        return [(bk, bB)]

    def scores_pair(kc):
        res = []
        pp = []
        for k2 in (kc, kc + 1):
            bk, bB = cx.bank(PA)
            parts = kts(k2)
            (l, lb), (r, rb) = parts[0], qparts[0]
            S.mm(bk[:], l, r, True, False, R=list(lb) + list(rb), W=[bB])
            res.append((bk, bB))
            pp.append(parts[1])
        for j, (l, lb) in enumerate(pp):
            r, rb = qparts[1] if j == 0 else q_alt
            S.mm(res[j][0][:], l, r, False, True, R=list(lb) + list(rb), W=[res[j][1]])
        return res
    step = 2 if q_alt is not None else 1
    fn = scores_pair if q_alt is not None else scores
    nxt = fn(0)
    for kc0 in range(0, nk, step):
        cur = nxt
        if kc0 + step < nk:
            nxt = fn(kc0 + step)
        for j, (bk, bB) in enumerate(cur):
            kc = kc0 + j
            p, pB = pring.next()
            S.act(p[:], bk[:], AF.Exp, R=[bB], W=[pB], scale=scale)
            v, vb = v_of(kc)
            S.mm(bo[:], v, p[:], kc == 0, kc == nk - 1, R=list(vb) + [pB], W=[bO])
            if acc is None:
                S.mm(bs[:], oneb[:], p[:], kc == 0, kc == nk - 1, R=[consts["B"], pB], W=[bS])
            elif kc % 3 == 2:
                S.mm(bs[:], oneb[:], p[:], kc == 2, False, R=[consts["B"], pB], W=[bS])
            elif kc == 0:
                S.copy("dve", acc[0][:], p[:], R=[pB], W=[acc[1]])
            else:
                S.tt("dve", acc[0][:], acc[0][:], p[:], ALU.add, R=[acc[1], pB], W=[acc[1]])
    if acc is not None:
        S.mm(bs[:], consts["onef"][:], acc[0][:], False, True, R=[consts["B"], acc[1]], W=[bS])
    out_cb(bo, bO, bs, bS)


def build_even(nc, cx, d, consts):
    S = cx.S
    dB = Buf("dram_in")
    yB = d.get("yB") or Buf("y")
    w_in = d["w_in"]

    try:
        _build_even_body(nc, cx, d, consts, S, dB, yB, w_in)
    except StopBuild:
        pass
    S.barrier()
    S.wait_all_on("sp", [yB])


def _build_even_body(nc, cx, d, consts, S, dB, yB, w_in):
    with contextlib.ExitStack() as stA:
        omla = cx.sb(stA, "omla", [128, 8, NT], BF16)
        omlaB = [Buf("omla%d" % h) for h in range(8)]
        par = cx.sb(stA, "par", [128, 64], F32)
        parB = Buf("par")
        S.dma("sp", par[:, 0:8], d["conv_b"].rearrange("(c p) -> p c", p=128), reads=[dB], writes=[parB], allow_slow_non_contiguous=True)
        S.dma("sp", par[:, 8:16], d["conv_ln_g"].rearrange("(c p) -> p c", p=128), reads=[dB], writes=[parB], allow_slow_non_contiguous=True)
        S.dma("sp", par[:, 16:24], d["conv_ln_b"].rearrange("(c p) -> p c", p=128), reads=[dB], writes=[parB], allow_slow_non_contiguous=True)
        S.dma("sp", par[:, 24:30], d["q_norm"].rearrange("(c p) -> p c", p=128), reads=[dB], writes=[parB], allow_slow_non_contiguous=True)
        S.dma("sp", par[:, 30:32], d["kv_norm"].rearrange("(c p) -> p c", p=128), reads=[dB], writes=[parB], allow_slow_non_contiguous=True)
        S.dma("sp", par[0:64, 32:34], d["ropec"], reads=[dB], writes=[parB])

        with contextlib.ExitStack() as st1:
            ckvn = cx.sb(st1, "ckvn", [128, 2, SEQ], BF16)
            ckvnB = [Buf("ckvn%d" % i) for i in range(8)]
            krt = cx.sb(st1, "krt", [128, SEQ], BF16)
            krt2B = Buf("krt2")
            krtB = [Buf("krt%d" % i) for i in range(8)]
            cqg = cx.sb(st1, "cqg", [128, 6, NT], BF16)
            cqgB = [Buf("cqg%d" % i) for i in range(4)]
            rq = cx.sb(st1, "rq", [128, NT], F32)
            csq = cx.sb(st1, "csq", [64, NT], F32)
            snq = cx.sb(st1, "snq", [64, NT], F32)
            rqB = [Buf("rq%d" % i) for i in range(4)]
            posi = cx.sb(st1, "posi", [64, SEQ], I32)
            posB = Buf("posi")
            S.dma("sp", posi[:], d["pos_kv"].partition_broadcast(64), reads=[dB], writes=[posB])

            stop_at(1)
            with contextlib.ExitStack() as stK:
                wk = cx.sb(stK, "wk", [128, 8, 320], BF16)
                wks = cx.sb(stK, "wks", [128, 8, 64], BF16)
                wq = cx.sb(stK, "wq", [128, 8, 768], BF16)
                wB = Buf("wK")
                S.dma("pool", wk[:], wview(w_in, 0, 1024, E_CKV, 320), reads=[dB], writes=[wB])
                S.dma("pool", wks[:, :, 0:32], wview(w_in, 0, 1024, E_KR + 32, 32), reads=[dB], writes=[wB])
                S.dma("pool", wks[:, :, 32:64], wview(w_in, 0, 1024, E_KR, 32), reads=[dB], writes=[wB])
                for j in range(2):
                    S.dma("pool", wq[:, :, j * 384:(j + 1) * 384], wview(w_in, 0, 1024, E_CQ + j * 384, 384), reads=[dB], writes=[wB])
                xr_ring = Ring(cx, stK, "xr", [128, 4, 1024], BF16, 2)
                xT_ring = Ring(cx, stK, "xT", [128, 8, 512], BF16, 2)
                sq_ring = Ring(cx, stK, "sq", [128, 512], F32, 3)
                rr_ring = Ring(cx, stK, "rr", [128, 512], F32, 2)
                tmpf = [(cx.sb(stK, "tmpf", [128, 512], F32), Buf("tmpf%d" % i)) for i in range(3)]
                tmpi = (cx.sb(stK, "tmpi", [64, 512], I32), Buf("tmpi"))
                posf = (cx.sb(stK, "posf", [64, 512], F32), Buf("posf"))
                cs_t = (cx.sb(stK, "cs_t", [64, 512], F32), Buf("cs_t"))
                sn_t = (cx.sb(stK, "sn_t", [64, 512], F32), Buf("sn_t"))
                for tb in range(8):
                    own = tb < 4
                    if tb == 1:
                        stop_at(2)
                    xr, xrB = xr_ring.next()
                    S.dma("pool", xr[:], d["x_kv"][tb * 512:(tb + 1) * 512, :].rearrange("(t p) c -> p t c", p=128),
                          reads=[dB], writes=[xrB])
                    xT, xTB = xT_ring.next()
                    stop_at(1.1)
                    transpose_rows(cx, xr, xrB, 128, 4, xT, xTB, consts)
                    cols = slice(tb * 512, (tb + 1) * 512)
                    stop_at(1.2)
                    S.copy("dve", posf[0][:], posi[:, cols], R=[posB], W=[posf[1]])
                    _rope_blk(cx, posf, cs_t, sn_t, par, parB, tmpf, tmpi)
                    stop_at(1.3)
                    kvb = []
                    for fc in range(2):
                        bk, bB = cx.bank(PA)
                        for kc in range(8):
                            S.mm(bk[:], wk[:, kc, fc * 128:(fc + 1) * 128], xT[:, kc, :], kc == 0, kc == 7, R=[wB, xTB], W=[bB])
                        kvb.append((bk, bB))
                    bs, bS = cx.bank(PC)
                    for fc in range(2):
                        sq, sqB = sq_ring.next()
                        S.act(sq[:], kvb[fc][0][:], AF.Square, R=[kvb[fc][1]], W=[sqB])
                        S.mm(bs[:], consts["onef"][:], sq[:], fc == 0, fc == 1, R=[consts["B"], sqB], W=[bS])
                    r1, r1B = rr_ring.next()
                    _sqrt_eps(cx, r1, r1B, bs, bS, 1.0 / 256.0, RMS_EPS, consts)
                    S.recip(r1[:], r1[:], R=[r1B], W=[r1B])
                    for fc in range(2):
                        S.stt(ckvn[:, fc, cols], kvb[fc][0][:], par[:, 30 + fc:31 + fc], r1[:], ALU.mult, ALU.mult,
                              R=[kvb[fc][1], parB, r1B], W=[ckvnB[tb]])
                    stop_at(1.4)
                    ba, bA = cx.bank(PA)
                    for kc in range(8):
                        S.mm(ba[0:64, :], wk[:, kc, 256:320], xT[:, kc, :], kc == 0, kc == 7, R=[wB, xTB], W=[bA])
                    bb, bBb = cx.bank(PA)
                    for kc in range(8):
                        S.mm(bb[0:64, :], wks[:, kc, :], xT[:, kc, :], kc == 0, kc == 7, R=[wB, xTB], W=[bBb])
                    t1, t1B = tmpf[0]
                    t2, t2B = tmpf[1]
                    S.tt("dve", t1[0:64, :], ba[0:64, :], cs_t[0][:], ALU.mult, R=[bA, cs_t[1]], W=[t1B])
                    S.tt("dve", t2[0:64, :], bb[0:64, :], sn_t[0][:], ALU.mult, R=[bBb, sn_t[1]], W=[t2B])
                    S.tt("dve", krt[0:64, cols], t1[0:64, :], t2[0:64, :], ALU.add, R=[t1B, t2B], W=[krtB[tb]])
                    stop_at(1.5)
                    if own:
                        bs, bS = cx.bank(PC)
                        for fc in range(6):
                            bk, bB = cx.bank(PA)
                            for kc in range(8):
                                S.mm(bk[:], wq[:, kc, fc * 128:(fc + 1) * 128], xT[:, kc, :], kc == 0, kc == 7, R=[wB, xTB], W=[bB])
                            sq, sqB = sq_ring.next()
                            S.act(sq[:], bk[:], AF.Square, R=[bB], W=[sqB])
                            S.ts("dve", cqg[:, fc, cols], bk[:], par[:, 24 + fc:25 + fc], None, ALU.mult, R=[bB, parB], W=[cqgB[tb]])
                            S.mm(bs[:], consts["onef"][:], sq[:], fc == 0, fc == 5, R=[consts["B"], sqB], W=[bS])
                        _sqrt_eps(cx, rq[:, cols], rqB[tb], bs, bS, 1.0 / 768.0, RMS_EPS, consts, is_ap=True)
                        S.recip(rq[:, cols], rq[:, cols], R=[rqB[tb]], W=[rqB[tb]])
                        S.tt("dve", csq[:, cols], cs_t[0][:], rq[0:64, cols], ALU.mult, R=[cs_t[1], rqB[tb]], W=[rqB[tb]])
                        S.tt("dve", snq[:, cols], sn_t[0][:], rq[0:64, cols], ALU.mult, R=[sn_t[1], rqB[tb]], W=[rqB[tb]])
            S.dma("sp", krt[64:128, :], krt[0:64, :], reads=krtB, writes=[krt2B])
            S.barrier()
            stop_at(3)
            if "w_in_bf" in d:
                for kc in range(8):
                    rows = slice(kc * 128, (kc + 1) * 128)
                    for c0 in range(0, 6208, 2048):
                        n = min(2048, 6208 - c0)
                        S.dma("pool", d["w_in_bf"][rows, c0:c0 + n], w_in[rows, c0:c0 + n], reads=[dB], writes=[d["wbfB"]])
                for j in range(20):
                    rows = slice(j * 128, (j + 1) * 128)
                    S.dma("pool", d["w_out_bf"][rows, :], d["w_out"][rows, :], reads=[dB], writes=[d["wbfB"]])
                if "o_w_out_bf" in d:
                    for j in range(12):
                        rows = slice(j * 128, (j + 1) * 128)
                        S.dma("pool", d["o_w_out_bf"][rows, :], d["o_w_out"][rows, :], reads=[dB], writes=[d["wbfB"]])
            with contextlib.ExitStack() as stM:
                knt = cx.sb(stM, "knt", [128, SEQ], BF16)
                kntB = Buf("knt")
                vh = cx.sb(stM, "vh", [128, 32, 128], BF16)
                vhB = Buf("vh")
                qn = cx.sb(stM, "qn", [128, NT], BF16)
                qr = cx.sb(stM, "qr", [128, NT], BF16)
                q2B = Buf("qr2")
                qB = [Buf("q%d" % i) for i in range(4)]
                wring = Ring(cx, stM, "wkvh", [128, 2, 256], BF16, 2)
                wqring = Ring(cx, stM, "wqh", [128, 6, 256], BF16, 2)
                pring = Ring(cx, stM, "pT", [128, 512], BF16, 4)
                acc_ring = Ring(cx, stM, "accs", [128, 512], F32, 2)
                t1, t1B = (cx.sb(stM, "mt1", [128, 512], F32), Buf("mt1"))
                t2, t2B = (cx.sb(stM, "mt2", [128, 512], F32), Buf("mt2"))
                rec, recB = (cx.sb(stM, "rec", [128, 512], F32), Buf("rec"))
                scale = (128 + 64) ** -0.5
                for h in range(8):
                    if h == 1:
                        stop_at(4)
                    wkv, wkvB = wring.next()
                    S.dma("pool", wkv[:], wview(d["w_ukv"], 0, 256, h * 256, 256), reads=[dB], writes=[wkvB])
                    wqh, wqhB = wqring.next()
                    S.dma("pool", wqh[:, :, 0:192], wview(d["w_uq"], 0, 768, h * 192, 192), reads=[dB], writes=[wqhB])
                    S.dma("pool", wqh[:, :, 192:224], wview(d["w_uq"], 0, 768, h * 192 + 160, 32), reads=[dB], writes=[wqhB])
                    S.dma("pool", wqh[:, :, 224:256], wview(d["w_uq"], 0, 768, h * 192 + 128, 32), reads=[dB], writes=[wqhB])
                    for tb in range(8):
                        bk, bB = cx.bank(PA)
                        for kc in range(2):
                            S.mm(bk[:], wkv[:, kc, 0:128], ckvn[:, kc, tb * 512:(tb + 1) * 512], kc == 0, kc == 1,
                                 R=[wkvB, ckvnB[tb]], W=[bB])
                        S.copy("act" if tb % 2 == 0 else "dve", knt[:, tb * 512:(tb + 1) * 512], bk[:], R=[bB], W=[kntB])
                    for tb in range(8):
                        bk, bB = cx.bank(PA)
                        for j in range(4):
                            c0 = tb * 512 + j * 128
                            for kc in range(2):
                                S.mm(bk[:, j * 128:(j + 1) * 128], ckvn[:, kc, c0:c0 + 128], wkv[:, kc, 128:256], kc == 0, kc == 1,
                                     R=[wkvB, ckvnB[tb]], W=[bB])
                        S.copy("dve" if tb % 2 == 0 else "act", vh[:, tb * 4:(tb + 1) * 4, :],
                               bk[:].rearrange("p (j d) -> p j d", j=4), R=[bB], W=[vhB])
                    for qb in range(4):
                        cols = slice(qb * 512, (qb + 1) * 512)
                        bn, bN = cx.bank(PA)
                        for kc in range(6):
                            S.mm(bn[:], wqh[:, kc, 0:128], cqg[:, kc, cols], kc == 0, kc == 5, R=[wqhB, cqgB[qb]], W=[bN])
                        ba, bA = cx.bank(PA)
                        for kc in range(6):
                            S.mm(ba[0:64, :], wqh[:, kc, 128:192], cqg[:, kc, cols], kc == 0, kc == 5, R=[wqhB, cqgB[qb]], W=[bA])
                        bb, bBb = cx.bank(PA)
                        for kc in range(6):
                            S.mm(bb[0:64, :], wqh[:, kc, 192:256], cqg[:, kc, cols], kc == 0, kc == 5, R=[wqhB, cqgB[qb]], W=[bBb])
                        S.tt("dve", qn[:, cols], bn[:], rq[:, cols], ALU.mult, R=[bN, rqB[qb]], W=[qB[qb]])
                        S.tt("dve", t1[0:64, :], ba[0:64, :], csq[:, cols], ALU.mult, R=[bA, rqB[qb]], W=[t1B])
                        S.tt("dve", t2[0:64, :], bb[0:64, :], snq[:, cols], ALU.mult, R=[bBb, rqB[qb]], W=[t2B])
                        S.tt("dve", qr[0:64, cols], t1[0:64, :], t2[0:64, :], ALU.add, R=[t1B, t2B], W=[qB[qb]])
                    S.dma("sp", qr[64:128, :], qr[0:64, :], reads=qB, writes=[q2B])
                    for qb in range(4):
                        cols = slice(qb * 512, (qb + 1) * 512)

                        def kts(kc):
                            ks = slice(kc * 128, (kc + 1) * 128)
                            if kc % 2 == 0:
                                return [(knt[:, ks], [kntB]), (krt[0:64, ks], [krtB[kc // 4]])]
                            return [(knt[:, ks], [kntB]), (krt[64:128, ks], [krt2B])]

                        def v_of(kc):
                            return vh[:, kc, :], [vhB]

                        def fin(bo, bO, bs, bS, h=h, cols=cols):
                            S.recip(rec[:], bs[:], R=[bS], W=[recB])
                            S.tt("dve", omla[:, h, cols], bo[:], rec[:], ALU.mult, R=[bO, recB], W=[omlaB[h]])
                        attention_block(cx, kts, [(qn[:, cols], [qB[qb]]), (qr[0:64, cols], [qB[qb]])], v_of, 32, scale,
                                        pring, consts, fin, acc=acc_ring.next(), q_alt=(qr[64:128, cols], [q2B]))
        S.barrier()
        stop_at(5)
        if "hook_after_mla" in d:
            d["hook_after_mla"](stA)
        with contextlib.ExitStack() as st2:
            _even_ranges(cx, st2, d, consts, dB, yB, par, parB, omla, omlaB)
        if "hook_end" in d:
            S.barrier()
            d["hook_end"]()


def _sqrt_eps(cx, out, outB, bs, bS, scale, eps, consts, is_ap=False):
    S = cx.S
    o = out if is_ap else out[:]
    S.act(o, bs[:], AF.Sqrt, R=[bS, consts["B"]], W=[outB], scale=scale, bias=consts["eps"][eps])


def _rope_blk(cx, posf, cs_t, sn_t, par, parB, tmpf, tmpi):
    S = cx.S
    t, fr, m = tmpf
    ti = tmpi
    n = 512
    for which, out, shift in (("sin", sn_t, 0.0), ("cos", cs_t, 0.25)):
        S.ts("dve", t[0][0:64, 0:n], posf[0][:], par[0:64, 32:33], 1.0 / TWO_PI, ALU.mult, ALU.mult, R=[posf[1], parB], W=[t[1]])
        if shift:
            S.ts("dve", t[0][0:64, 0:n], t[0][0:64, 0:n], shift, None, ALU.add, R=[t[1]], W=[t[1]])
        S.copy("dve", ti[0][0:64, 0:n], t[0][0:64, 0:n], R=[t[1]], W=[ti[1]])
        S.copy("dve", fr[0][0:64, 0:n], ti[0][0:64, 0:n], R=[ti[1]], W=[fr[1]])
        S.tt("dve", fr[0][0:64, 0:n], t[0][0:64, 0:n], fr[0][0:64, 0:n], ALU.subtract, R=[t[1], fr[1]], W=[fr[1]])
        S.ts("dve", m[0][0:64, 0:n], fr[0][0:64, 0:n], 0.5, None, ALU.is_gt, R=[fr[1]], W=[m[1]])
        S.tt("dve", fr[0][0:64, 0:n], fr[0][0:64, 0:n], m[0][0:64, 0:n], ALU.subtract, R=[fr[1], m[1]], W=[fr[1]])
        S.ts("dve", m[0][0:64, 0:n], fr[0][0:64, 0:n], -0.5, None, ALU.is_lt, R=[fr[1]], W=[m[1]])
        S.tt("dve", fr[0][0:64, 0:n], fr[0][0:64, 0:n], m[0][0:64, 0:n], ALU.add, R=[fr[1], m[1]], W=[fr[1]])
        S.act(out[0][:], fr[0][0:64, 0:n], AF.Sin, R=[fr[1]], W=[out[1]], scale=TWO_PI * (1.0 - 2e-6))
        if which == "sin":
            S.ts("dve", out[0][:], out[0][:], par[0:64, 33:34], None, ALU.mult, R=[out[1], parB], W=[out[1]])


def _even_ranges(cx, st, d, consts, dB, yB, par, parB, omla, omlaB):
    S = cx.S
    w_in = d["w_in"]
    bfm = "w_in_bf" in d

    def wload(dst, c0, n, wB_):
        if bfm:
            S.dma("sp", dst, wview(d["w_in_bf"], 0, 1024, c0, n), reads=[d["wbfB"]], writes=[wB_])
        else:
            S.dma("pool", dst, wview(w_in, 0, 1024, c0, n), reads=[dB], writes=[wB_])
    idf = consts["idf"]
    onef = consts["onef"]
    gt = cx.sb(st, "gt", [128, 1024], F32)
    bt = cx.sb(st, "bt", [128, 1024], F32)
    gbB = Buf("gb")
    S.dma("sp", gt[:], d["ln_g"].partition_broadcast(128), reads=[dB], writes=[gbB])
    S.dma("sp", bt[:], d["ln_b"].partition_broadcast(128), reads=[dB], writes=[gbB])
    cwT = cx.sb(st, "cwT", [128, 8, 31], F32)
    cwB = Buf("cwT")
    kmT = cx.sb(st, "kmT", [128, 4, 256], BF16)
    vm = cx.sb(st, "vm", [128, 2, 512], BF16)
    kmB = Buf("kmv")
    with contextlib.ExitStack() as s0:
        cwn = cx.sb(s0, "cwn", [31, 1024], F32)
        cwnB = Buf("cwn")
        S.dma("sp", cwn[:], d["conv_w"], reads=[dB], writes=[cwnB])
        for cc in range(8):
            bk, bB = cx.bank(PB)
            S.mm(bk[:, 0:32], cwn[0:31, cc * 128:(cc + 1) * 128], idf[0:31, 0:32], True, True, R=[cwnB, consts["B"]], W=[bB])
            S.copy("dve", cwT[:, cc, :], bk[:, 0:31], R=[bB], W=[cwB])
        memr = cx.sb(s0, "memr", [128, 2, 1024], BF16)
        memrB = Buf("memr")
        S.dma("pool", memr[:], d["mem"].rearrange("(t p) c -> p t c", p=128), reads=[dB], writes=[memrB])
        memT = cx.sb(s0, "memT", [128, 8, 256], BF16)
        memTB = Buf("memT")
        transpose_rows(cx, memr, memrB, 128, 2, memT, memTB, consts)
        wm = cx.sb(s0, "wm", [128, 8, 1024], BF16)
        wmB = Buf("wm")
        for j in range(2):
            S.dma("pool", wm[:, :, j * 512:(j + 1) * 512], wview(d["mem_kv"], 0, 1024, j * 512, 512), reads=[dB], writes=[wmB])
        for h in range(4):
            bk, bB = cx.bank(PA)
            for kc in range(8):
                S.mm(bk[:, 0:256], wm[:, kc, h * 128:(h + 1) * 128], memT[:, kc, :], kc == 0, kc == 7, R=[wmB, memTB], W=[bB])
            S.copy("act", kmT[:, h, :], bk[:, 0:256], R=[bB], W=[kmB])
        for mc in range(2):
            bk, bB = cx.bank(PA)
            for kc in range(8):
                S.mm(bk[:], memT[:, kc, mc * 128:(mc + 1) * 128], wm[:, kc, 512:1024], kc == 0, kc == 7, R=[wmB, memTB], W=[bB])
            S.copy("dve", vm[:, mc, :], bk[:], R=[bB], W=[kmB])
    S.barrier()
    stop_at(6)
    xr = cx.sb(st, "xr5", [128, 5, 1024], BF16)
    xrB = Buf("xr5")
    xT = cx.sb(st, "xTR", [128, 8, 544], BF16)
    xTB = Buf("xTR")
    glu = cx.sb(st, "glu", [128, 8, 544], BF16)
    gluB = [Buf("glu%d" % i) for i in range(8)]
    dw = cx.sb(st, "dw", [128, 8, 512], F32)
    dwB = [Buf("dw%d" % i) for i in range(8)]
    mix = cx.sb(st, "mix", [128, 20, 512], BF16)
    mixB = [Buf("mix%d" % i) for i in range(20)]
    wab_ring = Ring(cx, st, "wab", [128, 8, 256], BF16, 2)
    wg_ring = Ring(cx, st, "wg", [128, 8, 128], BF16, 3)
    tf = Ring(cx, st, "tf", [128, 512], F32, 6)
    pring = Ring(cx, st, "pTr", [128, 512], BF16, 4)
    dg_ring = Ring(cx, st, "dg", [128, 128], BF16, 6)
    mean_t = (cx.sb(st, "mean_t", [128, 512], F32), Buf("mean_t"))
    rs_t = (cx.sb(st, "rs_t", [128, 512], F32), Buf("rs_t"))
    qm_ring = Ring(cx, st, "qm", [128, 512], BF16, 2)
    wo_ring = Ring(cx, st, "woq", [128, 20, 256], BF16, 2)
    z = cx.sb(st, "z", [128, 4, 1024], F32)
    zB = [Buf("z%d" % i) for i in range(4)]
    stats = cx.sb(st, "stats", [128, 2, 6], F32)
    mv = cx.sb(st, "mv", [128, 2], F32)
    rstd = cx.sb(st, "rstd", [128, 1], F32)
    stB = Buf("stats")
    mscale = 128 ** -0.5

    for rb in range(4):
        r0 = rb * 512
        if rb == 1:
            stop_at(7)
        if rb == 0:
            S.dma("pool", xr[:, 0:4, :], d["x_halo"][r0:r0 + 512, :].rearrange("(t p) c -> p t c", p=128), reads=[dB], writes=[xrB])
            S.dma("pool", xr[0:30, 4, :], d["x_halo"][r0 + 512:r0 + 542, :], reads=[dB], writes=[xrB])
        transpose_rows(cx, xr, xrB, 30, 5, xT, xTB, consts)
        if rb + 1 < 4:
            r1 = r0 + 512
            S.dma("pool", xr[:, 0:4, :], d["x_halo"][r1:r1 + 512, :].rearrange("(t p) c -> p t c", p=128), reads=[dB], writes=[xrB])
            S.dma("pool", xr[0:30, 4, :], d["x_halo"][r1 + 512:r1 + 542, :], reads=[dB], writes=[xrB])
        S.dma("sp", z[:], d["x_halo"][r0 + 15:r0 + 527, :].rearrange("(t p) c -> p t c", p=128), reads=[dB], writes=zB)
        for cc in range(8):
            wab, wabB = wab_ring.next()
            wload(wab[:, :, 0:128], E_CONV_IN + cc * 128, 128, wabB)
            wload(wab[:, :, 128:256], E_CONV_IN + 1024 + cc * 128, 128, wabB)
            for (c0, n) in ((0, 512), (512, 30)):
                ba, bA = cx.bank(PA)
                for kc in range(8):
                    S.mm(ba[:, 0:n], wab[:, kc, 0:128], xT[:, kc, c0:c0 + n], kc == 0, kc == 7, R=[wabB, xTB], W=[bA])
                bb, bBb = cx.bank(PA)
                for kc in range(8):
                    S.mm(bb[:, 0:n], wab[:, kc, 128:256], xT[:, kc, c0:c0 + n], kc == 0, kc == 7, R=[wabB, xTB], W=[bBb])
                sg, sgB = tf.next()
                S.act(sg[:, 0:n], bb[:, 0:n], AF.Sigmoid, R=[bBb], W=[sgB])
                S.tt("dve", glu[:, cc, c0:c0 + n], ba[:, 0:n], sg[:, 0:n], ALU.mult, R=[bA, sgB], W=[gluB[cc]])
        bsum, bSum = cx.banks[6]
        bsq, bSq = cx.banks[7]
        for cc in range(8):
            bk, bB = cx.bank(PA)
            for k in range(31):
                dg, dgB = dg_ring.next()
                if k % 2 == 0:
                    S.act(dg[:], idf[:], AF.Identity, R=[consts["B"], cwB], W=[dgB], scale=cwT[:, cc, k:k + 1])
                else:
                    S.ts("dve", dg[:], idf[:], cwT[:, cc, k:k + 1], None, ALU.mult, R=[consts["B"], cwB], W=[dgB])
                S.mm(bk[:], dg[:], glu[:, cc, k:k + 512], k == 0, k == 30, R=[dgB, gluB[cc]], W=[bB])
            S.act(dw[:, cc, :], bk[:], AF.Identity, R=[bB, parB], W=[dwB[cc]], bias=par[:, cc:cc + 1])
            sq, sqB = tf.next()
            S.act(sq[:], bk[:], AF.Square, R=[bB, parB], W=[sqB], bias=par[:, cc:cc + 1])
            S.mm(bsum[:], onef[:], dw[:, cc, :], cc == 0, cc == 7, R=[consts["B"], dwB[cc]], W=[bSum])
            S.mm(bsq[:], onef[:], sq[:], cc == 0, cc == 7, R=[consts["B"], sqB], W=[bSq])
        S.act(mean_t[0][:], bsum[:], AF.Copy, R=[bSum], W=[mean_t[1]], scale=1.0 / 1024.0)
        m2, m2B = tf.next()
        S.tt("dve", m2[:], mean_t[0][:], mean_t[0][:], ALU.mult, R=[mean_t[1]], W=[m2B])
        S.stt(m2[:], bsq[:], 1.0 / 1024.0, m2[:], ALU.mult, ALU.subtract, R=[bSq, m2B], W=[m2B])
        S.act(rs_t[0][:], m2[:], AF.Sqrt, R=[m2B, consts["B"]], W=[rs_t[1]], bias=consts["eps"][LN_EPS])
        S.recip(rs_t[0][:], rs_t[0][:], R=[rs_t[1]], W=[rs_t[1]])
        for cc in range(8):
            t1, t1B = tf.next()
            S.tt("pool", t1[:], dw[:, cc, :], mean_t[0][:], ALU.subtract, R=[dwB[cc], mean_t[1]], W=[t1B])
            S.tt("dve", t1[:], t1[:], rs_t[0][:], ALU.mult, R=[t1B, rs_t[1]], W=[t1B])
            S.act(t1[:], t1[:], AF.Silu, R=[t1B, parB], W=[t1B], scale=par[:, 8 + cc:9 + cc], bias=par[:, 16 + cc:17 + cc])
            wg, wgB = wg_ring.next()
            wload(wg[:], E_CONV_GATE + cc * 128, 128, wgB)
            bk, bB = cx.bank(PA)
            for kc in range(8):
                S.mm(bk[:], wg[:, kc, :], xT[:, kc, 15:527], kc == 0, kc == 7, R=[wgB, xTB], W=[bB])
            sg, sgB = tf.next()
            S.act(sg[:], bk[:], AF.Silu, R=[bB], W=[sgB])
            S.tt("dve", mix[:, cc, :], t1[:], sg[:], ALU.mult, R=[t1B, sgB], W=[mixB[cc]])
        for h in range(4):
            wg, wgB = wg_ring.next()
            wload(wg[:], E_MEMQ + h * 128, 128, wgB)
            bk, bB = cx.bank(PA)
            for kc in range(8):
                S.mm(bk[:], wg[:, kc, :], xT[:, kc, 15:527], kc == 0, kc == 7, R=[wgB, xTB], W=[bB])
            qm, qmB = qm_ring.next()
            S.copy("act", qm[:], bk[:], R=[bB], W=[qmB])
            wg2, wg2B = wg_ring.next()
            wload(wg2[:], E_MEMG + h * 128, 128, wg2B)
            bk2, bB2 = cx.bank(PA)
            for kc in range(8):
                S.mm(bk2[:], wg2[:, kc, :], xT[:, kc, 15:527], kc == 0, kc == 7, R=[wg2B, xTB], W=[bB2])
            gm, gmB = tf.next()
            S.act(gm[:], bk2[:], AF.Silu, R=[bB2], W=[gmB])

            def kts(mc, h=h):
                return [(kmT[:, h, mc * 128:(mc + 1) * 128], [kmB])]

            def v_of(mc, h=h):
                return vm[:, mc, h * 128:(h + 1) * 128], [kmB]

            def fin(bo, bO, bs, bS, h=h, gm=gm, gmB=gmB):
                rec, recB = tf.next()
                S.recip(rec[:], bs[:], R=[bS], W=[recB])
                S.tt("dve", rec[:], bo[:], rec[:], ALU.mult, R=[bO, recB], W=[recB])
                S.tt("dve", mix[:, 16 + h, :], rec[:], gm[:], ALU.mult, R=[recB, gmB], W=[mixB[16 + h]])
            attention_block(cx, kts, [(qm[:], [qmB])], v_of, 2, mscale, pring, consts, fin)
        for h in range(8):
            wg, wgB = wg_ring.next()
            wload(wg[:], E_MLAG + h * 128, 128, wgB)
            bk, bB = cx.bank(PA)
            for kc in range(8):
                S.mm(bk[:], wg[:, kc, :], xT[:, kc, 15:527], kc == 0, kc == 7, R=[wgB, xTB], W=[bB])
            sg, sgB = tf.next()
            S.act(sg[:], bk[:], AF.Silu, R=[bB], W=[sgB])
            S.tt("dve", mix[:, 8 + h, :], omla[:, h, r0:r0 + 512], sg[:], ALU.mult, R=[omlaB[h], sgB], W=[mixB[8 + h]])
        for qq in range(4):
            woq, woqB = wo_ring.next()
            csl = slice(qq * 256, (qq + 1) * 256)
            if bfm:
                S.dma("sp", woq[:], d["w_out_bf"][:, csl].rearrange("(j p) n -> p j n", p=128), reads=[d["wbfB"]], writes=[woqB])
            else:
                S.dma("pool", woq[:], d["w_out"][:, csl].rearrange("(j p) n -> p j n", p=128), reads=[dB], writes=[woqB])
            for t in range(4):
                bk, bB = cx.bank(PA)
                for j in range(20):
                    S.mm(bk[:, 0:256], mix[:, j, t * 128:(t + 1) * 128], woq[:, j, :], j == 0, j == 19, R=[mixB[j], woqB], W=[bB])
                zs = z[:, t, csl]
                S.stt(zs, zs, float(ALPHA), bk[:, 0:256], ALU.mult, ALU.add, R=[bB, zB[t]], W=[zB[t]])
        for t in range(4):
            for i in range(2):
                S.op("dve", lambda e, t=t, i=i: e.bn_stats(stats[:, i, :], z[:, t, i * 512:(i + 1) * 512]), [zB[t]], [stB])
            S.op("dve", lambda e: e.bn_aggr(mv[:], stats[:].rearrange("p a b -> p (a b)")), [stB], [stB])
            S.act(rstd[:], mv[:, 1:2], AF.Sqrt, R=[stB, consts["B"]], W=[stB], bias=consts["eps"][LN_EPS])
            S.recip(rstd[:], rstd[:], R=[stB], W=[stB])
            S.ts("dve", z[:, t, :], z[:, t, :], mv[:, 0:1], rstd[:], ALU.subtract, ALU.mult, R=[zB[t], stB], W=[zB[t]])
            S.tt("pool", z[:, t, :], z[:, t, :], gt[:], ALU.mult, R=[zB[t], gbB], W=[zB[t]])
            S.tt("dve", z[:, t, :], z[:, t, :], bt[:], ALU.add, R=[zB[t], gbB], W=[zB[t]])
            S.dma("sp", d["y"][r0 + t * 128:r0 + (t + 1) * 128, :], z[:, t, :], reads=[zB[t]], writes=[yB])


O_U, O_GATE, O_MEMQ, O_MEMG = 0, 1024, 2048, 2560
NG = 64


def sincos(cx, ang, n, cos_out, sin_out, R, tmp, W):
    S = cx.S
    t, fr, m, ti = tmp
    for out, shift in ((sin_out, 0.0), (cos_out, 0.25)):
        S.ts("dve", t[0][:, 0:n], ang, 1.0 / TWO_PI, shift, ALU.mult, ALU.add, R=R, W=[t[1]])
        S.copy("dve", ti[0][:, 0:n], t[0][:, 0:n], R=[t[1]], W=[ti[1]])
        S.copy("dve", fr[0][:, 0:n], ti[0][:, 0:n], R=[ti[1]], W=[fr[1]])
        S.tt("dve", fr[0][:, 0:n], t[0][:, 0:n], fr[0][:, 0:n], ALU.subtract, R=[t[1], fr[1]], W=[fr[1]])
        S.ts("dve", m[0][:, 0:n], fr[0][:, 0:n], 0.5, None, ALU.is_gt, R=[fr[1]], W=[m[1]])
        S.tt("dve", fr[0][:, 0:n], fr[0][:, 0:n], m[0][:, 0:n], ALU.subtract, R=[fr[1], m[1]], W=[fr[1]])
        S.ts("dve", m[0][:, 0:n], fr[0][:, 0:n], -0.5, None, ALU.is_lt, R=[fr[1]], W=[m[1]])
        S.tt("dve", fr[0][:, 0:n], fr[0][:, 0:n], m[0][:, 0:n], ALU.add, R=[fr[1], m[1]], W=[fr[1]])
        S.act(out, fr[0][:, 0:n], AF.Sin, R=[fr[1]], W=W, scale=TWO_PI * (1.0 - 2e-6))


def s5_prefetch(cx, st, d, consts, dB, persist=False):
    S = cx.S
    idf = consts["idf"]
    P = {}

    def f32(name, shape):
        return cx.sb(st, name, shape, F32)
    are = f32("are", [128, NG]); aim = f32("aim", [128, NG]); ldt = f32("ldt", [128, NG])
    pB = Buf("s5par")
    bt_re = f32("bt_re", [128, NG, 16]); bt_im = f32("bt_im", [128, NG, 16])
    ct_re = f32("ct_re", [128, NG, 16]); ct_im = f32("ct_im", [128, NG, 16])
    bcB = Buf("btct")
    with contextlib.ExitStack() as s0_:
        s0 = st if persist else s0_
        nat = cx.sb(s0, "nat", [64, 128], F32)
        natB = Buf("nat")
        for name, dst in (("s_a_re", are), ("s_a_im", aim)):
            S.dma("sp", nat[:, 0:64], d[name + "_A"], reads=[dB], writes=[natB])
            S.dma("sp", nat[:, 64:128], d[name + "_B"], reads=[dB], writes=[natB])
            bk, bB = cx.bank(PB)
            S.mm(bk[:, 0:64], nat[:, :], idf[0:64, 0:64], True, True, R=[natB, consts["B"]], W=[bB])
            S.copy("dve", dst[:], bk[:, 0:64], R=[bB], W=[pB])
        S.dma("sp", ldt[0:64, :], d["s_log_dt_A"].partition_broadcast(64), reads=[dB], writes=[pB])
        S.dma("sp", ldt[64:128, :], d["s_log_dt_B"].partition_broadcast(64), reads=[dB], writes=[pB])
        for nm, dst in (("s_b_re", bt_re), ("s_b_im", bt_im)):
            S.dma("sp", dst[0:64], d[nm + "_A"].rearrange("g p c -> p g c"), reads=[dB], writes=[bcB])
            S.dma("sp", dst[64:128], d[nm + "_B"].rearrange("g p c -> p g c"), reads=[dB], writes=[bcB])
        cn_ring = Ring(cx, s0, "cn", [128, 128], F32, 2)
        for nm, dst in (("s_c_re", ct_re), ("s_c_im", ct_im)):
            for gb in range(8):
                cn, cnB = cn_ring.next()
                S.dma("sp", cn[:, 0:64], d[nm + "_A"][gb * 8:(gb + 1) * 8].rearrange("g c p -> (g c) p"), reads=[dB], writes=[cnB])
                S.dma("sp", cn[:, 64:128], d[nm + "_B"][gb * 8:(gb + 1) * 8].rearrange("g c p -> (g c) p"), reads=[dB], writes=[cnB])
                bk, bB = cx.bank(PB)
                S.mm(bk[:, 0:128], cn[:], idf[:], True, True, R=[cnB, consts["B"]], W=[bB])
                S.copy("dve", dst[:, gb * 8:(gb + 1) * 8, :], bk[:, 0:128].rearrange("p (g c) -> p g c", g=8), R=[bB], W=[bcB])
    if not persist:
        S.barrier()
    return dict(are=are, aim=aim, ldt=ldt, pB=pB, bt_re=bt_re, bt_im=bt_im, ct_re=ct_re, ct_im=ct_im, bcB=bcB)


def s5_tables(cx, d, consts, dB, a8, a8B, scr, pre=None):
    S = cx.S
    idf = consts["idf"]
    with contextlib.ExitStack() as st:
        def f32(name, shape):
            return cx.sb(st, name, shape, F32)
        if pre is None:
            pre = s5_prefetch(cx, st, d, consts, dB)
        are, aim, ldt, pB = pre["are"], pre["aim"], pre["ldt"], pre["pB"]
        bt_re, bt_im, ct_re, ct_im, bcB = pre["bt_re"], pre["bt_im"], pre["ct_re"], pre["ct_im"], pre["bcB"]
        se = f32("s5e", [128, 28])
        seB = Buf("s5e")
        S.dma("sp", se[:], d["s5e"], reads=[dB], writes=[seB])
        tmk = f32("tmask", [128, 2, 4, 128])
        tmB = Buf("tmask")
        for j in range(4):
            S.dma("sp", tmk[:, :, j, :], d["tmask"], reads=[dB], writes=[tmB])
        lre = f32("lre", [128, NG]); dtt = f32("dtt", [128, NG]); lrd = f32("lrd", [128, NG]); lid = f32("lid", [128, NG])
        S.ts("dve", lre[:], are[:], -1e-4, None, ALU.min, R=[pB], W=[pB])
        S.act(dtt[:], ldt[:], AF.Exp, R=[pB], W=[pB])
        S.tt("dve", lrd[:], lre[:], dtt[:], ALU.mult, R=[pB], W=[pB])
        S.tt("dve", lid[:], aim[:], dtt[:], ALU.mult, R=[pB], W=[pB])
        tmp = [(f32("sct%d" % i, [128, 512]), Buf("sct%d" % i)) for i in range(3)]
        tmp.append((cx.sb(st, "scti", [128, 512], I32), Buf("scti")))
        mag = f32("mag", [128, 512]); cs = f32("cs", [128, 512]); sn = f32("sn", [128, 512]); arg = f32("arg", [128, 512])
        wB = Buf("s5work")

        def cpow(E, n, out_re, out_im):
            m = NG * n
            if E is None:
                a_re, a_im = lrd[:], lid[:]
                rr = [pB]
            else:
                S.tt("dve", arg[:, 0:m].rearrange("p (g e) -> p g e", e=n), lrd[:].unsqueeze(2).broadcast_to([128, NG, n]),
                     E.unsqueeze(1).broadcast_to([128, NG, n]), ALU.mult, R=[pB, seB], W=[wB])
                a_re = arg[:, 0:m]
                rr = [wB]
            S.act(mag[:, 0:m], a_re, AF.Exp, R=rr, W=[wB])
            if E is not None:
                S.tt("dve", arg[:, 0:m].rearrange("p (g e) -> p g e", e=n), lid[:].unsqueeze(2).broadcast_to([128, NG, n]),
                     E.unsqueeze(1).broadcast_to([128, NG, n]), ALU.mult, R=[pB, seB, wB], W=[wB])
                a_im = arg[:, 0:m]
            sincos(cx, a_im, m, cs[:, 0:m], sn[:, 0:m], [wB, pB], tmp, [wB])
            S.tt("dve", out_re, mag[:, 0:m], cs[:, 0:m], ALU.mult, R=[wB], W=[wB])
            S.tt("dve", out_im, mag[:, 0:m], sn[:, 0:m], ALU.mult, R=[wB], W=[wB])
        lb_re = f32("lb_re", [128, NG]); lb_im = f32("lb_im", [128, NG])
        cpow(None, 1, lb_re[:], lb_im[:])
        den = f32("den", [128, NG]); nr = f32("nr", [128, NG]); f_re = f32("f_re", [128, NG]); f_im = f32("f_im", [128, NG])
        t0 = f32("t0", [128, NG])
        S.tt("dve", den[:], lre[:], lre[:], ALU.mult, R=[pB], W=[wB])
        S.tt("dve", t0[:], aim[:], aim[:], ALU.mult, R=[pB], W=[wB])
        S.tt("dve", den[:], den[:], t0[:], ALU.add, R=[wB], W=[wB])
        S.recip(den[:], den[:], R=[wB], W=[wB])
        S.ts("dve", nr[:], lb_re[:], -1.0, None, ALU.add, R=[wB], W=[wB])
        S.tt("dve", f_re[:], nr[:], lre[:], ALU.mult, R=[wB, pB], W=[wB])
        S.tt("dve", t0[:], lb_im[:], aim[:], ALU.mult, R=[wB, pB], W=[wB])
        S.tt("dve", f_re[:], f_re[:], t0[:], ALU.add, R=[wB], W=[wB])
        S.tt("dve", f_re[:], f_re[:], den[:], ALU.mult, R=[wB], W=[wB])
        S.tt("dve", f_im[:], lb_im[:], lre[:], ALU.mult, R=[wB, pB], W=[wB])
        S.tt("dve", t0[:], nr[:], aim[:], ALU.mult, R=[wB, pB], W=[wB])
        S.tt("dve", f_im[:], f_im[:], t0[:], ALU.subtract, R=[wB], W=[wB])
        S.tt("dve", f_im[:], f_im[:], den[:], ALU.mult, R=[wB], W=[wB])
        bb_re = f32("bb_re", [128, NG, 16]); bb_im = f32("bb_im", [128, NG, 16]); t1 = f32("t1k", [128, NG, 16])
        fre_b = f_re[:].unsqueeze(2).broadcast_to([128, NG, 16])
        fim_b = f_im[:].unsqueeze(2).broadcast_to([128, NG, 16])
        S.tt("dve", bb_re[:], bt_re[:], fre_b, ALU.mult, R=[wB, bcB], W=[wB])
        S.tt("dve", t1[:], bt_im[:], fim_b, ALU.mult, R=[wB, bcB], W=[wB])
        S.tt("dve", bb_re[:], bb_re[:], t1[:], ALU.subtract, R=[wB], W=[wB])
        S.tt("dve", bb_im[:], bt_im[:], fre_b, ALU.mult, R=[wB, bcB], W=[wB])
        S.tt("dve", t1[:], bt_re[:], fim_b, ALU.mult, R=[wB, bcB], W=[wB])
        S.tt("dve", bb_im[:], bb_im[:], t1[:], ALU.add, R=[wB], W=[wB])
        pin_re = f32("pin_re", [128, NG, 8]); pin_im = f32("pin_im", [128, NG, 8])
        pt_re = f32("pt_re", [128, NG, 8]); pt_im = f32("pt_im", [128, NG, 8])
        p3_re = f32("p3_re", [128, NG, 8]); p3_im = f32("p3_im", [128, NG, 8])
        cpow(se[:, 0:8], 8, pin_re[:].rearrange("p g e -> p (g e)"), pin_im[:].rearrange("p g e -> p (g e)"))
        cpow(se[:, 8:16], 8, pt_re[:].rearrange("p g e -> p (g e)"), pt_im[:].rearrange("p g e -> p (g e)"))
        cpow(se[:, 16:24], 8, p3_re[:].rearrange("p g e -> p (g e)"), p3_im[:].rearrange("p g e -> p (g e)"))
        S.ts("dve", arg[:, 0:NG], lrd[:], 8.0, None, ALU.mult, R=[pB], W=[wB])
        S.act(mag[:, 0:NG], arg[:, 0:NG], AF.Exp, R=[wB], W=[wB])
        S.ts("dve", arg[:, 0:NG], lid[:], 8.0, None, ALU.mult, R=[pB, wB], W=[wB])
        sincos(cx, arg[:, 0:NG], NG, cs[:, 0:NG], sn[:, 0:NG], [wB], tmp, [wB])
        S.tt("dve", a8[:, 0, :], mag[:, 0:NG], cs[:, 0:NG], ALU.mult, R=[wB], W=[a8B])
        S.tt("dve", a8[:, 1, :], mag[:, 0:NG], sn[:, 0:NG], ALU.mult, R=[wB], W=[a8B])
        GQ = 4
        NS = 2
        slots = []
        for sidx in range(NS):
            sl = {}
            sl["gin"] = [f32("gin%d" % i, [128, GQ, 8, 16]) for i in range(2)]
            sl["gt"] = [f32("gt%d" % i, [128, GQ, 8, 16]) for i in range(2)]
            sl["g3"] = [f32("g3%d" % i, [128, GQ, 8, 16]) for i in range(2)]
            sl["gtm"] = [[f32("gtm%d%d" % (i, k), [128, GQ, 8, 16]) for k in range(2)] for i in range(2)]
            sl["tqs"] = [f32("tq%d" % i, [128, GQ, 8, 16]) for i in range(3)]
            sl["w3t"] = cx.sb(st, "w3t", [128, 2, GQ, 2, 128], BF16)
            sl["w1t"] = cx.sb(st, "w1t", [128, GQ, 2, 128], BF16)
            sl["wtt"] = cx.sb(st, "wtt", [128, GQ, 128], BF16)
            sl["ginB"], sl["gtB"], sl["g3B"] = Buf("ginB"), Buf("gtB"), Buf("g3B")
            sl["gmB"], sl["w3B"], sl["w1B"], sl["wtB"] = Buf("gtmask"), Buf("w3tB"), Buf("w1tB"), Buf("wttB")
            slots.append(sl)
        tA = f32("tA", [128, 512]); tBt = f32("tBt", [128, 512])
        gB = Buf("gwork")
        scrB = scr["B"]

        def stage1(q):
            sl = slots[q % NS]
            gs = slice(q * GQ, (q + 1) * GQ)
            gin, gt, g3, gtm, tqs, w3t = sl["gin"], sl["gt"], sl["g3"], sl["gtm"], sl["tqs"], sl["w3t"]

            def cmul(eng, tq, oBuf, out_re, out_im, p_re, p_im, x_re, x_im, neg_im):
                pr = p_re[:, gs, :].unsqueeze(3).broadcast_to([128, GQ, 8, 16])
                pi = p_im[:, gs, :].unsqueeze(3).broadcast_to([128, GQ, 8, 16])
                xr = x_re[:, gs, :].unsqueeze(2).broadcast_to([128, GQ, 8, 16])
                xi = x_im[:, gs, :].unsqueeze(2).broadcast_to([128, GQ, 8, 16])
                S.tt(eng, out_re, pr, xr, ALU.mult, R=[wB], W=[oBuf])
                S.tt(eng, tq[:], pi, xi, ALU.mult, R=[wB], W=[oBuf])
                S.tt(eng, out_re, out_re, tq[:], ALU.subtract, R=[oBuf], W=[oBuf])
                S.tt(eng, out_im, pr, xi, ALU.mult, R=[wB], W=[oBuf])
                S.tt(eng, tq[:], pi, xr, ALU.mult, R=[wB, oBuf], W=[oBuf])
                if neg_im and eng == "dve":
                    S.stt(out_im, out_im, -1.0, tq[:], ALU.mult, ALU.subtract, R=[oBuf], W=[oBuf])
                else:
                    S.tt(eng, out_im, out_im, tq[:], ALU.add, R=[oBuf], W=[oBuf])
            cmul("pool", tqs[1], sl["gtB"], gt[0][:], gt[1][:], pt_re, pt_im, ct_re, ct_im, True)
            cmul("dve", tqs[0], sl["ginB"], gin[0][:], gin[1][:], pin_re, pin_im, bb_re, bb_im, False)
            cmul("dve", tqs[2], sl["g3B"], g3[0][:], g3[1][:], p3_re, p3_im, ct_re, ct_im, True)
            for i in range(2):
                for k in range(2):
                    sc = se[:, 24 + k:25 + k] if i == 0 else se[:, 26 + k:27 + k]
                    S.act(gtm[i][k][:].rearrange("p g e c -> p (g e c)"), gt[i][:].rearrange("p g e c -> p (g e c)"), AF.Identity,
                          R=[sl["gtB"], seB], W=[sl["gmB"]], scale=sc)
            for i in range(2):
                for k in range(2):
                    S.act(w3t[:, k, :, i, :], g3[i][:].rearrange("p g e c -> p g (e c)"), AF.Identity,
                          R=[sl["g3B"], seB], W=[sl["w3B"]], scale=se[:, 24 + k:25 + k])
            for k in range(2):
                S.dma("sp", scr["w3"][k, gs].rearrange("g p r m -> p g r m"), w3t[:, k], reads=[sl["w3B"]], writes=[Buf("scrw")])

        def stage2(q):
            sl = slots[q % NS]
            gs = slice(q * GQ, (q + 1) * GQ)
            gin, gtm, w1t, wtt = sl["gin"], sl["gtm"], sl["w1t"], sl["wtt"]
            for g4 in range(GQ // 4):
                for i in range(2):
                    bk, bB = cx.bank(PA)
                    for gg in range(4):
                        g = g4 * 4 + gg
                        S.mm(bk[:, gg * 128:(gg + 1) * 128], gin[i][:, g].rearrange("p e c -> p (e c)"), idf[:], True, True,
                             R=[sl["ginB"], consts["B"]], W=[bB])
                    S.copy("act", w1t[:, g4 * 4:(g4 + 1) * 4, i, :], bk[:].rearrange("p (g m) -> p g m", g=4), R=[bB], W=[sl["w1B"]])
                bks = []
                for k in range(2):
                    bk, bB = cx.bank(PA)
                    for gg in range(4):
                        g = g4 * 4 + gg
                        for i in range(2):
                            S.mm(bk[:, gg * 128:(gg + 1) * 128], gin[i][:, g].rearrange("p e c -> p (e c)"),
                                 gtm[i][k][:, g].rearrange("p e c -> p (e c)"), i == 0, i == 1, R=[sl["ginB"], sl["gmB"]], W=[bB])
                    bks.append((bk, bB))
                S.tt("dve", tA[:], bks[0][0][:], tmk[:, 0].rearrange("p j m -> p (j m)"), ALU.mult, R=[bks[0][1], tmB], W=[gB])
                S.tt("dve", tBt[:], bks[1][0][:], tmk[:, 1].rearrange("p j m -> p (j m)"), ALU.mult, R=[bks[1][1], tmB], W=[gB])
                S.tt("dve", wtt[:, g4 * 4:(g4 + 1) * 4, :], tA[:].rearrange("p (g m) -> p g m", g=4),
                     tBt[:].rearrange("p (g m) -> p g m", g=4), ALU.add, R=[gB], W=[sl["wtB"]])
            S.dma("sp", scr["w1"][gs].rearrange("g p r m -> p g r m"), w1t[:], reads=[sl["w1B"]], writes=[Buf("scrw")])
            S.dma("sp", scr["wt"][gs].rearrange("g p m -> p g m"), wtt[:], reads=[sl["wtB"]], writes=[Buf("scrw")])
        nq = NG // GQ
        stage1(0)
        for q in range(nq):
            if q + 1 < nq:
                stage1(q + 1)
            stage2(q)
    S.barrier()


def make_scr(nc):
    scr = {"B": Buf("scr")}
    scr["w1"] = nc.dram_tensor("scr_w1", [NG, 128, 2, 128], BF16, kind="Internal").ap()
    scr["w3"] = nc.dram_tensor("scr_w3", [2, NG, 128, 2, 128], BF16, kind="Internal").ap()
    scr["wt"] = nc.dram_tensor("scr_wt", [NG, 128, 128], BF16, kind="Internal").ap()
    return scr


def build_odd(nc, cx, d, consts):
    S = cx.S
    dB = Buf("dram_in_o")
    yB = Buf("y_o")
    w_in = d["o_w_in"]
    idb = consts["idb"]
    scr = d.get("scr") or make_scr(nc)
    scrB = scr["B"]
    with contextlib.ExitStack() as stA:
        if "a8" in d:
            a8, a8B = d["a8"]
        else:
            a8 = cx.sb(stA, "a8", [128, 2, NG], F32)
            a8B = Buf("a8")
            s5_tables(cx, d, consts, dB, a8, a8B, scr, pre=d.get("s5pre"))
        aa = cx.sb(stA, "aa", [128, 2, NG], F32)
        ab = cx.sb(stA, "ab", [128, 2, NG], F32)
        S.copy("dve", aa[:, 0, :], a8[:, 0, :], R=[a8B], W=[a8B])
        S.copy("dve", aa[:, 1, :], a8[:, 0, :], R=[a8B], W=[a8B])
        S.ts("dve", ab[:, 0, :], a8[:, 1, :], -1.0, None, ALU.mult, R=[a8B], W=[a8B])
        S.copy("dve", ab[:, 1, :], a8[:, 1, :], R=[a8B], W=[a8B])
        carry = cx.sb(stA, "carry", [128, 2, NG], F32)
        carB = [Buf("carA"), Buf("carB")]
        S.op("dve", lambda e: e.memset(carry[:], 0.0), writes=carB)
        dtile = cx.sb(stA, "dtile", [128, 1024], F32)
        gt = cx.sb(stA, "gt_o", [128, 1024], F32)
        bt = cx.sb(stA, "bt_o", [128, 1024], F32)
        gbB = Buf("gb_o")
        S.dma("sp", dtile[:], d["o_d"].partition_broadcast(128), reads=[dB], writes=[gbB])
        S.dma("sp", gt[:], d["o_ln_g"].partition_broadcast(128), reads=[dB], writes=[gbB])
        S.dma("sp", bt[:], d["o_ln_b"].partition_broadcast(128), reads=[dB], writes=[gbB])
        kmT = cx.sb(stA, "kmT_o", [128, 4, 256], BF16)
        vm = cx.sb(stA, "vm_o", [128, 2, 512], BF16)
        kmB = Buf("kmv_o")
        mem_kv_setup(cx, d["mem"], d["o_mem_kv"], consts, dB, kmT, vm, kmB)
        S.barrier()
        xT = cx.sb(stA, "xT_o", [128, 8, 1024], BF16)
        xTB = Buf("xT_o")
        yT = cx.sb(stA, "yT_o", [128, 8, 1024], BF16)
        yTB = Buf("yT_o")
        fusedm = "h1" in d
        if fusedm:
            units = [(0, True, False, False), (1, True, True, True), (0, True, True, True)]
            xch_s = cx.sb(stA, "xch_s", [128, 2 * NG], F32)
            xch_o = cx.sb(stA, "xch_o", [128, 2 * NG], F32)
            xchB = Buf("xch")
            cbuf = nc.dram_tensor("cbuf_bounce", [64, 2 * NG], F32).ap()
            csum = nc.dram_tensor("csum_bounce", [64, 2 * NG], F32).ap()
            cbB, csB = Buf("cbuf"), Buf("csum")

            def xchg():
                S.dma("sp", cbuf, carry[0:64].rearrange("p r g -> p (r g)"), reads=[carB[0]], writes=[cbB])
                S.collective("AllReduce", ALU.add, PAIRS, cbuf, csum, reads=[cbB], writes=[csB])
                S.dma("sp", xch_s[64:128, :], csum, reads=[csB], writes=[xchB])
                S.dma("sp", xch_o[64:128, :], cbuf, reads=[cbB], writes=[xchB])
                S.tt("dve", carry[64:128].rearrange("p r g -> p (r g)"), xch_s[64:128, :], xch_o[64:128, :], ALU.subtract,
                     R=[xchB], W=[carB[1]])
        else:
            units = [(3, False, True, False), (2, False, True, False), (0, True, False, False),
                     (1, True, True, True), (0, True, True, True)]
        for ui, (rng, doA, doB, full) in enumerate(units):
            if rng == 0 and full:
                S.op("dve", lambda e: e.memset(carry[0:64], 0.0), writes=[carB[0]])
            with contextlib.ExitStack() as st:
                s5_unit(cx, st, d, consts, dB, scr, rng, doA, doB, full, xT, xTB, yT, yTB, aa, ab, a8B, carry, carB, dtile, gbB,
                        xchg=(xchg if (fusedm and rng == 1) else None))
            S.barrier()
            if full:
                with contextlib.ExitStack() as st:
                    odd_post(cx, st, d, consts, dB, yB, rng, xT, xTB, yT, yTB, kmT, vm, kmB, gt, bt, gbB)
                S.barrier()
    S.wait_all_on("sp", [yB])


def mem_kv_setup(cx, mem, wkv, consts, dB, kmT, vm, kmB):
    S = cx.S
    with contextlib.ExitStack() as s0:
        memr = cx.sb(s0, "memr", [128, 2, 1024], BF16)
        memrB = Buf("memr")
        S.dma("pool", memr[:], mem.rearrange("(t p) c -> p t c", p=128), reads=[dB], writes=[memrB])
        memT = cx.sb(s0, "memT", [128, 8, 256], BF16)
        memTB = Buf("memT")
        transpose_rows(cx, memr, memrB, 128, 2, memT, memTB, consts)
        wm = cx.sb(s0, "wm", [128, 8, 1024], BF16)
        wmB = Buf("wm")
        for j in range(2):
            S.dma("pool", wm[:, :, j * 512:(j + 1) * 512], wview(wkv, 0, 1024, j * 512, 512), reads=[dB], writes=[wmB])
        for h in range(4):
            bk, bB = cx.bank(PA)
            for kc in range(8):
                S.mm(bk[:, 0:256], wm[:, kc, h * 128:(h + 1) * 128], memT[:, kc, :], kc == 0, kc == 7, R=[wmB, memTB], W=[bB])
            S.copy("act", kmT[:, h, :], bk[:, 0:256], R=[bB], W=[kmB])
        for mc in range(2):
            bk, bB = cx.bank(PA)
            for kc in range(8):
                S.mm(bk[:], memT[:, kc, mc * 128:(mc + 1) * 128], wm[:, kc, 512:1024], kc == 0, kc == 7, R=[wmB, memTB], W=[bB])
            S.copy("dve", vm[:, mc, :], bk[:], R=[bB], W=[kmB])


def s5_unit(cx, st, d, consts, dB, scr, rng, doA, doB, full, xT, xTB, yT, yTB, aa, ab, a8B, carry, carB, dtile, gbB, xchg=None):
    S = cx.S
    idb = consts["idb"]
    scrB = scr["B"]
    w_in = d["o_w_in"]
    xry = cx.sb(st, "xry", [128, 8, 1024], BF16)
    xryB = Buf("xry")
    wu = cx.sb(st, "wu", [128, 8, 1024], BF16)
    wuB = Buf("wu")
    utm = cx.sb(st, "utm", [128, NG, 8, 16], BF16)
    utmB = Buf("utm")
    xb = cx.sb(st, "xb", [128, 2, NG, 129], BF16)
    xbB = [Buf("xbA"), Buf("xbB")]
    xsB = [Buf("xsA"), Buf("xsB")]
    u8_ring = Ring(cx, st, "u8", [128, 4, 128], BF16, 3)
    w1_ring = Ring(cx, st, "w1r", [128, 8, 2, 128], BF16, 2)
    r0 = rng * 1024
    comb = doA and doB and xchg is None
    fused = "h1" in d
    rev = fused and rng >= 2
    if not fused:
        for hh in range(2):
            S.dma("pool", xry[:, hh * 4:(hh + 1) * 4, :], d["h_seq"][r0 + hh * 512:r0 + (hh + 1) * 512, :].rearrange("(t p) c -> p t c", p=128),
                  reads=[dB], writes=[xryB])
    elif not rev:
        for hh in range(2):
            S.dma("pool", xry[:, hh * 4:(hh + 1) * 4, :], d["h1"][r0 + hh * 512:r0 + (hh + 1) * 512, :].rearrange("(t p) c -> p t c", p=128),
                  reads=[d["h1B"]], writes=[xryB])
    else:
        p0 = 1024 if rng == 2 else 0
        sa_ring = Ring(cx, st, "sa", [128, 2, 1024], F32, 2)
        sb_ring = Ring(cx, st, "sbb", [128, 2, 1024], F32, 2)
        for q in range(4):
            sa, saB = sa_ring.next()
            sb_, sbB = sb_ring.next()
            rows = slice(p0 + q * 256, p0 + (q + 1) * 256)
            S.dma("sp", sa[:], d["hsum"][rows, :].rearrange("(t p) c -> p t c", p=128), reads=[d["hsumB"]], writes=[saB])
            S.dma("sp", sb_[:], d["h1"][rows, :].rearrange("(t p) c -> p t c", p=128), reads=[d["h1B"]], writes=[sbB])
            S.tt("pool", xry[:, q * 2:(q + 1) * 2, :], sa[:], sb_[:], ALU.subtract, R=[saB, sbB], W=[xryB])
    for j in range(2):
        S.dma("pool", wu[:, :, j * 512:(j + 1) * 512], wview(w_in, 0, 1024, O_U + j * 512, 512), reads=[dB], writes=[wuB])
    transpose_rows(cx, xry, xryB, 128, 8, xT, xTB, consts, rev=rev)
    k = 0
    for s in range(8):
        for hf in range(2):
            bk, bB = cx.bank(PA)
            for kc in range(8):
                S.mm(bk[:], xT[:, kc, s:1024:8], wu[:, kc, hf * 512:(hf + 1) * 512], kc == 0, kc == 7, R=[xTB, wuB], W=[bB])
            S.copy("act" if k % 2 == 0 else "dve", utm[:, hf * 32:(hf + 1) * 32, s, :],
                   bk[:].rearrange("p (g c) -> p g c", c=16), R=[bB], W=[utmB])
            k += 1
    lanesA = slice(0, 64)
    lanesB = slice(64, 128)
    for gb in range(8):
        w1, w1B = w1_ring.next()
        S.dma("sp", w1[:], scr["w1"][gb * 8:(gb + 1) * 8].rearrange("g p r m -> p g r m"), reads=[scrB], writes=[w1B])
        for g4 in range(2):
            g0 = gb * 8 + g4 * 4
            u8, u8B = u8_ring.next()
            bk, bB = cx.bank(PB)
            for gg in range(4):
                S.mm(bk[:, gg * 128:(gg + 1) * 128], utm[:, g0 + gg].rearrange("p s c -> p (s c)"), idb[:], True, True,
                     R=[utmB, consts["B"]], W=[bB])
            S.copy("act", u8[:], bk[:].rearrange("p (g m) -> p g m", g=4), R=[bB], W=[u8B])
            for ri in range(2):
                bk2, bB2 = cx.bank(PA)
                for gg in range(4):
                    S.mm(bk2[:, gg * 128:(gg + 1) * 128], w1[:, g4 * 4 + gg, ri, :], u8[:, gg, :], True, True, R=[w1B, u8B], W=[bB2])
                src = bk2[:].rearrange("p (g m) -> p g m", g=4)
                if doA:
                    S.copy("act", xb[lanesA, ri, g0:g0 + 4, 1:129], src[lanesA], R=[bB2], W=[xbB[0]])
                if doB and not comb:
                    S.copy("dve", xb[lanesB, ri, g0:g0 + 4, 0:128], src[lanesB], R=[bB2], W=[xbB[1]])
                if doB and comb:
                    S.copy("dve", xb[lanesB, ri, g0:g0 + 4, 128:0:-1], src[lanesB], R=[bB2], W=[xbB[1]])
    st_ring = [Ring(cx, st, "stA", [128, 2, NG], F32, 3), Ring(cx, st, "stB", [128, 2, NG], F32, 3)]
    ta = [cx.sb(st, "ta%d" % i, [128, 2, NG], F32) for i in range(2)]
    tb = [cx.sb(st, "tb%d" % i, [128, 2, NG], F32) for i in range(2)]
    tmB = [Buf("scanA"), Buf("scanB")]
    tbB = [Buf("scanAb"), Buf("scanBb")]
    bk6, b6B = cx.banks[6]
    bk7, b7B = cx.banks[7]

    def v3(ap):
        return ap.rearrange("p (r g) -> p r g", r=2)
    ps = {"aa": v3(bk6[:, 0:128]), "ab": v3(bk6[:, 128:256]), "b6": b6B, "b7": b7B,
          "tb": [v3(bk7[:, 0:128]), v3(bk7[:, 128:256])], "ad": [v3(bk7[:, 256:384]), v3(bk7[:, 384:512])]}
    if comb:
        allp = slice(0, 128)
        S.copy("dve", xb[:, :, :, 0], carry[:], R=[carB[0], carB[1]], W=[xsB[0], xsB[1]])
        pv, pvB = carry, None
        for step in range(128):
            col = step + 1
            rdeps = [a8B] + ([carB[0], carB[1]] if pvB is None else [pvB])
            S.tt("dve", ta[0][allp], aa[allp], pv[allp], ALU.mult, R=rdeps, W=[tmB[0]])
            S.tt("dve", tb[0][allp], ab[allp], pv[allp, ::-1, :], ALU.mult, R=rdeps, W=[tbB[0]])
            S.tt("dve", ta[0][allp], ta[0][allp], tb[0][allp], ALU.add, R=[tmB[0], tbB[0]], W=[tmB[0]])
            stt_, sttB = st_ring[0].next()
            S.tt("dve", stt_[allp], ta[0][allp], xb[allp, :, :, col], ALU.add, R=[tmB[0], xbB[0], xbB[1]], W=[sttB])
            S.copy("act", xb[allp, :, :, col], stt_[allp], R=[sttB], W=[xsB[0], xsB[1]])
            pv, pvB = stt_, sttB
        S.copy("dve", carry[:], pv[:], R=[pvB], W=[carB[0], carB[1]])
    elif xchg is None:
        _scan_single(cx, doA, doB, lanesA, lanesB, xb, xbB, xsB, carry, carB, aa, ab, a8B, ta, tb, tmB, tbB, st_ring, ps)
    else:
        _scan_single(cx, True, False, lanesA, lanesB, xb, xbB, xsB, carry, carB, aa, ab, a8B, ta, tb, tmB, tbB, st_ring, ps)
        xchg()
        _scan_single(cx, False, True, lanesA, lanesB, xb, xbB, xsB, carry, carB, aa, ab, a8B, ta, tb, tmB, tbB, st_ring, ps)
    if not full:
        return
    w3_ring = Ring(cx, st, "w3r", [128, 2, 8, 2, 128], BF16, 2)
    xf_ring = Ring(cx, st, "xf", [128, 2, 8, 129], BF16, 2)
    for xf_, xfB_ in xf_ring.tiles:
        S.op("dve", lambda e, t=xf_: e.memset(t[:], 0.0), writes=[xfB_])
    wt_ring = Ring(cx, st, "wtr", [128, 8, 128], BF16, 2)
    tf = Ring(cx, st, "tf5", [128, 512], F32, 3)
    yg = xry[:].rearrange("p r (g c) -> p g r c", c=16)
    for gb in range(8):
        w3, w3B = w3_ring.next()
        for k in range(2):
            S.dma("sp", w3[:, k], scr["w3"][k, gb * 8:(gb + 1) * 8].rearrange("g p r m -> p g r m"), reads=[scrB], writes=[w3B])
        wt, wtB = wt_ring.next()
        S.dma("sp", wt[:], scr["wt"][gb * 8:(gb + 1) * 8].rearrange("g p m -> p g m"), reads=[scrB], writes=[wtB])
        xf, xfB = xf_ring.next()
        if comb:
            S.copy("dve", xf[lanesB], xb[lanesB, :, gb * 8:(gb + 1) * 8, 128::-1], R=[xsB[1]], W=[xfB])
        else:
            S.copy("dve", xf[lanesB], xb[lanesB, :, gb * 8:(gb + 1) * 8, 0:129], R=[xsB[1]], W=[xfB])
        for g4 in range(2):
            g0 = gb * 8 + g4 * 4
            u8, u8B = u8_ring.next()
            bk, bB = cx.bank(PB)
            for gg in range(4):
                S.mm(bk[:, gg * 128:(gg + 1) * 128], utm[:, g0 + gg].rearrange("p s c -> p (s c)"), idb[:], True, True,
                     R=[utmB, consts["B"]], W=[bB])
            S.copy("act", u8[:], bk[:].rearrange("p (g m) -> p g m", g=4), R=[bB], W=[u8B])
            bo, bO = cx.bank(PA)
            for gg in range(4):
                g = g0 + gg
                gl = g4 * 4 + gg
                reg = bo[:, gg * 128:(gg + 1) * 128]
                S.mm(reg, xb[:, 0, g, 0:128], w3[:, 0, gl, 0, :], True, False, R=[xsB[0], xsB[1], w3B], W=[bO])
                S.mm(reg, xb[:, 1, g, 0:128], w3[:, 0, gl, 1, :], False, False, R=[xsB[0], xsB[1], w3B], W=[bO])
                S.mm(reg, xf[:, 0, gl, 1:129], w3[:, 1, gl, 0, :], False, False, R=[xfB, w3B], W=[bO])
                S.mm(reg, xf[:, 1, gl, 1:129], w3[:, 1, gl, 1, :], False, False, R=[xfB, w3B], W=[bO])
                S.mm(reg, u8[:, gg, :], wt[:, gl, :], False, True, R=[u8B, wtB], W=[bO])
            t, tB = tf.next()
            t4 = t[:].rearrange("p (g r c) -> p g r c", g=4, r=8)
            S.tt("pool", t4, utm[:, g0:g0 + 4], dtile[:, g0 * 16:(g0 + 4) * 16].rearrange("p (g c) -> p g c", c=16).unsqueeze(2).broadcast_to([128, 4, 8, 16]),
                 ALU.mult, R=[utmB, gbB], W=[tB])
            S.tt("dve", t[:], bo[:], t[:], ALU.add, R=[bO, tB], W=[tB])
            S.act(yg[:, g0:g0 + 4], t4, AF.Gelu_apprx_tanh, R=[tB], W=[xryB])
    k = 0
    for kc in range(8):
        for r4 in range(2):
            bk, bB = cx.bank(PB)
            for rr in range(4):
                r = r4 * 4 + rr
                S.mm(bk[:, rr * 128:(rr + 1) * 128], xry[:, r, kc * 128:(kc + 1) * 128], idb[:], True, True, R=[xryB, consts["B"]], W=[bB])
            dst = yT[:, kc, :].rearrange("p (j r) -> p r j", r=8)[:, r4 * 4:(r4 + 1) * 4, :]
            S.copy("act" if k % 2 == 0 else "dve", dst, bk[:].rearrange("p (r j) -> p r j", r=4), R=[bB], W=[yTB])
            k += 1


def odd_post(cx, st, d, consts, dB, yB, rng, xT, xTB, yT, yTB, kmT, vm, kmB, gt, bt, gbB):
    S = cx.S
    w_in = d["o_w_in"]
    mix = cx.sb(st, "mix_o", [128, 12, 1024], BF16)
    mixB = [[Buf("mixo%d_%d" % (i, b)) for b in range(2)] for i in range(12)]
    wgl_ring = Ring(cx, st, "wgl", [128, 8, 256], BF16, 2)
    wg_ring = Ring(cx, st, "wg_o", [128, 8, 128], BF16, 3)
    tf = Ring(cx, st, "tf_o", [128, 512], F32, 6)
    pring = Ring(cx, st, "pT_o", [128, 512], BF16, 4)
    qm_ring = Ring(cx, st, "qm_o", [128, 512], BF16, 2)
    if "o_w_out_bf" in d:
        wo_ring = Ring(cx, st, "woq_o", [128, 12, 256], BF16, 2)
    else:
        wo = cx.sb(st, "wo_o", [128, 12, 512], BF16)
        woB = Buf("wo_o")
    z = cx.sb(st, "z_o", [128, 4, 1024], F32)
    zB = [Buf("zo%d" % i) for i in range(4)]
    stats = cx.sb(st, "stats_o", [128, 2, 6], F32)
    mv = cx.sb(st, "mv_o", [128, 2], F32)
    rstd = cx.sb(st, "rstd_o", [128, 1], F32)
    stB = Buf("stats_o")
    mscale = 128 ** -0.5
    r0 = rng * 1024
    for f in range(8):
        wgl, wglB = wgl_ring.next()
        S.dma("pool", wgl[:, :, 0:128], wview(d["o_w_glu"], 0, 1024, f * 128, 128), reads=[dB], writes=[wglB])
        S.dma("pool", wgl[:, :, 128:256], wview(d["o_w_glu"], 0, 1024, 1024 + f * 128, 128), reads=[dB], writes=[wglB])
        wg, wgB = wg_ring.next()
        S.dma("pool", wg[:], wview(w_in, 0, 1024, O_GATE + f * 128, 128), reads=[dB], writes=[wgB])
        for b in range(2):
            cs = slice(b * 512, (b + 1) * 512)
            ba, bA = cx.bank(PA)
            for kc in range(8):
                S.mm(ba[:], wgl[:, kc, 0:128], yT[:, kc, cs], kc == 0, kc == 7, R=[wglB, yTB], W=[bA])
            bb, bBb = cx.bank(PA)
            for kc in range(8):
                S.mm(bb[:], wgl[:, kc, 128:256], yT[:, kc, cs], kc == 0, kc == 7, R=[wglB, yTB], W=[bBb])
            bg, bG = cx.bank(PA)
            for kc in range(8):
                S.mm(bg[:], wg[:, kc, :], xT[:, kc, cs], kc == 0, kc == 7, R=[wgB, xTB], W=[bG])
            sg, sgB = tf.next()
            S.act(sg[:], bb[:], AF.Sigmoid, R=[bBb], W=[sgB])
            s2, s2B = tf.next()
            S.act(s2[:], bg[:], AF.Silu, R=[bG], W=[s2B])
            S.tt("dve", sg[:], ba[:], sg[:], ALU.mult, R=[bA, sgB], W=[sgB])
            S.tt("dve", mix[:, f, cs], sg[:], s2[:], ALU.mult, R=[sgB, s2B], W=[mixB[f][b]])
    for b in range(2):
        cs = slice(b * 512, (b + 1) * 512)
        for h in range(4):
            wg, wgB = wg_ring.next()
            S.dma("pool", wg[:], wview(w_in, 0, 1024, O_MEMQ + h * 128, 128), reads=[dB], writes=[wgB])
            bk, bB = cx.bank(PA)
            for kc in range(8):
                S.mm(bk[:], wg[:, kc, :], xT[:, kc, cs], kc == 0, kc == 7, R=[wgB, xTB], W=[bB])
            qm, qmB = qm_ring.next()
            S.copy("act", qm[:], bk[:], R=[bB], W=[qmB])
            wg2, wg2B = wg_ring.next()
            S.dma("pool", wg2[:], wview(w_in, 0, 1024, O_MEMG + h * 128, 128), reads=[dB], writes=[wg2B])
            bk2, bB2 = cx.bank(PA)
            for kc in range(8):
                S.mm(bk2[:], wg2[:, kc, :], xT[:, kc, cs], kc == 0, kc == 7, R=[wg2B, xTB], W=[bB2])
            gm, gmB = tf.next()
            S.act(gm[:], bk2[:], AF.Silu, R=[bB2], W=[gmB])

            def kts(mc, h=h):
                return [(kmT[:, h, mc * 128:(mc + 1) * 128], [kmB])]

            def v_of(mc, h=h):
                return vm[:, mc, h * 128:(h + 1) * 128], [kmB]

            def fin(bo, bO, bs, bS, h=h, gm=gm, gmB=gmB, cs=cs, b=b):
                rec, recB = tf.next()
                S.recip(rec[:], bs[:], R=[bS], W=[recB])
                S.tt("dve", rec[:], bo[:], rec[:], ALU.mult, R=[bO, recB], W=[recB])
                S.tt("dve", mix[:, 8 + h, cs], rec[:], gm[:], ALU.mult, R=[recB, gmB], W=[mixB[8 + h][b]])
            attention_block(cx, kts, [(qm[:], [qmB])], v_of, 2, mscale, pring, consts, fin)
        if "h1" in d:
            S.dma("sp", z[:], d["h1"][r0 + b * 512:r0 + (b + 1) * 512, :].rearrange("(t p) c -> p t c", p=128), reads=[d["h1B"]], writes=zB)
        else:
            S.dma("sp", z[:], d["h_seq"][r0 + b * 512:r0 + (b + 1) * 512, :].rearrange("(t p) c -> p t c", p=128), reads=[dB], writes=zB)
        if "o_w_out_bf" in d:
            for qq in range(4):
                woq, woqB = wo_ring.next()
                csl = slice(qq * 256, (qq + 1) * 256)
                S.dma("sp", woq[:], d["o_w_out_bf"][:, csl].rearrange("(j p) n -> p j n", p=128), reads=[d["wbfB"]], writes=[woqB])
                for t in range(4):
                    bk, bB = cx.bank(PA)
                    for j in range(12):
                        S.mm(bk[:, 0:256], mix[:, j, b * 512 + t * 128:b * 512 + (t + 1) * 128], woq[:, j, :], j == 0, j == 11,
                             R=[mixB[j][b], woqB], W=[bB])
                    zs = z[:, t, csl]
                    S.stt(zs, zs, float(ALPHA), bk[:, 0:256], ALU.mult, ALU.add, R=[bB, zB[t]], W=[zB[t]])
        else:
            for half in range(2):
                S.dma("pool", wo[:], d["o_w_out"][:, half * 512:(half + 1) * 512].rearrange("(j p) n -> p j n", p=128),
                      reads=[dB], writes=[woB])
                for t in range(4):
                    bk, bB = cx.bank(PA)
                    for j in range(12):
                        S.mm(bk[:], mix[:, j, b * 512 + t * 128:b * 512 + (t + 1) * 128], wo[:, j, :], j == 0, j == 11,
                             R=[mixB[j][b], woB], W=[bB])
                    zs = z[:, t, half * 512:(half + 1) * 512]
                    S.stt(zs, zs, float(ALPHA), bk[:], ALU.mult, ALU.add, R=[bB, zB[t]], W=[zB[t]])
        ln_store(cx, consts, z, zB, stats, mv, rstd, stB, gt, bt, gbB, d["y"], r0 + b * 512, yB)


def ln_store(cx, consts, z, zB, stats, mv, rstd, stB, gt, bt, gbB, y, row0, yB):
    S = cx.S
    for t in range(4):
        for i in range(2):
            S.op("dve", lambda e, t=t, i=i: e.bn_stats(stats[:, i, :], z[:, t, i * 512:(i + 1) * 512]), [zB[t]], [stB])
        S.op("dve", lambda e: e.bn_aggr(mv[:], stats[:].rearrange("p a b -> p (a b)")), [stB], [stB])
        S.act(rstd[:], mv[:, 1:2], AF.Sqrt, R=[stB, consts["B"]], W=[stB], bias=consts["eps"][LN_EPS])
        S.recip(rstd[:], rstd[:], R=[stB], W=[stB])
        S.ts("dve", z[:, t, :], z[:, t, :], mv[:, 0:1], rstd[:], ALU.subtract, ALU.mult, R=[zB[t], stB], W=[zB[t]])
        S.tt("pool", z[:, t, :], z[:, t, :], gt[:], ALU.mult, R=[zB[t], gbB], W=[zB[t]])
        S.tt("dve", z[:, t, :], z[:, t, :], bt[:], ALU.add, R=[zB[t], gbB], W=[zB[t]])
        S.dma("sp", y[row0 + t * 128:row0 + (t + 1) * 128, :], z[:, t, :], reads=[zB[t]], writes=[yB])


def _scan_single(cx, doA, doB, lanesA, lanesB, xb, xbB, xsB, carry, carB, aa, ab, a8B, ta, tb, tmB, tbB, st_ring, ps):
    S = cx.S
    act_dirs = []
    if doA:
        act_dirs.append((0, lanesA, list(range(1, 129)), 0))
    if doB:
        act_dirs.append((1, lanesB, list(range(127, -1, -1)), 128))
    prev = {}
    for (di, lanes, cols, cin) in act_dirs:
        S.copy("dve", xb[lanes, :, :, cin], carry[lanes], R=[carB[di]], W=[xsB[di]])
        prev[di] = (carry, carB[di])
    for step in range(128):
        cur = {}
        for (di, lanes, cols, cin) in act_dirs:
            S.tt("dve", ta[di][lanes], aa[lanes], prev[di][0][lanes], ALU.mult, R=[a8B, prev[di][1]], W=[tmB[di]])
        for (di, lanes, cols, cin) in act_dirs:
            S.tt("dve", tb[di][lanes], ab[lanes], prev[di][0][lanes, ::-1, :], ALU.mult, R=[a8B, prev[di][1]], W=[tbB[di]])
        for (di, lanes, cols, cin) in act_dirs:
            S.tt("dve", ta[di][lanes], ta[di][lanes], tb[di][lanes], ALU.add, R=[tmB[di], tbB[di]], W=[tmB[di]])
        for (di, lanes, cols, cin) in act_dirs:
            stt_, sttB = st_ring[di].next()
            col = cols[step]
            S.tt("dve", stt_[lanes], ta[di][lanes], xb[lanes, :, :, col], ALU.add, R=[tmB[di], xbB[di]], W=[sttB])
            S.copy("act", xb[lanes, :, :, col], stt_[lanes], R=[sttB], W=[xsB[di]])
            cur[di] = (stt_, sttB)
        prev = cur
    for (di, lanes, cols, cin) in act_dirs:
        S.copy("dve", carry[lanes], prev[di][0][lanes], R=[prev[di][1]], W=[carB[di]])


EVEN_W = [("w_in", [1024, 6208]), ("conv_w", [31, 1024]), ("conv_b", [1024]), ("conv_ln_g", [1024]),
          ("conv_ln_b", [1024]), ("q_norm", [768]), ("w_uq", [768, 1536]), ("kv_norm", [256]),
          ("w_ukv", [256, 2048]), ("mem_kv", [1024, 1024]), ("w_out", [2560, 1024]), ("ln_g", [1024]), ("ln_b", [1024])]


def build_even_nc():
    nc = bass.Bass("TRN2", target_bir_lowering=False)
    d = {}
    d["x_kv"] = nc.dram_tensor("x_kv", [SEQ, 1024], F32, kind="ExternalInput").ap()
    d["x_halo"] = nc.dram_tensor("x_halo", [NT + 30, 1024], F32, kind="ExternalInput").ap()
    d["pos_kv"] = nc.dram_tensor("pos_kv", [SEQ], I32, kind="ExternalInput").ap()
    d["mem"] = nc.dram_tensor("mem", [256, 1024], F32, kind="ExternalInput").ap()
    d["ident"] = nc.dram_tensor("ident", [128, 128], F32, kind="ExternalInput").ap()
    d["ropec"] = nc.dram_tensor("ropec", [64, 2], F32, kind="ExternalInput").ap()
    for n, shp in EVEN_W:
        d[n] = nc.dram_tensor("e_" + n, shp, F32, kind="ExternalInput").ap()
    d["y"] = nc.dram_tensor("y", [NT, 1024], F32, kind="ExternalOutput").ap()
    st = contextlib.ExitStack()
    cx = Ctx(nc, st)
    consts = load_consts(cx, st, d)
    build_even(nc, cx, d, consts)
    cx.S.emit()
    try:
        st.close()
    except AssertionError:
        pass
    return nc


def host_consts():
    ident = np.eye(128, dtype=np.float32)
    i = np.arange(64) % 32
    invf = (np.float32(10000.0) ** (-(2.0 * i).astype(np.float32) / np.float32(64.0))).astype(np.float32)
    sgn = np.where(np.arange(64) < 32, -1.0, 1.0).astype(np.float32)
    return ident, np.stack([invf, sgn], axis=1).astype(np.float32)


def even_in_maps(x, mem, positions, inputs):
    ident, ropec = host_consts()
    maps = []
    for c in range(8):
        b, half = c // 2, c % 2
        own = slice(half * NT, (half + 1) * NT)
        oth = slice((1 - half) * NT, (2 - half) * NT)
        x_kv = np.concatenate([x[b, own], x[b, oth]], axis=0)
        pos_kv = np.concatenate([positions[b, own], positions[b, oth]], axis=0).astype(np.int32)
        xp = np.zeros((SEQ + 30, 1024), np.float32)
        xp[15:15 + SEQ] = x[b]
        x_halo = xp[half * NT: half * NT + NT + 30]
        m = {"x_kv": np.ascontiguousarray(x_kv), "x_halo": np.ascontiguousarray(x_halo), "pos_kv": pos_kv,
             "mem": np.ascontiguousarray(mem[b]), "ident": ident, "ropec": ropec}
        for n, shp in EVEN_W:
            m["e_" + n] = np.ascontiguousarray(inputs["e_" + n][0].reshape(shp))
        maps.append(m)
    return maps


def run_even(x, mem, positions, inputs):
    nc = build_even_nc()
    res = run_bass_kernel_spmd(nc, even_in_maps(x, mem, positions, inputs), core_ids=list(range(8)))
    out = np.zeros((BATCH, SEQ, 1024), np.float32)
    for c in range(8):
        b, half = c // 2, c % 2
        out[b, half * NT:(half + 1) * NT] = res.results[c]["y"]
    return out


ODD_W = [("o_w_in", [1024, 3072]), ("o_d", [1024]), ("o_w_glu", [1024, 2048]), ("o_mem_kv", [1024, 1024]),
         ("o_w_out", [1536, 1024]), ("o_ln_g", [1024]), ("o_ln_b", [1024])]
S5_P = [("a_re", [64, 64]), ("a_im", [64, 64]), ("log_dt", [64]), ("b_re", [64, 64, 16]), ("b_im", [64, 64, 16]),
        ("c_re", [64, 16, 64]), ("c_im", [64, 16, 64])]


def odd_dram(nc, d):
    d["mem"] = d.get("mem") or nc.dram_tensor("mem", [256, 1024], F32, kind="ExternalInput").ap()
    for n, shp in ODD_W:
        d[n] = nc.dram_tensor(n, shp, F32, kind="ExternalInput").ap()
    for n, shp in S5_P:
        for dr in ("A", "B"):
            d["s_%s_%s" % (n, dr)] = nc.dram_tensor("s_%s_%s" % (n, dr), shp, F32, kind="ExternalInput").ap()
    d["s5e"] = nc.dram_tensor("s5e", [128, 28], F32, kind="ExternalInput").ap()
    d["tmask"] = nc.dram_tensor("tmask", [128, 2, 128], F32, kind="ExternalInput").ap()


def build_odd_nc():
    nc = bass.Bass("TRN2", target_bir_lowering=False)
    d = {}
    d["h_seq"] = nc.dram_tensor("h_seq", [SEQ, 1024], F32, kind="ExternalInput").ap()
    d["ident"] = nc.dram_tensor("ident", [128, 128], F32, kind="ExternalInput").ap()
    odd_dram(nc, d)
    d["y"] = nc.dram_tensor("y", [NT, 1024], F32, kind="ExternalOutput").ap()
    st = contextlib.ExitStack()
    cx = Ctx(nc, st)
    consts = load_consts(cx, st, d)
    build_odd(nc, cx, d, consts)
    cx.S.emit()
    try:
        st.close()
    except AssertionError:
        pass
    return nc


def odd_consts():
    s5e = np.zeros((128, 28), np.float32)
    i = np.arange(8, dtype=np.float32)
    s5e[0:64, 0:8] = 7 - i
    s5e[64:128, 0:8] = i
    s5e[0:64, 8:16] = i - 7
    s5e[64:128, 8:16] = -i
    s5e[0:64, 16:24] = i + 1
    s5e[64:128, 16:24] = 8 - i
    s5e[0:64, 24] = 1.0
    s5e[64:128, 25] = 1.0
    s5e[0:64, 26] = -1.0
    s5e[64:128, 27] = -1.0
    s_idx = np.arange(128) // 16
    tmask = np.zeros((128, 2, 128), np.float32)
    tmask[:, 0, :] = (s_idx[None, :] >= s_idx[:, None])
    tmask[:, 1, :] = (s_idx[:, None] >= s_idx[None, :])
    return s5e, tmask


def odd_param_maps(inputs, half):
    s5e, tmask = odd_consts()
    m = {"s5e": s5e, "tmask": tmask}
    for n, shp in ODD_W:
        m[n] = np.ascontiguousarray(inputs[n][0].reshape(shp))
    dirs = ("f", "b") if half == 0 else ("b", "f")
    for n, shp in S5_P:
        for dr, src in zip(("A", "B"), dirs):
            m["s_%s_%s" % (n, dr)] = np.ascontiguousarray(inputs["o_%s_%s" % (n, src)][0].reshape(shp))
    return m


def run_odd(h, mem, inputs):
    nc = build_odd_nc()
    ident, _ = host_consts()
    maps = []
    for c in range(8):
        b, half = c // 2, c % 2
        hs = h[b] if half == 0 else h[b][::-1]
        m = {"h_seq": np.ascontiguousarray(hs), "mem": np.ascontiguousarray(mem[b]), "ident": ident}
        m.update(odd_param_maps(inputs, half))
        maps.append(m)
    res = run_bass_kernel_spmd(nc, maps, core_ids=list(range(8)))
    out = np.zeros((BATCH, SEQ, 1024), np.float32)
    for c in range(8):
        b, half = c // 2, c % 2
        y = res.results[c]["y"]
        if half == 0:
            out[b, 0:NT] = y
        else:
            out[b, NT:SEQ] = y[::-1]
    return out


PAIRS = [[0, 1], [2, 3], [4, 5], [6, 7]]
ALL_INPUT_NAMES = ["x", "mem", "positions", "e_w_in", "e_conv_w", "e_conv_b", "e_conv_ln_g", "e_conv_ln_b", "e_q_norm",
                   "e_w_uq", "e_kv_norm", "e_w_ukv", "e_mem_kv", "e_w_out", "e_ln_g", "e_ln_b", "o_w_in",
                   "o_a_re_f", "o_a_im_f", "o_log_dt_f", "o_b_re_f", "o_b_im_f", "o_c_re_f", "o_c_im_f",
                   "o_a_re_b", "o_a_im_b", "o_log_dt_b", "o_b_re_b", "o_b_im_b", "o_c_re_b", "o_c_im_b",
                   "o_d", "o_w_glu", "o_mem_kv", "o_w_out", "o_ln_g", "o_ln_b"]


def build_fused_nc():
    nc = bass.Bass("TRN2", target_bir_lowering=False)
    d = {}
    d["x_kv"] = nc.dram_tensor("x_kv", [SEQ, 1024], F32, kind="ExternalInput").ap()
    d["x_halo"] = nc.dram_tensor("x_halo", [NT + 30, 1024], F32, kind="ExternalInput").ap()
    d["pos_kv"] = nc.dram_tensor("pos_kv", [SEQ], I32, kind="ExternalInput").ap()
    d["mem"] = nc.dram_tensor("mem", [256, 1024], F32, kind="ExternalInput").ap()
    d["ident"] = nc.dram_tensor("ident", [128, 128], F32, kind="ExternalInput").ap()
    d["rident"] = nc.dram_tensor("rident", [128, 128], F32, kind="ExternalInput").ap()
    d["ropec"] = nc.dram_tensor("ropec", [64, 2], F32, kind="ExternalInput").ap()
    for n, shp in EVEN_W:
        d[n] = nc.dram_tensor("e_" + n, shp, F32, kind="ExternalInput").ap()
    odd_dram(nc, d)
    h1 = nc.dram_tensor("h1_bounce", [NT, 1024], F32).ap()
    hsum = nc.dram_tensor("hsum_bounce", [NT, 1024], F32).ap()
    yout = nc.dram_tensor("y", [NT, 1024], F32, kind="ExternalOutput").ap()
    h1B = Buf("h1")
    hsumB = Buf("hsum")
    st = contextlib.ExitStack()
    cx = Ctx(nc, st)
    consts = load_consts(cx, st, d)
    d["y"] = h1
    d["yB"] = h1B
    d["w_in_bf"] = nc.dram_tensor("wbf_in", [1024, 6208], BF16, kind="Internal").ap()
    d["w_out_bf"] = nc.dram_tensor("wbf_out", [2560, 1024], BF16, kind="Internal").ap()
    d["wbfB"] = Buf("wbf")
    d["o_w_out_bf"] = nc.dram_tensor("wbf_oout", [1536, 1024], BF16, kind="Internal").ap()
    a8 = cx.sb(st, "a8", [128, 2, NG], F32)
    d["a8"] = (a8, Buf("a8"))
    d["scr"] = make_scr(nc)
    preB = Buf("dram_in_pre")

    def hook_after_mla(stA):
        d["s5pre"] = s5_prefetch(cx, stA, d, consts, preB, persist=True)

    def hook_end():
        s5_tables(cx, d, consts, preB, d["a8"][0], d["a8"][1], d["scr"], pre=d["s5pre"])
    d["hook_after_mla"] = hook_after_mla
    d["hook_end"] = hook_end
    build_even(nc, cx, d, consts)
    d["y"] = yout
    d["h1"] = h1
    d["h1B"] = h1B
    d["hsum"] = hsum
    d["hsumB"] = hsumB
    build_odd(nc, cx, d, consts)
    cx.S.emit()
    try:
        st.close()
    except AssertionError:
        pass
    return nc


def fused_in_maps(x, mem, positions, inputs):
    ident, ropec = host_consts()
    rident = np.ascontiguousarray(ident[::-1])
    maps = []
    for c in range(8):
        b, half = c // 2, c % 2
        own = slice(half * NT, (half + 1) * NT)
        oth = slice((1 - half) * NT, (2 - half) * NT)
        xp = np.zeros((SEQ + 30, 1024), np.float32)
        xp[15:15 + SEQ] = x[b]
        x_halo = xp[half * NT: half * NT + NT + 30]
        xo, xt = x[b, own], x[b, oth]
        po, pt = positions[b, own], positions[b, oth]
        cw = inputs["e_conv_w"][0].reshape(31, 1024)
        if half == 1:
            x_halo, xo, xt, po, pt, cw = x_halo[::-1], xo[::-1], xt[::-1], po[::-1], pt[::-1], cw[::-1]
        m = {"x_kv": np.ascontiguousarray(np.concatenate([xo, xt], axis=0)), "x_halo": np.ascontiguousarray(x_halo),
             "pos_kv": np.ascontiguousarray(np.concatenate([po, pt], axis=0)).astype(np.int32),
             "mem": np.ascontiguousarray(mem[b]), "ident": ident, "rident": rident, "ropec": ropec}
        for n, shp in EVEN_W:
            m["e_" + n] = np.ascontiguousarray(inputs["e_" + n][0].reshape(shp))
        m["e_conv_w"] = np.ascontiguousarray(cw)
        m.update(odd_param_maps(inputs, half))
        maps.append(m)
    return maps


def kernel(**inputs):
    inputs = {k: np.asarray(inputs[k]) for k in ALL_INPUT_NAMES}
    x = inputs["x"].astype(np.float32)
    mem = inputs["mem"].astype(np.float32)
    pos = inputs["positions"]
    nc = build_fused_nc()
    res = run_bass_kernel_spmd(nc, fused_in_maps(x, mem, pos, inputs), core_ids=list(range(8)))
    out = np.zeros((BATCH, SEQ, 1024), np.float32)
    for c in range(8):
        b, half = c // 2, c % 2
        y = res.results[c]["y"]
        if half == 0:
            out[b, 0:NT] = y
        else:
            out[b, NT:SEQ] = y[::-1]
    return out
```

```python
import contextlib
import math
import numpy as np
import concourse.bass as bass
import concourse.mybir as mybir
from concourse.bass_utils import run_bass_kernel_spmd

F32 = mybir.dt.float32
BF16 = mybir.dt.bfloat16
I32 = mybir.dt.int32
AF = mybir.ActivationFunctionType
ALU = mybir.AluOpType

D_MODEL = 1024
SEQ = 4096
BATCH = 4
NT = 2048
ALPHA = (2 * 2) ** 0.25
LN_EPS = 1e-5
RMS_EPS = 1e-6
TWO_PI = 2.0 * math.pi


class Buf:
    __slots__ = ("name", "lw", "rd", "excl")

    def __init__(self, name, excl=False):
        self.name = name
        self.excl = excl
        self.lw = None
        self.rd = {}


class Sched:
    COMPUTE = ("pe", "act", "dve", "pool")

    def __init__(self, nc, stack, n_dma_sems=12, same_engine_sync=True):
        self.nc = nc
        self.eng = {"pe": nc.tensor, "act": nc.scalar, "dve": nc.vector,
                    "pool": nc.gpsimd, "sp": nc.sync}
        self.sems = {}
        for e in self.COMPUTE:
            self.sems[e] = stack.enter_context(nc.semaphore("sem_" + e))
        self.cnt = {e: 0 for e in self.COMPUTE}
        self.dma_ring = {}
        for q in ("sp", "pool", "act"):
            n = n_dma_sems if q != "act" else 4
            ring = []
            for i in range(n):
                key = "dma_%s_%d" % (q, i)
                self.sems[key] = stack.enter_context(nc.semaphore(key))
                ring.append(key)
            self.dma_ring[q] = {"keys": ring, "cnt": [0] * n, "next": 0}
        self.sems["cc"] = stack.enter_context(nc.semaphore("cc_sem"))
        self.cc_cnt = 0
        self.cc_dummy = stack.enter_context(nc.sbuf_tensor("cc_dummy", [128, 8], F32))[:]
        self.seen = {e: {} for e in ("pe", "act", "dve", "pool", "sp")}
        self.prog = {e: [] for e in ("pe", "act", "dve", "pool", "sp")}
        self.same_engine_sync = same_engine_sync
        self.ninst = 0

    def _wait(self, e, semkey, val):
        if self.seen[e].get(semkey, 0) >= val:
            return
        self.seen[e][semkey] = val
        self.prog[e].append(("wait", semkey, val))

    def _deps(self, e, reads, writes):
        deps = {}

        def add(ev):
            if ev is None:
                return
            k, v = ev
            if deps.get(k, 0) < v:
                deps[k] = v
        for b in reads:
            add(b.lw)
            if b.excl:
                for k, v in b.rd.items():
                    if k != e:
                        add((k, v))
        for b in writes:
            add(b.lw)
            for k, v in b.rd.items():
                add((k, v))
        return deps

    def op(self, e, fn, reads=(), writes=()):
        deps = self._deps(e, reads, writes)
        for k, v in deps.items():
            if k == e:
                if e == "pe" or not self.same_engine_sync:
                    continue
            self._wait(e, k, v)
        self.cnt[e] += 1
        ev = (e, self.cnt[e])
        self.prog[e].append(("inst", fn, e, 1))
        for b in writes:
            b.lw = ev
            b.rd = {}
        for b in reads:
            if b.rd.get(e, 0) < ev[1]:
                b.rd[e] = ev[1]
        self.ninst += 1
        return ev

    def dma(self, q, out, in_, reads=(), writes=(), **kw):
        ring = self.dma_ring[q]
        s = ring["next"]
        ring["next"] = (s + 1) % len(ring["keys"])
        key = ring["keys"][s]
        deps = self._deps(q, reads, writes)
        for k, v in deps.items():
            self._wait(q, k, v)
        if ring["cnt"][s] > 0:
            self._wait(q, key, 16 * ring["cnt"][s])
        ring["cnt"][s] += 1
        ev = (key, 16 * ring["cnt"][s])

        def fn(eng, out=out, in_=in_, kw=kw):
            return eng.dma_start(out=out, in_=in_, **kw)
        self.prog[q].append(("inst", fn, key, 16))
        if q in self.COMPUTE:
            pass
        for b in writes:
            b.lw = ev
            b.rd = {}
        for b in reads:
            if b.rd.get(key, 0) < ev[1]:
                b.rd[key] = ev[1]
        self.ninst += 1
        return ev

    def collective(self, kind, op, groups, in_ap, out_ap, reads=(), writes=()):
        if "cc" not in self.sems:
            raise RuntimeError("no cc semaphore")
        deps = self._deps("pool", reads, writes)
        for k, v in deps.items():
            self._wait("pool", k, v)
        self.cc_cnt += 1
        ev = ("cc", self.cc_cnt)

        def fn(eng):
            return eng.collective_compute(kind, op, replica_groups=groups, ins=[in_ap.opt()], outs=[out_ap.opt()])
        self.prog["pool"].append(("inst", fn, "cc", 1))
        self._wait("pool", "cc", self.cc_cnt)
        dummy = self.cc_dummy
        return self.op("pool", lambda e: e.memset(dummy, 0.0), reads=(), writes=list(writes))

    def mm(self, out, lhsT, rhs, start, stop, R=(), W=()):
        return self.op("pe", lambda e: e.matmul(out, lhsT, rhs, start=start, stop=stop), R, W)

    def act(self, out, in_, func, R=(), W=(), bias=None, scale=None, eng="act"):
        kw = {}
        if bias is not None:
            kw["bias"] = bias
        if scale is not None:
            kw["scale"] = scale
        return self.op("act", lambda e: e.activation(out, in_, func, **kw), R, W)

    def tt(self, eng, out, in0, in1, op, R=(), W=()):
        return self.op(eng, lambda e: e.tensor_tensor(out, in0, in1, op), R, W)

    def ts(self, eng, out, in0, s1, s2, op0, op1=None, R=(), W=()):
        if op1 is None:
            return self.op(eng, lambda e: e.tensor_scalar(out, in0, s1, None, op0), R, W)
        return self.op(eng, lambda e: e.tensor_scalar(out, in0, s1, s2, op0, op1), R, W)

    def stt(self, out, in0, scalar, in1, op0, op1, R=(), W=()):
        return self.op("dve", lambda e: e.scalar_tensor_tensor(out, in0, scalar, in1, op0, op1), R, W)

    def copy(self, eng, out, in_, R=(), W=()):
        if eng == "act":
            return self.op("act", lambda e: e.copy(out, in_), R, W)
        return self.op(eng, lambda e: e.tensor_copy(out, in_), R, W)

    def recip(self, out, in_, R=(), W=()):
        return self.op("dve", lambda e: e.reciprocal(out, in_), R, W)

    def barrier(self):
        evs = [(e, self.cnt[e]) for e in self.COMPUTE if self.cnt[e] > 0]
        for q, ring in self.dma_ring.items():
            for key, c in zip(ring["keys"], ring["cnt"]):
                if c > 0:
                    evs.append((key, 16 * c))
        for e in ("pe", "act", "dve", "pool", "sp"):
            for k, v in evs:
                if k == e:
                    continue
                self._wait(e, k, v)

    def wait_all_on(self, e, bufs):
        for b in bufs:
            if b.lw is not None:
                self._wait(e, b.lw[0], b.lw[1])

    def emit(self):
        nc = self.nc
        sems = self.sems
        prog = self.prog
        with nc.Block() as block:
            def make(e):
                def body(eng):
                    for item in prog[e]:
                        if item[0] == "wait":
                            eng.wait_ge(sems[item[1]], item[2])
                        else:
                            _, fn, key, inc = item
                            fn(eng).then_inc(sems[key], inc)
                return body
            block.tensor(make("pe"))
            block.scalar(make("act"))
            block.vector(make("dve"))
            block.gpsimd(make("pool"))
            block.sync(make("sp"))


STOP = [99]


class StopBuild(Exception):
    pass


def stop_at(level):
    if STOP[0] <= level:
        raise StopBuild()


class Ctx:
    def __init__(self, nc, stack):
        self.nc = nc
        self.stack = stack
        self.S = Sched(nc, stack)
        self.banks = []
        for i in range(8):
            t = stack.enter_context(nc.psum_tensor("bank%d" % i, [128, 512], F32))
            self.banks.append((t, Buf("bank%d" % i, excl=True)))
        self.rr = {}
        self.uid = 0

    def sb(self, stack, name, shape, dt):
        self.uid += 1
        return stack.enter_context(self.nc.sbuf_tensor("%s_%d" % (name, self.uid), shape, dt))

    def bank(self, pool):
        i = self.rr.get(pool, 0)
        self.rr[pool] = (i + 1) % len(pool)
        return self.banks[pool[i]]


class Ring:
    def __init__(self, cx, stack, name, shape, dt, n):
        self.tiles = [(cx.sb(stack, name, shape, dt), Buf(name + str(i))) for i in range(n)]
        self.i = 0

    def next(self):
        t = self.tiles[self.i]
        self.i = (self.i + 1) % len(self.tiles)
        return t


PA = (0, 1, 2, 3)
PB = (4, 5)
PC = (6, 7)

E_CONV_IN, E_CONV_GATE, E_CQ, E_CKV, E_KR, E_MLAG, E_MEMQ, E_MEMG = 0, 2048, 3072, 3840, 4096, 4160, 5184, 5696


def wview(w, r0, nrows, c0, ncols):
    return w[r0:r0 + nrows, c0:c0 + ncols].rearrange("(kc p) n -> p kc n", p=128)


def load_consts(cx, st, d):
    S = cx.S
    c = {}
    c["idf"] = cx.sb(st, "idf", [128, 128], F32)
    c["idb"] = cx.sb(st, "idb", [128, 128], BF16)
    c["onef"] = cx.sb(st, "onef", [128, 128], F32)
    c["oneb"] = cx.sb(st, "oneb", [128, 128], BF16)
    c["B"] = Buf("consts")
    S.dma("sp", c["idf"][:], d["ident"], reads=[], writes=[c["B"]])
    S.dma("pool", c["idb"][:], d["ident"], reads=[], writes=[c["B"]])
    if "rident" in d:
        c["ridb"] = cx.sb(st, "ridb", [128, 128], BF16)
        S.dma("pool", c["ridb"][:], d["rident"], reads=[], writes=[c["B"]])
    S.op("dve", lambda e: e.memset(c["onef"][:], 1.0), writes=[c["B"]])
    S.op("dve", lambda e: e.memset(c["oneb"][:], 1.0), writes=[c["B"]])
    c["eps"] = {}
    for v in (RMS_EPS, LN_EPS):
        t = cx.sb(st, "eps", [128, 1], F32)
        S.op("dve", lambda e, t=t, v=v: e.memset(t[:], float(v)), writes=[c["B"]])
        c["eps"][v] = t[:]
    return c


def transpose_rows(cx, xr, xrB, nrows_last, ntiles, xT, xTB, consts, col0=0, rev=False):
    S = cx.S
    idb = consts["ridb"] if rev else consts["idb"]
    full = ntiles if nrows_last == 128 else ntiles - 1
    k = 0
    for kc in range(8):
        for t0 in range(0, full, 4):
            nt = min(4, full - t0)
            bk, bB = cx.bank(PB)
            for t in range(nt):
                srct = (ntiles - 1 - (t0 + t)) if rev else (t0 + t)
                S.mm(bk[:, t * 128:(t + 1) * 128], xr[:, srct, kc * 128:(kc + 1) * 128], idb[:],
                     True, True, R=[xrB, consts["B"]], W=[bB])
            eng = "act" if k % 2 == 0 else "dve"
            k += 1
            S.copy(eng, xT[:, kc, col0 + t0 * 128: col0 + (t0 + nt) * 128], bk[:, 0:nt * 128], R=[bB], W=[xTB])
        if nrows_last != 128:
            bk, bB = cx.bank(PB)
            n = nrows_last
            S.mm(bk[:, 0:n], xr[0:n, ntiles - 1, kc * 128:(kc + 1) * 128], idb[0:n, 0:n], True, True,
                 R=[xrB, consts["B"]], W=[bB])
            S.copy("dve", xT[:, kc, col0 + full * 128: col0 + full * 128 + n], bk[:, 0:n], R=[bB], W=[xTB])


def attention_block(cx, kts, qparts, v_of, nk, scale, pring, consts, out_cb, acc=None, q_alt=None):
    S = cx.S
    bo, bO = cx.banks[6]
    bs, bS = cx.banks[7]
    oneb = consts["oneb"]

    def scores(kc):
        bk, bB = cx.bank(PA)
        parts = kts(kc)
        for i, ((l, lb), (r, rb)) in enumerate(zip(parts, qparts)):
            S.mm(bk[:], l, r, i == 0, i == len(parts) - 1, R=list(lb) + list(rb), W=[bB])
        return [(bk, bB)]

    def scores_pair(kc):
        res = []
        pp = []
        for k2 in (kc, kc + 1):
            bk, bB = cx.bank(PA)
            parts = kts(k2)
            (l, lb), (r, rb) = parts[0], qparts[0]
            S.mm(bk[:], l, r, True, False, R=list(lb) + list(rb), W=[bB])
            res.append((bk, bB))
            pp.append(parts[1])
        for j, (l, lb) in enumerate(pp):
            r, rb = qparts[1] if j == 0 else q_alt
            S.mm(res[j][0][:], l, r, False, True, R=list(lb) + list(rb), W=[res[j][1]])
        return res
    step = 2 if q_alt is not None else 1
    fn = scores_pair if q_alt is not None else scores
    nxt = fn(0)
    for kc0 in range(0, nk, step):
        cur = nxt
        if kc0 + step < nk:
            nxt = fn(kc0 + step)
        for j, (bk, bB) in enumerate(cur):
            kc = kc0 + j
            p, pB = pring.next()
            S.act(p[:], bk[:], AF.Exp, R=[bB], W=[pB], scale=scale)
            v, vb = v_of(kc)
            S.mm(bo[:], v, p[:], kc == 0, kc == nk - 1, R=list(vb) + [pB], W=[bO])
            if acc is None:
                S.mm(bs[:], oneb[:], p[:], kc == 0, kc == nk - 1, R=[consts["B"], pB], W=[bS])
            elif kc % 3 == 2:
                S.mm(bs[:], oneb[:], p[:], kc == 2, False, R=[consts["B"], pB], W=[bS])
            elif kc == 0:
                S.copy("dve", acc[0][:], p[:], R=[pB], W=[acc[1]])
            else:
                S.tt("dve", acc[0][:], acc[0][:], p[:], ALU.add, R=[acc[1], pB], W=[acc[1]])
    if acc is not None:
        S.mm(bs[:], consts["onef"][:], acc[0][:], False, True, R=[consts["B"], acc[1]], W=[bS])
    out_cb(bo, bO, bs, bS)


def build_even(nc, cx, d, consts):
    S = cx.S
    dB = Buf("dram_in")
    yB = d.get("yB") or Buf("y")
    w_in = d["w_in"]

    try:
        _build_even_body(nc, cx, d, consts, S, dB, yB, w_in)
    except StopBuild:
        pass
    S.barrier()
    S.wait_all_on("sp", [yB])


def _build_even_body(nc, cx, d, consts, S, dB, yB, w_in):
    with contextlib.ExitStack() as stA:
        omla = cx.sb(stA, "omla", [128, 8, NT], BF16)
        omlaB = [Buf("omla%d" % h) for h in range(8)]
        par = cx.sb(stA, "par", [128, 64], F32)
        parB = Buf("par")
        S.dma("sp", par[:, 0:8], d["conv_b"].rearrange("(c p) -> p c", p=128), reads=[dB], writes=[parB], allow_slow_non_contiguous=True)
        S.dma("sp", par[:, 8:16], d["conv_ln_g"].rearrange("(c p) -> p c", p=128), reads=[dB], writes=[parB], allow_slow_non_contiguous=True)
        S.dma("sp", par[:, 16:24], d["conv_ln_b"].rearrange("(c p) -> p c", p=128), reads=[dB], writes=[parB], allow_slow_non_contiguous=True)
        S.dma("sp", par[:, 24:30], d["q_norm"].rearrange("(c p) -> p c", p=128), reads=[dB], writes=[parB], allow_slow_non_contiguous=True)
        S.dma("sp", par[:, 30:32], d["kv_norm"].rearrange("(c p) -> p c", p=128), reads=[dB], writes=[parB], allow_slow_non_contiguous=True)
        S.dma("sp", par[0:64, 32:34], d["ropec"], reads=[dB], writes=[parB])

        with contextlib.ExitStack() as st1:
            ckvn = cx.sb(st1, "ckvn", [128, 2, SEQ], BF16)
            ckvnB = [Buf("ckvn%d" % i) for i in range(8)]
            krt = cx.sb(st1, "krt", [128, SEQ], BF16)
            krt2B = Buf("krt2")
            krtB = [Buf("krt%d" % i) for i in range(8)]
            cqg = cx.sb(st1, "cqg", [128, 6, NT], BF16)
            cqgB = [Buf("cqg%d" % i) for i in range(4)]
            rq = cx.sb(st1, "rq", [128, NT], F32)
            csq = cx.sb(st1, "csq", [64, NT], F32)
            snq = cx.sb(st1, "snq", [64, NT], F32)
            rqB = [Buf("rq%d" % i) for i in range(4)]
            posi = cx.sb(st1, "posi", [64, SEQ], I32)
            posB = Buf("posi")
            S.dma("sp", posi[:], d["pos_kv"].partition_broadcast(64), reads=[dB], writes=[posB])

            stop_at(1)
            with contextlib.ExitStack() as stK:
                wk = cx.sb(stK, "wk", [128, 8, 320], BF16)
                wks = cx.sb(stK, "wks", [128, 8, 64], BF16)
                wq = cx.sb(stK, "wq", [128, 8, 768], BF16)
                wB = Buf("wK")
                S.dma("pool", wk[:], wview(w_in, 0, 1024, E_CKV, 320), reads=[dB], writes=[wB])
                S.dma("pool", wks[:, :, 0:32], wview(w_in, 0, 1024, E_KR + 32, 32), reads=[dB], writes=[wB])
                S.dma("pool", wks[:, :, 32:64], wview(w_in, 0, 1024, E_KR, 32), reads=[dB], writes=[wB])
                for j in range(2):
                    S.dma("pool", wq[:, :, j * 384:(j + 1) * 384], wview(w_in, 0, 1024, E_CQ + j * 384, 384), reads=[dB], writes=[wB])
                xr_ring = Ring(cx, stK, "xr", [128, 4, 1024], BF16, 2)
                xT_ring = Ring(cx, stK, "xT", [128, 8, 512], BF16, 2)
                sq_ring = Ring(cx, stK, "sq", [128, 512], F32, 3)
                rr_ring = Ring(cx, stK, "rr", [128, 512], F32, 2)
                tmpf = [(cx.sb(stK, "tmpf", [128, 512], F32), Buf("tmpf%d" % i)) for i in range(3)]
                tmpi = (cx.sb(stK, "tmpi", [64, 512], I32), Buf("tmpi"))
                posf = (cx.sb(stK, "posf", [64, 512], F32), Buf("posf"))
                cs_t = (cx.sb(stK, "cs_t", [64, 512], F32), Buf("cs_t"))
                sn_t = (cx.sb(stK, "sn_t", [64, 512], F32), Buf("sn_t"))
                for tb in range(8):
                    own = tb < 4
                    if tb == 1:
                        stop_at(2)
                    xr, xrB = xr_ring.next()
                    S.dma("pool", xr[:], d["x_kv"][tb * 512:(tb + 1) * 512, :].rearrange("(t p) c -> p t c", p=128),
                          reads=[dB], writes=[xrB])
                    xT, xTB = xT_ring.next()
                    stop_at(1.1)
                    transpose_rows(cx, xr, xrB, 128, 4, xT, xTB, consts)
                    cols = slice(tb * 512, (tb + 1) * 512)
                    stop_at(1.2)
                    S.copy("dve", posf[0][:], posi[:, cols], R=[posB], W=[posf[1]])
                    _rope_blk(cx, posf, cs_t, sn_t, par, parB, tmpf, tmpi)
                    stop_at(1.3)
                    kvb = []
                    for fc in range(2):
                        bk, bB = cx.bank(PA)
                        for kc in range(8):
                            S.mm(bk[:], wk[:, kc, fc * 128:(fc + 1) * 128], xT[:, kc, :], kc == 0, kc == 7, R=[wB, xTB], W=[bB])
                        kvb.append((bk, bB))
                    bs, bS = cx.bank(PC)
                    for fc in range(2):
                        sq, sqB = sq_ring.next()
                        S.act(sq[:], kvb[fc][0][:], AF.Square, R=[kvb[fc][1]], W=[sqB])
                        S.mm(bs[:], consts["onef"][:], sq[:], fc == 0, fc == 1, R=[consts["B"], sqB], W=[bS])
                    r1, r1B = rr_ring.next()
                    _sqrt_eps(cx, r1, r1B, bs, bS, 1.0 / 256.0, RMS_EPS, consts)
                    S.recip(r1[:], r1[:], R=[r1B], W=[r1B])
                    for fc in range(2):
                        S.stt(ckvn[:, fc, cols], kvb[fc][0][:], par[:, 30 + fc:31 + fc], r1[:], ALU.mult, ALU.mult,
                              R=[kvb[fc][1], parB, r1B], W=[ckvnB[tb]])
                    stop_at(1.4)
                    ba, bA = cx.bank(PA)
                    for kc in range(8):
                        S.mm(ba[0:64, :], wk[:, kc, 256:320], xT[:, kc, :], kc == 0, kc == 7, R=[wB, xTB], W=[bA])
                    bb, bBb = cx.bank(PA)
                    for kc in range(8):
                        S.mm(bb[0:64, :], wks[:, kc, :], xT[:, kc, :], kc == 0, kc == 7, R=[wB, xTB], W=[bBb])
                    t1, t1B = tmpf[0]
                    t2, t2B = tmpf[1]
                    S.tt("dve", t1[0:64, :], ba[0:64, :], cs_t[0][:], ALU.mult, R=[bA, cs_t[1]], W=[t1B])
                    S.tt("dve", t2[0:64, :], bb[0:64, :], sn_t[0][:], ALU.mult, R=[bBb, sn_t[1]], W=[t2B])
                    S.tt("dve", krt[0:64, cols], t1[0:64, :], t2[0:64, :], ALU.add, R=[t1B, t2B], W=[krtB[tb]])
                    stop_at(1.5)
                    if own:
                        bs, bS = cx.bank(PC)
                        for fc in range(6):
                            bk, bB = cx.bank(PA)
                            for kc in range(8):
                                S.mm(bk[:], wq[:, kc, fc * 128:(fc + 1) * 128], xT[:, kc, :], kc == 0, kc == 7, R=[wB, xTB], W=[bB])
                            sq, sqB = sq_ring.next()
                            S.act(sq[:], bk[:], AF.Square, R=[bB], W=[sqB])
                            S.ts("dve", cqg[:, fc, cols], bk[:], par[:, 24 + fc:25 + fc], None, ALU.mult, R=[bB, parB], W=[cqgB[tb]])
                            S.mm(bs[:], consts["onef"][:], sq[:], fc == 0, fc == 5, R=[consts["B"], sqB], W=[bS])
                        _sqrt_eps(cx, rq[:, cols], rqB[tb], bs, bS, 1.0 / 768.0, RMS_EPS, consts, is_ap=True)
                        S.recip(rq[:, cols], rq[:, cols], R=[rqB[tb]], W=[rqB[tb]])
                        S.tt("dve", csq[:, cols], cs_t[0][:], rq[0:64, cols], ALU.mult, R=[cs_t[1], rqB[tb]], W=[rqB[tb]])
                        S.tt("dve", snq[:, cols], sn_t[0][:], rq[0:64, cols], ALU.mult, R=[sn_t[1], rqB[tb]], W=[rqB[tb]])
            S.dma("sp", krt[64:128, :], krt[0:64, :], reads=krtB, writes=[krt2B])
            S.barrier()
            stop_at(3)
            if "w_in_bf" in d:
                for kc in range(8):
                    rows = slice(kc * 128, (kc + 1) * 128)
                    for c0 in range(0, 6208, 2048):
                        n = min(2048, 6208 - c0)
                        S.dma("pool", d["w_in_bf"][rows, c0:c0 + n], w_in[rows, c0:c0 + n], reads=[dB], writes=[d["wbfB"]])
                for j in range(20):
                    rows = slice(j * 128, (j + 1) * 128)
                    S.dma("pool", d["w_out_bf"][rows, :], d["w_out"][rows, :], reads=[dB], writes=[d["wbfB"]])
            with contextlib.ExitStack() as stM:
                knt = cx.sb(stM, "knt", [128, SEQ], BF16)
                kntB = Buf("knt")
                vh = cx.sb(stM, "vh", [128, 32, 128], BF16)
                vhB = Buf("vh")
                qn = cx.sb(stM, "qn", [128, NT], BF16)
                qr = cx.sb(stM, "qr", [128, NT], BF16)
                q2B = Buf("qr2")
                qB = [Buf("q%d" % i) for i in range(4)]
                wring = Ring(cx, stM, "wkvh", [128, 2, 256], BF16, 2)
                wqring = Ring(cx, stM, "wqh", [128, 6, 256], BF16, 2)
                pring = Ring(cx, stM, "pT", [128, 512], BF16, 4)
                acc_ring = Ring(cx, stM, "accs", [128, 512], F32, 2)
                t1, t1B = (cx.sb(stM, "mt1", [128, 512], F32), Buf("mt1"))
                t2, t2B = (cx.sb(stM, "mt2", [128, 512], F32), Buf("mt2"))
                rec, recB = (cx.sb(stM, "rec", [128, 512], F32), Buf("rec"))
                scale = (128 + 64) ** -0.5
                for h in range(8):
                    if h == 1:
                        stop_at(4)
                    wkv, wkvB = wring.next()
                    S.dma("pool", wkv[:], wview(d["w_ukv"], 0, 256, h * 256, 256), reads=[dB], writes=[wkvB])
                    wqh, wqhB = wqring.next()
                    S.dma("pool", wqh[:, :, 0:192], wview(d["w_uq"], 0, 768, h * 192, 192), reads=[dB], writes=[wqhB])
                    S.dma("pool", wqh[:, :, 192:224], wview(d["w_uq"], 0, 768, h * 192 + 160, 32), reads=[dB], writes=[wqhB])
                    S.dma("pool", wqh[:, :, 224:256], wview(d["w_uq"], 0, 768, h * 192 + 128, 32), reads=[dB], writes=[wqhB])
                    for tb in range(8):
                        bk, bB = cx.bank(PA)
                        for kc in range(2):
                            S.mm(bk[:], wkv[:, kc, 0:128], ckvn[:, kc, tb * 512:(tb + 1) * 512], kc == 0, kc == 1,
                                 R=[wkvB, ckvnB[tb]], W=[bB])
                        S.copy("act" if tb % 2 == 0 else "dve", knt[:, tb * 512:(tb + 1) * 512], bk[:], R=[bB], W=[kntB])
                    for tb in range(8):
                        bk, bB = cx.bank(PA)
                        for j in range(4):
                            c0 = tb * 512 + j * 128
                            for kc in range(2):
                                S.mm(bk[:, j * 128:(j + 1) * 128], ckvn[:, kc, c0:c0 + 128], wkv[:, kc, 128:256], kc == 0, kc == 1,
                                     R=[wkvB, ckvnB[tb]], W=[bB])
                        S.copy("dve" if tb % 2 == 0 else "act", vh[:, tb * 4:(tb + 1) * 4, :],
                               bk[:].rearrange("p (j d) -> p j d", j=4), R=[bB], W=[vhB])
                    for qb in range(4):
                        cols = slice(qb * 512, (qb + 1) * 512)
                        bn, bN = cx.bank(PA)
                        for kc in range(6):
                            S.mm(bn[:], wqh[:, kc, 0:128], cqg[:, kc, cols], kc == 0, kc == 5, R=[wqhB, cqgB[qb]], W=[bN])
                        ba, bA = cx.bank(PA)
                        for kc in range(6):
                            S.mm(ba[0:64, :], wqh[:, kc, 128:192], cqg[:, kc, cols], kc == 0, kc == 5, R=[wqhB, cqgB[qb]], W=[bA])
                        bb, bBb = cx.bank(PA)
                        for kc in range(6):
                            S.mm(bb[0:64, :], wqh[:, kc, 192:256], cqg[:, kc, cols], kc == 0, kc == 5, R=[wqhB, cqgB[qb]], W=[bBb])
                        S.tt("dve", qn[:, cols], bn[:], rq[:, cols], ALU.mult, R=[bN, rqB[qb]], W=[qB[qb]])
                        S.tt("dve", t1[0:64, :], ba[0:64, :], csq[:, cols], ALU.mult, R=[bA, rqB[qb]], W=[t1B])
                        S.tt("dve", t2[0:64, :], bb[0:64, :], snq[:, cols], ALU.mult, R=[bBb, rqB[qb]], W=[t2B])
                        S.tt("dve", qr[0:64, cols], t1[0:64, :], t2[0:64, :], ALU.add, R=[t1B, t2B], W=[qB[qb]])
                    S.dma("sp", qr[64:128, :], qr[0:64, :], reads=qB, writes=[q2B])
                    for qb in range(4):
                        cols = slice(qb * 512, (qb + 1) * 512)

                        def kts(kc):
                            ks = slice(kc * 128, (kc + 1) * 128)
                            if kc % 2 == 0:
                                return [(knt[:, ks], [kntB]), (krt[0:64, ks], [krtB[kc // 4]])]
                            return [(knt[:, ks], [kntB]), (krt[64:128, ks], [krt2B])]

                        def v_of(kc):
                            return vh[:, kc, :], [vhB]

                        def fin(bo, bO, bs, bS, h=h, cols=cols):
                            S.recip(rec[:], bs[:], R=[bS], W=[recB])
                            S.tt("dve", omla[:, h, cols], bo[:], rec[:], ALU.mult, R=[bO, recB], W=[omlaB[h]])
                        attention_block(cx, kts, [(qn[:, cols], [qB[qb]]), (qr[0:64, cols], [qB[qb]])], v_of, 32, scale,
                                        pring, consts, fin, acc=acc_ring.next(), q_alt=(qr[64:128, cols], [q2B]))
        S.barrier()
        stop_at(5)
        if "hook_after_mla" in d:
            d["hook_after_mla"](stA)
        with contextlib.ExitStack() as st2:
            _even_ranges(cx, st2, d, consts, dB, yB, par, parB, omla, omlaB)
        if "hook_end" in d:
            S.barrier()
            d["hook_end"]()


def _sqrt_eps(cx, out, outB, bs, bS, scale, eps, consts, is_ap=False):
    S = cx.S
    o = out if is_ap else out[:]
    S.act(o, bs[:], AF.Sqrt, R=[bS, consts["B"]], W=[outB], scale=scale, bias=consts["eps"][eps])


def _rope_blk(cx, posf, cs_t, sn_t, par, parB, tmpf, tmpi):
    S = cx.S
    t, fr, m = tmpf
    ti = tmpi
    n = 512
    for which, out, shift in (("sin", sn_t, 0.0), ("cos", cs_t, 0.25)):
        S.ts("dve", t[0][0:64, 0:n], posf[0][:], par[0:64, 32:33], 1.0 / TWO_PI, ALU.mult, ALU.mult, R=[posf[1], parB], W=[t[1]])
        if shift:
            S.ts("dve", t[0][0:64, 0:n], t[0][0:64, 0:n], shift, None, ALU.add, R=[t[1]], W=[t[1]])
        S.copy("dve", ti[0][0:64, 0:n], t[0][0:64, 0:n], R=[t[1]], W=[ti[1]])
        S.copy("dve", fr[0][0:64, 0:n], ti[0][0:64, 0:n], R=[ti[1]], W=[fr[1]])
        S.tt("dve", fr[0][0:64, 0:n], t[0][0:64, 0:n], fr[0][0:64, 0:n], ALU.subtract, R=[t[1], fr[1]], W=[fr[1]])
        S.ts("dve", m[0][0:64, 0:n], fr[0][0:64, 0:n], 0.5, None, ALU.is_gt, R=[fr[1]], W=[m[1]])
        S.tt("dve", fr[0][0:64, 0:n], fr[0][0:64, 0:n], m[0][0:64, 0:n], ALU.subtract, R=[fr[1], m[1]], W=[fr[1]])
        S.ts("dve", m[0][0:64, 0:n], fr[0][0:64, 0:n], -0.5, None, ALU.is_lt, R=[fr[1]], W=[m[1]])
        S.tt("dve", fr[0][0:64, 0:n], fr[0][0:64, 0:n], m[0][0:64, 0:n], ALU.add, R=[fr[1], m[1]], W=[fr[1]])
        S.act(out[0][:], fr[0][0:64, 0:n], AF.Sin, R=[fr[1]], W=[out[1]], scale=TWO_PI * (1.0 - 2e-6))
        if which == "sin":
            S.ts("dve", out[0][:], out[0][:], par[0:64, 33:34], None, ALU.mult, R=[out[1], parB], W=[out[1]])


def _even_ranges(cx, st, d, consts, dB, yB, par, parB, omla, omlaB):
    S = cx.S
    w_in = d["w_in"]
    bfm = "w_in_bf" in d

    def wload(dst, c0, n, wB_):
        if bfm:
            S.dma("sp", dst, wview(d["w_in_bf"], 0, 1024, c0, n), reads=[d["wbfB"]], writes=[wB_])
        else:
            S.dma("pool", dst, wview(w_in, 0, 1024, c0, n), reads=[dB], writes=[wB_])
    idf = consts["idf"]
    onef = consts["onef"]
    gt = cx.sb(st, "gt", [128, 1024], F32)
    bt = cx.sb(st, "bt", [128, 1024], F32)
    gbB = Buf("gb")
    S.dma("sp", gt[:], d["ln_g"].partition_broadcast(128), reads=[dB], writes=[gbB])
    S.dma("sp", bt[:], d["ln_b"].partition_broadcast(128), reads=[dB], writes=[gbB])
    cwT = cx.sb(st, "cwT", [128, 8, 31], F32)
    cwB = Buf("cwT")
    kmT = cx.sb(st, "kmT", [128, 4, 256], BF16)
    vm = cx.sb(st, "vm", [128, 2, 512], BF16)
    kmB = Buf("kmv")
    with contextlib.ExitStack() as s0:
        cwn = cx.sb(s0, "cwn", [31, 1024], F32)
        cwnB = Buf("cwn")
        S.dma("sp", cwn[:], d["conv_w"], reads=[dB], writes=[cwnB])
        for cc in range(8):
            bk, bB = cx.bank(PB)
            S.mm(bk[:, 0:32], cwn[0:31, cc * 128:(cc + 1) * 128], idf[0:31, 0:32], True, True, R=[cwnB, consts["B"]], W=[bB])
            S.copy("dve", cwT[:, cc, :], bk[:, 0:31], R=[bB], W=[cwB])
        memr = cx.sb(s0, "memr", [128, 2, 1024], BF16)
        memrB = Buf("memr")
        S.dma("pool", memr[:], d["mem"].rearrange("(t p) c -> p t c", p=128), reads=[dB], writes=[memrB])
        memT = cx.sb(s0, "memT", [128, 8, 256], BF16)
        memTB = Buf("memT")
        transpose_rows(cx, memr, memrB, 128, 2, memT, memTB, consts)
        wm = cx.sb(s0, "wm", [128, 8, 1024], BF16)
        wmB = Buf("wm")
        for j in range(2):
            S.dma("pool", wm[:, :, j * 512:(j + 1) * 512], wview(d["mem_kv"], 0, 1024, j * 512, 512), reads=[dB], writes=[wmB])
        for h in range(4):
            bk, bB = cx.bank(PA)
            for kc in range(8):
                S.mm(bk[:, 0:256], wm[:, kc, h * 128:(h + 1) * 128], memT[:, kc, :], kc == 0, kc == 7, R=[wmB, memTB], W=[bB])
            S.copy("act", kmT[:, h, :], bk[:, 0:256], R=[bB], W=[kmB])
        for mc in range(2):
            bk, bB = cx.bank(PA)
            for kc in range(8):
                S.mm(bk[:], memT[:, kc, mc * 128:(mc + 1) * 128], wm[:, kc, 512:1024], kc == 0, kc == 7, R=[wmB, memTB], W=[bB])
            S.copy("dve", vm[:, mc, :], bk[:], R=[bB], W=[kmB])
    S.barrier()
    stop_at(6)
    xr = cx.sb(st, "xr5", [128, 5, 1024], BF16)
    xrB = Buf("xr5")
    xT = cx.sb(st, "xTR", [128, 8, 544], BF16)
    xTB = Buf("xTR")
    glu = cx.sb(st, "glu", [128, 8, 544], BF16)
    gluB = [Buf("glu%d" % i) for i in range(8)]
    dw = cx.sb(st, "dw", [128, 8, 512], F32)
    dwB = [Buf("dw%d" % i) for i in range(8)]
    mix = cx.sb(st, "mix", [128, 20, 512], BF16)
    mixB = [Buf("mix%d" % i) for i in range(20)]
    wab_ring = Ring(cx, st, "wab", [128, 8, 256], BF16, 2)
    wg_ring = Ring(cx, st, "wg", [128, 8, 128], BF16, 3)
    tf = Ring(cx, st, "tf", [128, 512], F32, 6)
    pring = Ring(cx, st, "pTr", [128, 512], BF16, 4)
    dg_ring = Ring(cx, st, "dg", [128, 128], BF16, 6)
    mean_t = (cx.sb(st, "mean_t", [128, 512], F32), Buf("mean_t"))
    rs_t = (cx.sb(st, "rs_t", [128, 512], F32), Buf("rs_t"))
    qm_ring = Ring(cx, st, "qm", [128, 512], BF16, 2)
    wo_ring = Ring(cx, st, "woq", [128, 20, 256], BF16, 2)
    z = cx.sb(st, "z", [128, 4, 1024], F32)
    zB = [Buf("z%d" % i) for i in range(4)]
    stats = cx.sb(st, "stats", [128, 2, 6], F32)
    mv = cx.sb(st, "mv", [128, 2], F32)
    rstd = cx.sb(st, "rstd", [128, 1], F32)
    stB = Buf("stats")
    mscale = 128 ** -0.5

    for rb in range(4):
        r0 = rb * 512
        if rb == 1:
            stop_at(7)
        if rb == 0:
            S.dma("pool", xr[:, 0:4, :], d["x_halo"][r0:r0 + 512, :].rearrange("(t p) c -> p t c", p=128), reads=[dB], writes=[xrB])
            S.dma("pool", xr[0:30, 4, :], d["x_halo"][r0 + 512:r0 + 542, :], reads=[dB], writes=[xrB])
        transpose_rows(cx, xr, xrB, 30, 5, xT, xTB, consts)
        if rb + 1 < 4:
            r1 = r0 + 512
            S.dma("pool", xr[:, 0:4, :], d["x_halo"][r1:r1 + 512, :].rearrange("(t p) c -> p t c", p=128), reads=[dB], writes=[xrB])
            S.dma("pool", xr[0:30, 4, :], d["x_halo"][r1 + 512:r1 + 542, :], reads=[dB], writes=[xrB])
        S.dma("sp", z[:], d["x_halo"][r0 + 15:r0 + 527, :].rearrange("(t p) c -> p t c", p=128), reads=[dB], writes=zB)
        for cc in range(8):
            wab, wabB = wab_ring.next()
            wload(wab[:, :, 0:128], E_CONV_IN + cc * 128, 128, wabB)
            wload(wab[:, :, 128:256], E_CONV_IN + 1024 + cc * 128, 128, wabB)
            for (c0, n) in ((0, 512), (512, 30)):
                ba, bA = cx.bank(PA)
                for kc in range(8):
                    S.mm(ba[:, 0:n], wab[:, kc, 0:128], xT[:, kc, c0:c0 + n], kc == 0, kc == 7, R=[wabB, xTB], W=[bA])
                bb, bBb = cx.bank(PA)
                for kc in range(8):
                    S.mm(bb[:, 0:n], wab[:, kc, 128:256], xT[:, kc, c0:c0 + n], kc == 0, kc == 7, R=[wabB, xTB], W=[bBb])
                sg, sgB = tf.next()
                S.act(sg[:, 0:n], bb[:, 0:n], AF.Sigmoid, R=[bBb], W=[sgB])
                S.tt("dve", glu[:, cc, c0:c0 + n], ba[:, 0:n], sg[:, 0:n], ALU.mult, R=[bA, sgB], W=[gluB[cc]])
        bsum, bSum = cx.banks[6]
        bsq, bSq = cx.banks[7]
        for cc in range(8):
            bk, bB = cx.bank(PA)
            for k in range(31):
                dg, dgB = dg_ring.next()
                if k % 2 == 0:
                    S.act(dg[:], idf[:], AF.Identity, R=[consts["B"], cwB], W=[dgB], scale=cwT[:, cc, k:k + 1])
                else:
                    S.ts("dve", dg[:], idf[:], cwT[:, cc, k:k + 1], None, ALU.mult, R=[consts["B"], cwB], W=[dgB])
                S.mm(bk[:], dg[:], glu[:, cc, k:k + 512], k == 0, k == 30, R=[dgB, gluB[cc]], W=[bB])
            S.act(dw[:, cc, :], bk[:], AF.Identity, R=[bB, parB], W=[dwB[cc]], bias=par[:, cc:cc + 1])
            sq, sqB = tf.next()
            S.act(sq[:], bk[:], AF.Square, R=[bB, parB], W=[sqB], bias=par[:, cc:cc + 1])
            S.mm(bsum[:], onef[:], dw[:, cc, :], cc == 0, cc == 7, R=[consts["B"], dwB[cc]], W=[bSum])
            S.mm(bsq[:], onef[:], sq[:], cc == 0, cc == 7, R=[consts["B"], sqB], W=[bSq])
        S.act(mean_t[0][:], bsum[:], AF.Copy, R=[bSum], W=[mean_t[1]], scale=1.0 / 1024.0)
        m2, m2B = tf.next()
        S.tt("dve", m2[:], mean_t[0][:], mean_t[0][:], ALU.mult, R=[mean_t[1]], W=[m2B])
        S.stt(m2[:], bsq[:], 1.0 / 1024.0, m2[:], ALU.mult, ALU.subtract, R=[bSq, m2B], W=[m2B])
        S.act(rs_t[0][:], m2[:], AF.Sqrt, R=[m2B, consts["B"]], W=[rs_t[1]], bias=consts["eps"][LN_EPS])
        S.recip(rs_t[0][:], rs_t[0][:], R=[rs_t[1]], W=[rs_t[1]])
        for cc in range(8):
            t1, t1B = tf.next()
            S.tt("pool", t1[:], dw[:, cc, :], mean_t[0][:], ALU.subtract, R=[dwB[cc], mean_t[1]], W=[t1B])
            S.tt("dve", t1[:], t1[:], rs_t[0][:], ALU.mult, R=[t1B, rs_t[1]], W=[t1B])
            S.act(t1[:], t1[:], AF.Silu, R=[t1B, parB], W=[t1B], scale=par[:, 8 + cc:9 + cc], bias=par[:, 16 + cc:17 + cc])
            wg, wgB = wg_ring.next()
            wload(wg[:], E_CONV_GATE + cc * 128, 128, wgB)
            bk, bB = cx.bank(PA)
            for kc in range(8):
                S.mm(bk[:], wg[:, kc, :], xT[:, kc, 15:527], kc == 0, kc == 7, R=[wgB, xTB], W=[bB])
            sg, sgB = tf.next()
            S.act(sg[:], bk[:], AF.Silu, R=[bB], W=[sgB])
            S.tt("dve", mix[:, cc, :], t1[:], sg[:], ALU.mult, R=[t1B, sgB], W=[mixB[cc]])
        for h in range(4):
            wg, wgB = wg_ring.next()
            wload(wg[:], E_MEMQ + h * 128, 128, wgB)
            bk, bB = cx.bank(PA)
            for kc in range(8):
                S.mm(bk[:], wg[:, kc, :], xT[:, kc, 15:527], kc == 0, kc == 7, R=[wgB, xTB], W=[bB])
            qm, qmB = qm_ring.next()
            S.copy("act", qm[:], bk[:], R=[bB], W=[qmB])
            wg2, wg2B = wg_ring.next()
            wload(wg2[:], E_MEMG + h * 128, 128, wg2B)
            bk2, bB2 = cx.bank(PA)
            for kc in range(8):
                S.mm(bk2[:], wg2[:, kc, :], xT[:, kc, 15:527], kc == 0, kc == 7, R=[wg2B, xTB], W=[bB2])
            gm, gmB = tf.next()
            S.act(gm[:], bk2[:], AF.Silu, R=[bB2], W=[gmB])

            def kts(mc, h=h):
                return [(kmT[:, h, mc * 128:(mc + 1) * 128], [kmB])]

            def v_of(mc, h=h):
                return vm[:, mc, h * 128:(h + 1) * 128], [kmB]

            def fin(bo, bO, bs, bS, h=h, gm=gm, gmB=gmB):
                rec, recB = tf.next()
                S.recip(rec[:], bs[:], R=[bS], W=[recB])
                S.tt("dve", rec[:], bo[:], rec[:], ALU.mult, R=[bO, recB], W=[recB])
                S.tt("dve", mix[:, 16 + h, :], rec[:], gm[:], ALU.mult, R=[recB, gmB], W=[mixB[16 + h]])
            attention_block(cx, kts, [(qm[:], [qmB])], v_of, 2, mscale, pring, consts, fin)
        for h in range(8):
            wg, wgB = wg_ring.next()
            wload(wg[:], E_MLAG + h * 128, 128, wgB)
            bk, bB = cx.bank(PA)
            for kc in range(8):
                S.mm(bk[:], wg[:, kc, :], xT[:, kc, 15:527], kc == 0, kc == 7, R=[wgB, xTB], W=[bB])
            sg, sgB = tf.next()
            S.act(sg[:], bk[:], AF.Silu, R=[bB], W=[sgB])
            S.tt("dve", mix[:, 8 + h, :], omla[:, h, r0:r0 + 512], sg[:], ALU.mult, R=[omlaB[h], sgB], W=[mixB[8 + h]])
        for qq in range(4):
            woq, woqB = wo_ring.next()
            csl = slice(qq * 256, (qq + 1) * 256)
            if bfm:
                S.dma("sp", woq[:], d["w_out_bf"][:, csl].rearrange("(j p) n -> p j n", p=128), reads=[d["wbfB"]], writes=[woqB])
            else:
                S.dma("pool", woq[:], d["w_out"][:, csl].rearrange("(j p) n -> p j n", p=128), reads=[dB], writes=[woqB])
            for t in range(4):
                bk, bB = cx.bank(PA)
                for j in range(20):
                    S.mm(bk[:, 0:256], mix[:, j, t * 128:(t + 1) * 128], woq[:, j, :], j == 0, j == 19, R=[mixB[j], woqB], W=[bB])
                zs = z[:, t, csl]
                S.stt(zs, zs, float(ALPHA), bk[:, 0:256], ALU.mult, ALU.add, R=[bB, zB[t]], W=[zB[t]])
        for t in range(4):
            for i in range(2):
                S.op("dve", lambda e, t=t, i=i: e.bn_stats(stats[:, i, :], z[:, t, i * 512:(i + 1) * 512]), [zB[t]], [stB])
            S.op("dve", lambda e: e.bn_aggr(mv[:], stats[:].rearrange("p a b -> p (a b)")), [stB], [stB])
            S.act(rstd[:], mv[:, 1:2], AF.Sqrt, R=[stB, consts["B"]], W=[stB], bias=consts["eps"][LN_EPS])
            S.recip(rstd[:], rstd[:], R=[stB], W=[stB])
            S.ts("dve", z[:, t, :], z[:, t, :], mv[:, 0:1], rstd[:], ALU.subtract, ALU.mult, R=[zB[t], stB], W=[zB[t]])
            S.tt("dve", z[:, t, :], z[:, t, :], gt[:], ALU.mult, R=[zB[t], gbB], W=[zB[t]])
            S.tt("dve", z[:, t, :], z[:, t, :], bt[:], ALU.add, R=[zB[t], gbB], W=[zB[t]])
            S.dma("sp", d["y"][r0 + t * 128:r0 + (t + 1) * 128, :], z[:, t, :], reads=[zB[t]], writes=[Buf("ystore")])


O_U, O_GATE, O_MEMQ, O_MEMG = 0, 1024, 2048, 2560
NG = 64


def sincos(cx, ang, n, cos_out, sin_out, R, tmp, W):
    S = cx.S
    t, fr, m, ti = tmp
    for out, shift in ((sin_out, 0.0), (cos_out, 0.25)):
        S.ts("dve", t[0][:, 0:n], ang, 1.0 / TWO_PI, shift, ALU.mult, ALU.add, R=R, W=[t[1]])
        S.copy("dve", ti[0][:, 0:n], t[0][:, 0:n], R=[t[1]], W=[ti[1]])
        S.copy("dve", fr[0][:, 0:n], ti[0][:, 0:n], R=[ti[1]], W=[fr[1]])
        S.tt("dve", fr[0][:, 0:n], t[0][:, 0:n], fr[0][:, 0:n], ALU.subtract, R=[t[1], fr[1]], W=[fr[1]])
        S.ts("dve", m[0][:, 0:n], fr[0][:, 0:n], 0.5, None, ALU.is_gt, R=[fr[1]], W=[m[1]])
        S.tt("dve", fr[0][:, 0:n], fr[0][:, 0:n], m[0][:, 0:n], ALU.subtract, R=[fr[1], m[1]], W=[fr[1]])
        S.ts("dve", m[0][:, 0:n], fr[0][:, 0:n], -0.5, None, ALU.is_lt, R=[fr[1]], W=[m[1]])
        S.tt("dve", fr[0][:, 0:n], fr[0][:, 0:n], m[0][:, 0:n], ALU.add, R=[fr[1], m[1]], W=[fr[1]])
        S.act(out, fr[0][:, 0:n], AF.Sin, R=[fr[1]], W=W, scale=TWO_PI * (1.0 - 2e-6))


def s5_prefetch(cx, st, d, consts, dB, persist=False):
    S = cx.S
    idf = consts["idf"]
    P = {}

    def f32(name, shape):
        return cx.sb(st, name, shape, F32)
    are = f32("are", [128, NG]); aim = f32("aim", [128, NG]); ldt = f32("ldt", [128, NG])
    pB = Buf("s5par")
    bt_re = f32("bt_re", [128, NG, 16]); bt_im = f32("bt_im", [128, NG, 16])
    ct_re = f32("ct_re", [128, NG, 16]); ct_im = f32("ct_im", [128, NG, 16])
    bcB = Buf("btct")
    with contextlib.ExitStack() as s0_:
        s0 = st if persist else s0_
        nat = cx.sb(s0, "nat", [64, 128], F32)
        natB = Buf("nat")
        for name, dst in (("s_a_re", are), ("s_a_im", aim)):
            S.dma("sp", nat[:, 0:64], d[name + "_A"], reads=[dB], writes=[natB])
            S.dma("sp", nat[:, 64:128], d[name + "_B"], reads=[dB], writes=[natB])
            bk, bB = cx.bank(PB)
            S.mm(bk[:, 0:64], nat[:, :], idf[0:64, 0:64], True, True, R=[natB, consts["B"]], W=[bB])
            S.copy("dve", dst[:], bk[:, 0:64], R=[bB], W=[pB])
        S.dma("sp", ldt[0:64, :], d["s_log_dt_A"].partition_broadcast(64), reads=[dB], writes=[pB])
        S.dma("sp", ldt[64:128, :], d["s_log_dt_B"].partition_broadcast(64), reads=[dB], writes=[pB])
        for nm, dst in (("s_b_re", bt_re), ("s_b_im", bt_im)):
            S.dma("sp", dst[0:64], d[nm + "_A"].rearrange("g p c -> p g c"), reads=[dB], writes=[bcB])
            S.dma("sp", dst[64:128], d[nm + "_B"].rearrange("g p c -> p g c"), reads=[dB], writes=[bcB])
        cn_ring = Ring(cx, s0, "cn", [128, 128], F32, 2)
        for nm, dst in (("s_c_re", ct_re), ("s_c_im", ct_im)):
            for gb in range(8):
                cn, cnB = cn_ring.next()
                S.dma("sp", cn[:, 0:64], d[nm + "_A"][gb * 8:(gb + 1) * 8].rearrange("g c p -> (g c) p"), reads=[dB], writes=[cnB])
                S.dma("sp", cn[:, 64:128], d[nm + "_B"][gb * 8:(gb + 1) * 8].rearrange("g c p -> (g c) p"), reads=[dB], writes=[cnB])
                bk, bB = cx.bank(PB)
                S.mm(bk[:, 0:128], cn[:], idf[:], True, True, R=[cnB, consts["B"]], W=[bB])
                S.copy("dve", dst[:, gb * 8:(gb + 1) * 8, :], bk[:, 0:128].rearrange("p (g c) -> p g c", g=8), R=[bB], W=[bcB])
    if not persist:
        S.barrier()
    return dict(are=are, aim=aim, ldt=ldt, pB=pB, bt_re=bt_re, bt_im=bt_im, ct_re=ct_re, ct_im=ct_im, bcB=bcB)


def s5_tables(cx, d, consts, dB, a8, a8B, scr, pre=None):
    S = cx.S
    idf = consts["idf"]
    with contextlib.ExitStack() as st:
        def f32(name, shape):
            return cx.sb(st, name, shape, F32)
        if pre is None:
            pre = s5_prefetch(cx, st, d, consts, dB)
        are, aim, ldt, pB = pre["are"], pre["aim"], pre["ldt"], pre["pB"]
        bt_re, bt_im, ct_re, ct_im, bcB = pre["bt_re"], pre["bt_im"], pre["ct_re"], pre["ct_im"], pre["bcB"]
        se = f32("s5e", [128, 28])
        seB = Buf("s5e")
        S.dma("sp", se[:], d["s5e"], reads=[dB], writes=[seB])
        tmk = f32("tmask", [128, 2, 4, 128])
        tmB = Buf("tmask")
        for j in range(4):
            S.dma("sp", tmk[:, :, j, :], d["tmask"], reads=[dB], writes=[tmB])
        lre = f32("lre", [128, NG]); dtt = f32("dtt", [128, NG]); lrd = f32("lrd", [128, NG]); lid = f32("lid", [128, NG])
        S.ts("dve", lre[:], are[:], -1e-4, None, ALU.min, R=[pB], W=[pB])
        S.act(dtt[:], ldt[:], AF.Exp, R=[pB], W=[pB])
        S.tt("dve", lrd[:], lre[:], dtt[:], ALU.mult, R=[pB], W=[pB])
        S.tt("dve", lid[:], aim[:], dtt[:], ALU.mult, R=[pB], W=[pB])
        tmp = [(f32("sct%d" % i, [128, 512]), Buf("sct%d" % i)) for i in range(3)]
        tmp.append((cx.sb(st, "scti", [128, 512], I32), Buf("scti")))
        mag = f32("mag", [128, 512]); cs = f32("cs", [128, 512]); sn = f32("sn", [128, 512]); arg = f32("arg", [128, 512])
        wB = Buf("s5work")

        def cpow(E, n, out_re, out_im):
            m = NG * n
            if E is None:
                a_re, a_im = lrd[:], lid[:]
                rr = [pB]
            else:
                S.tt("dve", arg[:, 0:m].rearrange("p (g e) -> p g e", e=n), lrd[:].unsqueeze(2).broadcast_to([128, NG, n]),
                     E.unsqueeze(1).broadcast_to([128, NG, n]), ALU.mult, R=[pB, seB], W=[wB])
                a_re = arg[:, 0:m]
                rr = [wB]
            S.act(mag[:, 0:m], a_re, AF.Exp, R=rr, W=[wB])
            if E is not None:
                S.tt("dve", arg[:, 0:m].rearrange("p (g e) -> p g e", e=n), lid[:].unsqueeze(2).broadcast_to([128, NG, n]),
                     E.unsqueeze(1).broadcast_to([128, NG, n]), ALU.mult, R=[pB, seB, wB], W=[wB])
                a_im = arg[:, 0:m]
            sincos(cx, a_im, m, cs[:, 0:m], sn[:, 0:m], [wB, pB], tmp, [wB])
            S.tt("dve", out_re, mag[:, 0:m], cs[:, 0:m], ALU.mult, R=[wB], W=[wB])
            S.tt("dve", out_im, mag[:, 0:m], sn[:, 0:m], ALU.mult, R=[wB], W=[wB])
        lb_re = f32("lb_re", [128, NG]); lb_im = f32("lb_im", [128, NG])
        cpow(None, 1, lb_re[:], lb_im[:])
        den = f32("den", [128, NG]); nr = f32("nr", [128, NG]); f_re = f32("f_re", [128, NG]); f_im = f32("f_im", [128, NG])
        t0 = f32("t0", [128, NG])
        S.tt("dve", den[:], lre[:], lre[:], ALU.mult, R=[pB], W=[wB])
        S.tt("dve", t0[:], aim[:], aim[:], ALU.mult, R=[pB], W=[wB])
        S.tt("dve", den[:], den[:], t0[:], ALU.add, R=[wB], W=[wB])
        S.recip(den[:], den[:], R=[wB], W=[wB])
        S.ts("dve", nr[:], lb_re[:], -1.0, None, ALU.add, R=[wB], W=[wB])
        S.tt("dve", f_re[:], nr[:], lre[:], ALU.mult, R=[wB, pB], W=[wB])
        S.tt("dve", t0[:], lb_im[:], aim[:], ALU.mult, R=[wB, pB], W=[wB])
        S.tt("dve", f_re[:], f_re[:], t0[:], ALU.add, R=[wB], W=[wB])
        S.tt("dve", f_re[:], f_re[:], den[:], ALU.mult, R=[wB], W=[wB])
        S.tt("dve", f_im[:], lb_im[:], lre[:], ALU.mult, R=[wB, pB], W=[wB])
        S.tt("dve", t0[:], nr[:], aim[:], ALU.mult, R=[wB, pB], W=[wB])
        S.tt("dve", f_im[:], f_im[:], t0[:], ALU.subtract, R=[wB], W=[wB])
        S.tt("dve", f_im[:], f_im[:], den[:], ALU.mult, R=[wB], W=[wB])
        bb_re = f32("bb_re", [128, NG, 16]); bb_im = f32("bb_im", [128, NG, 16]); t1 = f32("t1k", [128, NG, 16])
        fre_b = f_re[:].unsqueeze(2).broadcast_to([128, NG, 16])
        fim_b = f_im[:].unsqueeze(2).broadcast_to([128, NG, 16])
        S.tt("dve", bb_re[:], bt_re[:], fre_b, ALU.mult, R=[wB, bcB], W=[wB])
        S.tt("dve", t1[:], bt_im[:], fim_b, ALU.mult, R=[wB, bcB], W=[wB])
        S.tt("dve", bb_re[:], bb_re[:], t1[:], ALU.subtract, R=[wB], W=[wB])
        S.tt("dve", bb_im[:], bt_im[:], fre_b, ALU.mult, R=[wB, bcB], W=[wB])
        S.tt("dve", t1[:], bt_re[:], fim_b, ALU.mult, R=[wB, bcB], W=[wB])
        S.tt("dve", bb_im[:], bb_im[:], t1[:], ALU.add, R=[wB], W=[wB])
        pin_re = f32("pin_re", [128, NG, 8]); pin_im = f32("pin_im", [128, NG, 8])
        pt_re = f32("pt_re", [128, NG, 8]); pt_im = f32("pt_im", [128, NG, 8])
        p3_re = f32("p3_re", [128, NG, 8]); p3_im = f32("p3_im", [128, NG, 8])
        cpow(se[:, 0:8], 8, pin_re[:].rearrange("p g e -> p (g e)"), pin_im[:].rearrange("p g e -> p (g e)"))
        cpow(se[:, 8:16], 8, pt_re[:].rearrange("p g e -> p (g e)"), pt_im[:].rearrange("p g e -> p (g e)"))
        cpow(se[:, 16:24], 8, p3_re[:].rearrange("p g e -> p (g e)"), p3_im[:].rearrange("p g e -> p (g e)"))
        S.ts("dve", arg[:, 0:NG], lrd[:], 8.0, None, ALU.mult, R=[pB], W=[wB])
        S.act(mag[:, 0:NG], arg[:, 0:NG], AF.Exp, R=[wB], W=[wB])
        S.ts("dve", arg[:, 0:NG], lid[:], 8.0, None, ALU.mult, R=[pB, wB], W=[wB])
        sincos(cx, arg[:, 0:NG], NG, cs[:, 0:NG], sn[:, 0:NG], [wB], tmp, [wB])
        S.tt("dve", a8[:, 0, :], mag[:, 0:NG], cs[:, 0:NG], ALU.mult, R=[wB], W=[a8B])
        S.tt("dve", a8[:, 1, :], mag[:, 0:NG], sn[:, 0:NG], ALU.mult, R=[wB], W=[a8B])
        GQ = 4
        NS = 2
        slots = []
        for sidx in range(NS):
            sl = {}
            sl["gin"] = [f32("gin%d" % i, [128, GQ, 8, 16]) for i in range(2)]
            sl["gt"] = [f32("gt%d" % i, [128, GQ, 8, 16]) for i in range(2)]
            sl["g3"] = [f32("g3%d" % i, [128, GQ, 8, 16]) for i in range(2)]
            sl["gtm"] = [[f32("gtm%d%d" % (i, k), [128, GQ, 8, 16]) for k in range(2)] for i in range(2)]
            sl["tqs"] = [f32("tq%d" % i, [128, GQ, 8, 16]) for i in range(3)]
            sl["w3t"] = cx.sb(st, "w3t", [128, 2, GQ, 2, 128], BF16)
            sl["w1t"] = cx.sb(st, "w1t", [128, GQ, 2, 128], BF16)
            sl["wtt"] = cx.sb(st, "wtt", [128, GQ, 128], BF16)
            sl["ginB"], sl["gtB"], sl["g3B"] = Buf("ginB"), Buf("gtB"), Buf("g3B")
            sl["gmB"], sl["w3B"], sl["w1B"], sl["wtB"] = Buf("gtmask"), Buf("w3tB"), Buf("w1tB"), Buf("wttB")
            slots.append(sl)
        tA = f32("tA", [128, 512]); tBt = f32("tBt", [128, 512])
        gB = Buf("gwork")
        scrB = scr["B"]

        def stage1(q):
            sl = slots[q % NS]
            gs = slice(q * GQ, (q + 1) * GQ)
            gin, gt, g3, gtm, tqs, w3t = sl["gin"], sl["gt"], sl["g3"], sl["gtm"], sl["tqs"], sl["w3t"]

            def cmul(eng, tq, oBuf, out_re, out_im, p_re, p_im, x_re, x_im, neg_im):
                pr = p_re[:, gs, :].unsqueeze(3).broadcast_to([128, GQ, 8, 16])
                pi = p_im[:, gs, :].unsqueeze(3).broadcast_to([128, GQ, 8, 16])
                xr = x_re[:, gs, :].unsqueeze(2).broadcast_to([128, GQ, 8, 16])
                xi = x_im[:, gs, :].unsqueeze(2).broadcast_to([128, GQ, 8, 16])
                S.tt(eng, out_re, pr, xr, ALU.mult, R=[wB], W=[oBuf])
                S.tt(eng, tq[:], pi, xi, ALU.mult, R=[wB], W=[oBuf])
                S.tt(eng, out_re, out_re, tq[:], ALU.subtract, R=[oBuf], W=[oBuf])
                S.tt(eng, out_im, pr, xi, ALU.mult, R=[wB], W=[oBuf])
                S.tt(eng, tq[:], pi, xr, ALU.mult, R=[wB, oBuf], W=[oBuf])
                if neg_im and eng == "dve":
                    S.stt(out_im, out_im, -1.0, tq[:], ALU.mult, ALU.subtract, R=[oBuf], W=[oBuf])
                else:
                    S.tt(eng, out_im, out_im, tq[:], ALU.add, R=[oBuf], W=[oBuf])
            cmul("pool", tqs[1], sl["gtB"], gt[0][:], gt[1][:], pt_re, pt_im, ct_re, ct_im, True)
            cmul("dve", tqs[0], sl["ginB"], gin[0][:], gin[1][:], pin_re, pin_im, bb_re, bb_im, False)
            cmul("dve", tqs[2], sl["g3B"], g3[0][:], g3[1][:], p3_re, p3_im, ct_re, ct_im, True)
            for i in range(2):
                for k in range(2):
                    sc = se[:, 24 + k:25 + k] if i == 0 else se[:, 26 + k:27 + k]
                    S.act(gtm[i][k][:].rearrange("p g e c -> p (g e c)"), gt[i][:].rearrange("p g e c -> p (g e c)"), AF.Identity,
                          R=[sl["gtB"], seB], W=[sl["gmB"]], scale=sc)
            for i in range(2):
                for k in range(2):
                    S.act(w3t[:, k, :, i, :], g3[i][:].rearrange("p g e c -> p g (e c)"), AF.Identity,
                          R=[sl["g3B"], seB], W=[sl["w3B"]], scale=se[:, 24 + k:25 + k])
            for k in range(2):
                S.dma("sp", scr["w3"][k, gs].rearrange("g p r m -> p g r m"), w3t[:, k], reads=[sl["w3B"]], writes=[Buf("scrw")])

        def stage2(q):
            sl = slots[q % NS]
            gs = slice(q * GQ, (q + 1) * GQ)
            gin, gtm, w1t, wtt = sl["gin"], sl["gtm"], sl["w1t"], sl["wtt"]
            for g4 in range(GQ // 4):
                for i in range(2):
                    bk, bB = cx.bank(PA)
                    for gg in range(4):
                        g = g4 * 4 + gg
                        S.mm(bk[:, gg * 128:(gg + 1) * 128], gin[i][:, g].rearrange("p e c -> p (e c)"), idf[:], True, True,
                             R=[sl["ginB"], consts["B"]], W=[bB])
                    S.copy("act", w1t[:, g4 * 4:(g4 + 1) * 4, i, :], bk[:].rearrange("p (g m) -> p g m", g=4), R=[bB], W=[sl["w1B"]])
                bks = []
                for k in range(2):
                    bk, bB = cx.bank(PA)
                    for gg in range(4):
                        g = g4 * 4 + gg
                        for i in range(2):
                            S.mm(bk[:, gg * 128:(gg + 1) * 128], gin[i][:, g].rearrange("p e c -> p (e c)"),
                                 gtm[i][k][:, g].rearrange("p e c -> p (e c)"), i == 0, i == 1, R=[sl["ginB"], sl["gmB"]], W=[bB])
                    bks.append((bk, bB))
                S.tt("dve", tA[:], bks[0][0][:], tmk[:, 0].rearrange("p j m -> p (j m)"), ALU.mult, R=[bks[0][1], tmB], W=[gB])
                S.tt("dve", tBt[:], bks[1][0][:], tmk[:, 1].rearrange("p j m -> p (j m)"), ALU.mult, R=[bks[1][1], tmB], W=[gB])
                S.tt("dve", wtt[:, g4 * 4:(g4 + 1) * 4, :], tA[:].rearrange("p (g m) -> p g m", g=4),
                     tBt[:].rearrange("p (g m) -> p g m", g=4), ALU.add, R=[gB], W=[sl["wtB"]])
            S.dma("sp", scr["w1"][gs].rearrange("g p r m -> p g r m"), w1t[:], reads=[sl["w1B"]], writes=[Buf("scrw")])
            S.dma("sp", scr["wt"][gs].rearrange("g p m -> p g m"), wtt[:], reads=[sl["wtB"]], writes=[Buf("scrw")])
        nq = NG // GQ
        stage1(0)
        for q in range(nq):
            if q + 1 < nq:
                stage1(q + 1)
            stage2(q)
    S.barrier()


def make_scr(nc):
    scr = {"B": Buf("scr")}
    scr["w1"] = nc.dram_tensor("scr_w1", [NG, 128, 2, 128], BF16, kind="Internal").ap()
    scr["w3"] = nc.dram_tensor("scr_w3", [2, NG, 128, 2, 128], BF16, kind="Internal").ap()
    scr["wt"] = nc.dram_tensor("scr_wt", [NG, 128, 128], BF16, kind="Internal").ap()
    return scr


def build_odd(nc, cx, d, consts):
    S = cx.S
    dB = Buf("dram_in_o")
    yB = Buf("y_o")
    w_in = d["o_w_in"]
    idb = consts["idb"]
    scr = d.get("scr") or make_scr(nc)
    scrB = scr["B"]
    with contextlib.ExitStack() as stA:
        if "a8" in d:
            a8, a8B = d["a8"]
        else:
            a8 = cx.sb(stA, "a8", [128, 2, NG], F32)
            a8B = Buf("a8")
            s5_tables(cx, d, consts, dB, a8, a8B, scr, pre=d.get("s5pre"))
        aa = cx.sb(stA, "aa", [128, 2, NG], F32)
        ab = cx.sb(stA, "ab", [128, 2, NG], F32)
        S.copy("dve", aa[:, 0, :], a8[:, 0, :], R=[a8B], W=[a8B])
        S.copy("dve", aa[:, 1, :], a8[:, 0, :], R=[a8B], W=[a8B])
        S.ts("dve", ab[:, 0, :], a8[:, 1, :], -1.0, None, ALU.mult, R=[a8B], W=[a8B])
        S.copy("dve", ab[:, 1, :], a8[:, 1, :], R=[a8B], W=[a8B])
        carry = cx.sb(stA, "carry", [128, 2, NG], F32)
        carB = [Buf("carA"), Buf("carB")]
        S.op("dve", lambda e: e.memset(carry[:], 0.0), writes=carB)
        dtile = cx.sb(stA, "dtile", [128, 1024], F32)
        gt = cx.sb(stA, "gt_o", [128, 1024], F32)
        bt = cx.sb(stA, "bt_o", [128, 1024], F32)
        gbB = Buf("gb_o")
        S.dma("sp", dtile[:], d["o_d"].partition_broadcast(128), reads=[dB], writes=[gbB])
        S.dma("sp", gt[:], d["o_ln_g"].partition_broadcast(128), reads=[dB], writes=[gbB])
        S.dma("sp", bt[:], d["o_ln_b"].partition_broadcast(128), reads=[dB], writes=[gbB])
        kmT = cx.sb(stA, "kmT_o", [128, 4, 256], BF16)
        vm = cx.sb(stA, "vm_o", [128, 2, 512], BF16)
        kmB = Buf("kmv_o")
        mem_kv_setup(cx, d["mem"], d["o_mem_kv"], consts, dB, kmT, vm, kmB)
        S.barrier()
        xT = cx.sb(stA, "xT_o", [128, 8, 1024], BF16)
        xTB = Buf("xT_o")
        yT = cx.sb(stA, "yT_o", [128, 8, 1024], BF16)
        yTB = Buf("yT_o")
        fusedm = "h1" in d
        if fusedm:
            units = [(0, True, False, False), (1, True, True, True), (0, True, True, True)]
            xch_s = cx.sb(stA, "xch_s", [128, 2 * NG], F32)
            xch_o = cx.sb(stA, "xch_o", [128, 2 * NG], F32)
            xchB = Buf("xch")
            cbuf = nc.dram_tensor("cbuf_bounce", [64, 2 * NG], F32).ap()
            csum = nc.dram_tensor("csum_bounce", [64, 2 * NG], F32).ap()
            cbB, csB = Buf("cbuf"), Buf("csum")

            def xchg():
                S.dma("sp", cbuf, carry[0:64].rearrange("p r g -> p (r g)"), reads=[carB[0]], writes=[cbB])
                S.collective("AllReduce", ALU.add, PAIRS, cbuf, csum, reads=[cbB], writes=[csB])
                S.dma("sp", xch_s[64:128, :], csum, reads=[csB], writes=[xchB])
                S.dma("sp", xch_o[64:128, :], cbuf, reads=[cbB], writes=[xchB])
                S.tt("dve", carry[64:128].rearrange("p r g -> p (r g)"), xch_s[64:128, :], xch_o[64:128, :], ALU.subtract,
                     R=[xchB], W=[carB[1]])
        else:
            units = [(3, False, True, False), (2, False, True, False), (0, True, False, False),
                     (1, True, True, True), (0, True, True, True)]
        for ui, (rng, doA, doB, full) in enumerate(units):
            if rng == 0 and full:
                S.op("dve", lambda e: e.memset(carry[0:64], 0.0), writes=[carB[0]])
            with contextlib.ExitStack() as st:
                s5_unit(cx, st, d, consts, dB, scr, rng, doA, doB, full, xT, xTB, yT, yTB, aa, ab, a8B, carry, carB, dtile, gbB,
                        xchg=(xchg if (fusedm and rng == 1) else None))
            S.barrier()
            if full:
                with contextlib.ExitStack() as st:
                    odd_post(cx, st, d, consts, dB, yB, rng, xT, xTB, yT, yTB, kmT, vm, kmB, gt, bt, gbB)
                S.barrier()
    S.wait_all_on("sp", [yB])


def mem_kv_setup(cx, mem, wkv, consts, dB, kmT, vm, kmB):
    S = cx.S
    with contextlib.ExitStack() as s0:
        memr = cx.sb(s0, "memr", [128, 2, 1024], BF16)
        memrB = Buf("memr")
        S.dma("pool", memr[:], mem.rearrange("(t p) c -> p t c", p=128), reads=[dB], writes=[memrB])
        memT = cx.sb(s0, "memT", [128, 8, 256], BF16)
        memTB = Buf("memT")
        transpose_rows(cx, memr, memrB, 128, 2, memT, memTB, consts)
        wm = cx.sb(s0, "wm", [128, 8, 1024], BF16)
        wmB = Buf("wm")
        for j in range(2):
            S.dma("pool", wm[:, :, j * 512:(j + 1) * 512], wview(wkv, 0, 1024, j * 512, 512), reads=[dB], writes=[wmB])
        for h in range(4):
            bk, bB = cx.bank(PA)
            for kc in range(8):
                S.mm(bk[:, 0:256], wm[:, kc, h * 128:(h + 1) * 128], memT[:, kc, :], kc == 0, kc == 7, R=[wmB, memTB], W=[bB])
            S.copy("act", kmT[:, h, :], bk[:, 0:256], R=[bB], W=[kmB])
        for mc in range(2):
            bk, bB = cx.bank(PA)
            for kc in range(8):
                S.mm(bk[:], memT[:, kc, mc * 128:(mc + 1) * 128], wm[:, kc, 512:1024], kc == 0, kc == 7, R=[wmB, memTB], W=[bB])
            S.copy("dve", vm[:, mc, :], bk[:], R=[bB], W=[kmB])


def s5_unit(cx, st, d, consts, dB, scr, rng, doA, doB, full, xT, xTB, yT, yTB, aa, ab, a8B, carry, carB, dtile, gbB, xchg=None):
    S = cx.S
    idb = consts["idb"]
    scrB = scr["B"]
    w_in = d["o_w_in"]
    xry = cx.sb(st, "xry", [128, 8, 1024], BF16)
    xryB = Buf("xry")
    wu = cx.sb(st, "wu", [128, 8, 1024], BF16)
    wuB = Buf("wu")
    utm = cx.sb(st, "utm", [128, NG, 8, 16], BF16)
    utmB = Buf("utm")
    xb = cx.sb(st, "xb", [128, 2, NG, 129], BF16)
    xbB = [Buf("xbA"), Buf("xbB")]
    xsB = [Buf("xsA"), Buf("xsB")]
    u8_ring = Ring(cx, st, "u8", [128, 4, 128], BF16, 3)
    w1_ring = Ring(cx, st, "w1r", [128, 8, 2, 128], BF16, 2)
    r0 = rng * 1024
    comb = doA and doB and xchg is None
    fused = "h1" in d
    rev = fused and rng >= 2
    if not fused:
        for hh in range(2):
            S.dma("pool", xry[:, hh * 4:(hh + 1) * 4, :], d["h_seq"][r0 + hh * 512:r0 + (hh + 1) * 512, :].rearrange("(t p) c -> p t c", p=128),
                  reads=[dB], writes=[xryB])
    elif not rev:
        for hh in range(2):
            S.dma("pool", xry[:, hh * 4:(hh + 1) * 4, :], d["h1"][r0 + hh * 512:r0 + (hh + 1) * 512, :].rearrange("(t p) c -> p t c", p=128),
                  reads=[d["h1B"]], writes=[xryB])
    else:
        p0 = 1024 if rng == 2 else 0
        sa_ring = Ring(cx, st, "sa", [128, 2, 1024], F32, 2)
        sb_ring = Ring(cx, st, "sbb", [128, 2, 1024], F32, 2)
        for q in range(4):
            sa, saB = sa_ring.next()
            sb_, sbB = sb_ring.next()
            rows = slice(p0 + q * 256, p0 + (q + 1) * 256)
            S.dma("sp", sa[:], d["hsum"][rows, :].rearrange("(t p) c -> p t c", p=128), reads=[d["hsumB"]], writes=[saB])
            S.dma("sp", sb_[:], d["h1"][rows, :].rearrange("(t p) c -> p t c", p=128), reads=[d["h1B"]], writes=[sbB])
            S.tt("pool", xry[:, q * 2:(q + 1) * 2, :], sa[:], sb_[:], ALU.subtract, R=[saB, sbB], W=[xryB])
    for j in range(2):
        S.dma("pool", wu[:, :, j * 512:(j + 1) * 512], wview(w_in, 0, 1024, O_U + j * 512, 512), reads=[dB], writes=[wuB])
    transpose_rows(cx, xry, xryB, 128, 8, xT, xTB, consts, rev=rev)
    k = 0
    for s in range(8):
        for hf in range(2):
            bk, bB = cx.bank(PA)
            for kc in range(8):
                S.mm(bk[:], xT[:, kc, s:1024:8], wu[:, kc, hf * 512:(hf + 1) * 512], kc == 0, kc == 7, R=[xTB, wuB], W=[bB])
            S.copy("act" if k % 2 == 0 else "dve", utm[:, hf * 32:(hf + 1) * 32, s, :],
                   bk[:].rearrange("p (g c) -> p g c", c=16), R=[bB], W=[utmB])
            k += 1
    lanesA = slice(0, 64)
    lanesB = slice(64, 128)
    for gb in range(8):
        w1, w1B = w1_ring.next()
        S.dma("sp", w1[:], scr["w1"][gb * 8:(gb + 1) * 8].rearrange("g p r m -> p g r m"), reads=[scrB], writes=[w1B])
        for g4 in range(2):
            g0 = gb * 8 + g4 * 4
            u8, u8B = u8_ring.next()
            bk, bB = cx.bank(PB)
            for gg in range(4):
                S.mm(bk[:, gg * 128:(gg + 1) * 128], utm[:, g0 + gg].rearrange("p s c -> p (s c)"), idb[:], True, True,
                     R=[utmB, consts["B"]], W=[bB])
            S.copy("act", u8[:], bk[:].rearrange("p (g m) -> p g m", g=4), R=[bB], W=[u8B])
            for ri in range(2):
                bk2, bB2 = cx.bank(PA)
                for gg in range(4):
                    S.mm(bk2[:, gg * 128:(gg + 1) * 128], w1[:, g4 * 4 + gg, ri, :], u8[:, gg, :], True, True, R=[w1B, u8B], W=[bB2])
                src = bk2[:].rearrange("p (g m) -> p g m", g=4)
                if doA:
                    S.copy("act", xb[lanesA, ri, g0:g0 + 4, 1:129], src[lanesA], R=[bB2], W=[xbB[0]])
                if doB and not comb:
                    S.copy("dve", xb[lanesB, ri, g0:g0 + 4, 0:128], src[lanesB], R=[bB2], W=[xbB[1]])
                if doB and comb:
                    S.copy("dve", xb[lanesB, ri, g0:g0 + 4, 128:0:-1], src[lanesB], R=[bB2], W=[xbB[1]])
    st_ring = [Ring(cx, st, "stA", [128, 2, NG], F32, 3), Ring(cx, st, "stB", [128, 2, NG], F32, 3)]
    ta = [cx.sb(st, "ta%d" % i, [128, 2, NG], F32) for i in range(2)]
    tb = [cx.sb(st, "tb%d" % i, [128, 2, NG], F32) for i in range(2)]
    tmB = [Buf("scanA"), Buf("scanB")]
    tbB = [Buf("scanAb"), Buf("scanBb")]
    bk6, b6B = cx.banks[6]
    bk7, b7B = cx.banks[7]

    def v3(ap):
        return ap.rearrange("p (r g) -> p r g", r=2)
    ps = {"aa": v3(bk6[:, 0:128]), "ab": v3(bk6[:, 128:256]), "b6": b6B, "b7": b7B,
          "tb": [v3(bk7[:, 0:128]), v3(bk7[:, 128:256])], "ad": [v3(bk7[:, 256:384]), v3(bk7[:, 384:512])]}
    if comb:
        allp = slice(0, 128)
        S.copy("dve", xb[:, :, :, 0], carry[:], R=[carB[0], carB[1]], W=[xsB[0], xsB[1]])
        pv, pvB = carry, None
        for step in range(128):
            col = step + 1
            rdeps = [a8B] + ([carB[0], carB[1]] if pvB is None else [pvB])
            S.tt("dve", ta[0][allp], aa[allp], pv[allp], ALU.mult, R=rdeps, W=[tmB[0]])
            S.tt("dve", tb[0][allp], ab[allp], pv[allp, ::-1, :], ALU.mult, R=rdeps, W=[tbB[0]])
            S.tt("dve", ta[0][allp], ta[0][allp], tb[0][allp], ALU.add, R=[tmB[0], tbB[0]], W=[tmB[0]])
            stt_, sttB = st_ring[0].next()
            S.tt("dve", stt_[allp], ta[0][allp], xb[allp, :, :, col], ALU.add, R=[tmB[0], xbB[0], xbB[1]], W=[sttB])
            S.copy("act", xb[allp, :, :, col], stt_[allp], R=[sttB], W=[xsB[0], xsB[1]])
            pv, pvB = stt_, sttB
        S.copy("dve", carry[:], pv[:], R=[pvB], W=[carB[0], carB[1]])
    elif xchg is None:
        _scan_single(cx, doA, doB, lanesA, lanesB, xb, xbB, xsB, carry, carB, aa, ab, a8B, ta, tb, tmB, tbB, st_ring, ps)
    else:
        _scan_single(cx, True, False, lanesA, lanesB, xb, xbB, xsB, carry, carB, aa, ab, a8B, ta, tb, tmB, tbB, st_ring, ps)
        xchg()
        _scan_single(cx, False, True, lanesA, lanesB, xb, xbB, xsB, carry, carB, aa, ab, a8B, ta, tb, tmB, tbB, st_ring, ps)
    if not full:
        return
    w3_ring = Ring(cx, st, "w3r", [128, 2, 8, 2, 128], BF16, 2)
    xf_ring = Ring(cx, st, "xf", [128, 2, 8, 129], BF16, 2)
    for xf_, xfB_ in xf_ring.tiles:
        S.op("dve", lambda e, t=xf_: e.memset(t[:], 0.0), writes=[xfB_])
    wt_ring = Ring(cx, st, "wtr", [128, 8, 128], BF16, 2)
    tf = Ring(cx, st, "tf5", [128, 512], F32, 3)
    yg = xry[:].rearrange("p r (g c) -> p g r c", c=16)
    for gb in range(8):
        w3, w3B = w3_ring.next()
        for k in range(2):
            S.dma("sp", w3[:, k], scr["w3"][k, gb * 8:(gb + 1) * 8].rearrange("g p r m -> p g r m"), reads=[scrB], writes=[w3B])
        wt, wtB = wt_ring.next()
        S.dma("sp", wt[:], scr["wt"][gb * 8:(gb + 1) * 8].rearrange("g p m -> p g m"), reads=[scrB], writes=[wtB])
        xf, xfB = xf_ring.next()
        if comb:
            S.copy("dve", xf[lanesB], xb[lanesB, :, gb * 8:(gb + 1) * 8, 128::-1], R=[xsB[1]], W=[xfB])
        else:
            S.copy("dve", xf[lanesB], xb[lanesB, :, gb * 8:(gb + 1) * 8, 0:129], R=[xsB[1]], W=[xfB])
        for g4 in range(2):
            g0 = gb * 8 + g4 * 4
            u8, u8B = u8_ring.next()
            bk, bB = cx.bank(PB)
            for gg in range(4):
                S.mm(bk[:, gg * 128:(gg + 1) * 128], utm[:, g0 + gg].rearrange("p s c -> p (s c)"), idb[:], True, True,
                     R=[utmB, consts["B"]], W=[bB])
            S.copy("act", u8[:], bk[:].rearrange("p (g m) -> p g m", g=4), R=[bB], W=[u8B])
            bo, bO = cx.bank(PA)
            for gg in range(4):
                g = g0 + gg
                gl = g4 * 4 + gg
                reg = bo[:, gg * 128:(gg + 1) * 128]
                S.mm(reg, xb[:, 0, g, 0:128], w3[:, 0, gl, 0, :], True, False, R=[xsB[0], xsB[1], w3B], W=[bO])
                S.mm(reg, xb[:, 1, g, 0:128], w3[:, 0, gl, 1, :], False, False, R=[xsB[0], xsB[1], w3B], W=[bO])
                S.mm(reg, xf[:, 0, gl, 1:129], w3[:, 1, gl, 0, :], False, False, R=[xfB, w3B], W=[bO])
                S.mm(reg, xf[:, 1, gl, 1:129], w3[:, 1, gl, 1, :], False, False, R=[xfB, w3B], W=[bO])
                S.mm(reg, u8[:, gg, :], wt[:, gl, :], False, True, R=[u8B, wtB], W=[bO])
            t, tB = tf.next()
            t4 = t[:].rearrange("p (g r c) -> p g r c", g=4, r=8)
            S.tt("pool", t4, utm[:, g0:g0 + 4], dtile[:, g0 * 16:(g0 + 4) * 16].rearrange("p (g c) -> p g c", c=16).unsqueeze(2).broadcast_to([128, 4, 8, 16]),
                 ALU.mult, R=[utmB, gbB], W=[tB])
            S.tt("dve", t[:], bo[:], t[:], ALU.add, R=[bO, tB], W=[tB])
            S.act(yg[:, g0:g0 + 4], t4, AF.Gelu_apprx_tanh, R=[tB], W=[xryB])
    k = 0
    for kc in range(8):
        for r4 in range(2):
            bk, bB = cx.bank(PB)
            for rr in range(4):
                r = r4 * 4 + rr
                S.mm(bk[:, rr * 128:(rr + 1) * 128], xry[:, r, kc * 128:(kc + 1) * 128], idb[:], True, True, R=[xryB, consts["B"]], W=[bB])
            dst = yT[:, kc, :].rearrange("p (j r) -> p r j", r=8)[:, r4 * 4:(r4 + 1) * 4, :]
            S.copy("act" if k % 2 == 0 else "dve", dst, bk[:].rearrange("p (r j) -> p r j", r=4), R=[bB], W=[yTB])
            k += 1


def odd_post(cx, st, d, consts, dB, yB, rng, xT, xTB, yT, yTB, kmT, vm, kmB, gt, bt, gbB):
    S = cx.S
    w_in = d["o_w_in"]
    mix = cx.sb(st, "mix_o", [128, 12, 1024], BF16)
    mixB = [[Buf("mixo%d_%d" % (i, b)) for b in range(2)] for i in range(12)]
    wgl_ring = Ring(cx, st, "wgl", [128, 8, 256], BF16, 2)
    wg_ring = Ring(cx, st, "wg_o", [128, 8, 128], BF16, 3)
    tf = Ring(cx, st, "tf_o", [128, 512], F32, 6)
    pring = Ring(cx, st, "pT_o", [128, 512], BF16, 4)
    qm_ring = Ring(cx, st, "qm_o", [128, 512], BF16, 2)
    wo = cx.sb(st, "wo_o", [128, 12, 512], BF16)
    woB = Buf("wo_o")
    z = cx.sb(st, "z_o", [128, 4, 1024], F32)
    zB = [Buf("zo%d" % i) for i in range(4)]
    stats = cx.sb(st, "stats_o", [128, 2, 6], F32)
    mv = cx.sb(st, "mv_o", [128, 2], F32)
    rstd = cx.sb(st, "rstd_o", [128, 1], F32)
    stB = Buf("stats_o")
    mscale = 128 ** -0.5
    r0 = rng * 1024
    for f in range(8):
        wgl, wglB = wgl_ring.next()
        S.dma("pool", wgl[:, :, 0:128], wview(d["o_w_glu"], 0, 1024, f * 128, 128), reads=[dB], writes=[wglB])
        S.dma("pool", wgl[:, :, 128:256], wview(d["o_w_glu"], 0, 1024, 1024 + f * 128, 128), reads=[dB], writes=[wglB])
        wg, wgB = wg_ring.next()
        S.dma("pool", wg[:], wview(w_in, 0, 1024, O_GATE + f * 128, 128), reads=[dB], writes=[wgB])
        for b in range(2):
            cs = slice(b * 512, (b + 1) * 512)
            ba, bA = cx.bank(PA)
            for kc in range(8):
                S.mm(ba[:], wgl[:, kc, 0:128], yT[:, kc, cs], kc == 0, kc == 7, R=[wglB, yTB], W=[bA])
            bb, bBb = cx.bank(PA)
            for kc in range(8):
                S.mm(bb[:], wgl[:, kc, 128:256], yT[:, kc, cs], kc == 0, kc == 7, R=[wglB, yTB], W=[bBb])
            bg, bG = cx.bank(PA)
            for kc in range(8):
                S.mm(bg[:], wg[:, kc, :], xT[:, kc, cs], kc == 0, kc == 7, R=[wgB, xTB], W=[bG])
            sg, sgB = tf.next()
            S.act(sg[:], bb[:], AF.Sigmoid, R=[bBb], W=[sgB])
            s2, s2B = tf.next()
            S.act(s2[:], bg[:], AF.Silu, R=[bG], W=[s2B])
            S.tt("dve", sg[:], ba[:], sg[:], ALU.mult, R=[bA, sgB], W=[sgB])
            S.tt("dve", mix[:, f, cs], sg[:], s2[:], ALU.mult, R=[sgB, s2B], W=[mixB[f][b]])
    for b in range(2):
        cs = slice(b * 512, (b + 1) * 512)
        for h in range(4):
            wg, wgB = wg_ring.next()
            S.dma("pool", wg[:], wview(w_in, 0, 1024, O_MEMQ + h * 128, 128), reads=[dB], writes=[wgB])
            bk, bB = cx.bank(PA)
            for kc in range(8):
                S.mm(bk[:], wg[:, kc, :], xT[:, kc, cs], kc == 0, kc == 7, R=[wgB, xTB], W=[bB])
            qm, qmB = qm_ring.next()
            S.copy("act", qm[:], bk[:], R=[bB], W=[qmB])
            wg2, wg2B = wg_ring.next()
            S.dma("pool", wg2[:], wview(w_in, 0, 1024, O_MEMG + h * 128, 128), reads=[dB], writes=[wg2B])
            bk2, bB2 = cx.bank(PA)
            for kc in range(8):
                S.mm(bk2[:], wg2[:, kc, :], xT[:, kc, cs], kc == 0, kc == 7, R=[wg2B, xTB], W=[bB2])
            gm, gmB = tf.next()
            S.act(gm[:], bk2[:], AF.Silu, R=[bB2], W=[gmB])

            def kts(mc, h=h):
                return [(kmT[:, h, mc * 128:(mc + 1) * 128], [kmB])]

            def v_of(mc, h=h):
                return vm[:, mc, h * 128:(h + 1) * 128], [kmB]

            def fin(bo, bO, bs, bS, h=h, gm=gm, gmB=gmB, cs=cs, b=b):
                rec, recB = tf.next()
                S.recip(rec[:], bs[:], R=[bS], W=[recB])
                S.tt("dve", rec[:], bo[:], rec[:], ALU.mult, R=[bO, recB], W=[recB])
                S.tt("dve", mix[:, 8 + h, cs], rec[:], gm[:], ALU.mult, R=[recB, gmB], W=[mixB[8 + h][b]])
            attention_block(cx, kts, [(qm[:], [qmB])], v_of, 2, mscale, pring, consts, fin)
        if "h1" in d:
            S.dma("sp", z[:], d["h1"][r0 + b * 512:r0 + (b + 1) * 512, :].rearrange("(t p) c -> p t c", p=128), reads=[d["h1B"]], writes=zB)
        else:
            S.dma("sp", z[:], d["h_seq"][r0 + b * 512:r0 + (b + 1) * 512, :].rearrange("(t p) c -> p t c", p=128), reads=[dB], writes=zB)
        for half in range(2):
            S.dma("pool", wo[:], d["o_w_out"][:, half * 512:(half + 1) * 512].rearrange("(j p) n -> p j n", p=128),
                  reads=[dB], writes=[woB])
            for t in range(4):
                bk, bB = cx.bank(PA)
                for j in range(12):
                    S.mm(bk[:], mix[:, j, b * 512 + t * 128:b * 512 + (t + 1) * 128], wo[:, j, :], j == 0, j == 11,
                         R=[mixB[j][b], woB], W=[bB])
                zs = z[:, t, half * 512:(half + 1) * 512]
                S.stt(zs, zs, float(ALPHA), bk[:], ALU.mult, ALU.add, R=[bB, zB[t]], W=[zB[t]])
        ln_store(cx, consts, z, zB, stats, mv, rstd, stB, gt, bt, gbB, d["y"], r0 + b * 512, yB)


def ln_store(cx, consts, z, zB, stats, mv, rstd, stB, gt, bt, gbB, y, row0, yB):
    S = cx.S
    for t in range(4):
        for i in range(2):
            S.op("dve", lambda e, t=t, i=i: e.bn_stats(stats[:, i, :], z[:, t, i * 512:(i + 1) * 512]), [zB[t]], [stB])
        S.op("dve", lambda e: e.bn_aggr(mv[:], stats[:].rearrange("p a b -> p (a b)")), [stB], [stB])
        S.act(rstd[:], mv[:, 1:2], AF.Sqrt, R=[stB, consts["B"]], W=[stB], bias=consts["eps"][LN_EPS])
        S.recip(rstd[:], rstd[:], R=[stB], W=[stB])
        S.ts("dve", z[:, t, :], z[:, t, :], mv[:, 0:1], rstd[:], ALU.subtract, ALU.mult, R=[zB[t], stB], W=[zB[t]])
        S.tt("dve", z[:, t, :], z[:, t, :], gt[:], ALU.mult, R=[zB[t], gbB], W=[zB[t]])
        S.tt("dve", z[:, t, :], z[:, t, :], bt[:], ALU.add, R=[zB[t], gbB], W=[zB[t]])
        S.dma("sp", y[row0 + t * 128:row0 + (t + 1) * 128, :], z[:, t, :], reads=[zB[t]], writes=[Buf("ystore")])


def _scan_single(cx, doA, doB, lanesA, lanesB, xb, xbB, xsB, carry, carB, aa, ab, a8B, ta, tb, tmB, tbB, st_ring, ps):
    S = cx.S
    act_dirs = []
    if doA:
        act_dirs.append((0, lanesA, list(range(1, 129)), 0))
    if doB:
        act_dirs.append((1, lanesB, list(range(127, -1, -1)), 128))
    prev = {}
    for (di, lanes, cols, cin) in act_dirs:
        S.copy("dve", xb[lanes, :, :, cin], carry[lanes], R=[carB[di]], W=[xsB[di]])
        prev[di] = (carry, carB[di])
    for step in range(128):
        cur = {}
        for (di, lanes, cols, cin) in act_dirs:
            S.tt("dve", ta[di][lanes], aa[lanes], prev[di][0][lanes], ALU.mult, R=[a8B, prev[di][1]], W=[tmB[di]])
        for (di, lanes, cols, cin) in act_dirs:
            S.tt("dve", tb[di][lanes], ab[lanes], prev[di][0][lanes, ::-1, :], ALU.mult, R=[a8B, prev[di][1]], W=[tbB[di]])
        for (di, lanes, cols, cin) in act_dirs:
            S.tt("dve", ta[di][lanes], ta[di][lanes], tb[di][lanes], ALU.add, R=[tmB[di], tbB[di]], W=[tmB[di]])
        for (di, lanes, cols, cin) in act_dirs:
            stt_, sttB = st_ring[di].next()
            col = cols[step]
            S.tt("dve", stt_[lanes], ta[di][lanes], xb[lanes, :, :, col], ALU.add, R=[tmB[di], xbB[di]], W=[sttB])
            S.copy("act", xb[lanes, :, :, col], stt_[lanes], R=[sttB], W=[xsB[di]])
            cur[di] = (stt_, sttB)
        prev = cur
    for (di, lanes, cols, cin) in act_dirs:
        S.copy("dve", carry[lanes], prev[di][0][lanes], R=[prev[di][1]], W=[carB[di]])


EVEN_W = [("w_in", [1024, 6208]), ("conv_w", [31, 1024]), ("conv_b", [1024]), ("conv_ln_g", [1024]),
          ("conv_ln_b", [1024]), ("q_norm", [768]), ("w_uq", [768, 1536]), ("kv_norm", [256]),
          ("w_ukv", [256, 2048]), ("mem_kv", [1024, 1024]), ("w_out", [2560, 1024]), ("ln_g", [1024]), ("ln_b", [1024])]


def build_even_nc():
    nc = bass.Bass("TRN2", target_bir_lowering=False)
    d = {}
    d["x_kv"] = nc.dram_tensor("x_kv", [SEQ, 1024], F32, kind="ExternalInput").ap()
    d["x_halo"] = nc.dram_tensor("x_halo", [NT + 30, 1024], F32, kind="ExternalInput").ap()
    d["pos_kv"] = nc.dram_tensor("pos_kv", [SEQ], I32, kind="ExternalInput").ap()
    d["mem"] = nc.dram_tensor("mem", [256, 1024], F32, kind="ExternalInput").ap()
    d["ident"] = nc.dram_tensor("ident", [128, 128], F32, kind="ExternalInput").ap()
    d["ropec"] = nc.dram_tensor("ropec", [64, 2], F32, kind="ExternalInput").ap()
    for n, shp in EVEN_W:
        d[n] = nc.dram_tensor("e_" + n, shp, F32, kind="ExternalInput").ap()
    d["y"] = nc.dram_tensor("y", [NT, 1024], F32, kind="ExternalOutput").ap()
    st = contextlib.ExitStack()
    cx = Ctx(nc, st)
    consts = load_consts(cx, st, d)
    build_even(nc, cx, d, consts)
    cx.S.emit()
    try:
        st.close()
    except AssertionError:
        pass
    return nc


def host_consts():
    ident = np.eye(128, dtype=np.float32)
    i = np.arange(64) % 32
    invf = (np.float32(10000.0) ** (-(2.0 * i).astype(np.float32) / np.float32(64.0))).astype(np.float32)
    sgn = np.where(np.arange(64) < 32, -1.0, 1.0).astype(np.float32)
    return ident, np.stack([invf, sgn], axis=1).astype(np.float32)


def even_in_maps(x, mem, positions, inputs):
    ident, ropec = host_consts()
    maps = []
    for c in range(8):
        b, half = c // 2, c % 2
        own = slice(half * NT, (half + 1) * NT)
        oth = slice((1 - half) * NT, (2 - half) * NT)
        x_kv = np.concatenate([x[b, own], x[b, oth]], axis=0)
        pos_kv = np.concatenate([positions[b, own], positions[b, oth]], axis=0).astype(np.int32)
        xp = np.zeros((SEQ + 30, 1024), np.float32)
        xp[15:15 + SEQ] = x[b]
        x_halo = xp[half * NT: half * NT + NT + 30]
        m = {"x_kv": np.ascontiguousarray(x_kv), "x_halo": np.ascontiguousarray(x_halo), "pos_kv": pos_kv,
             "mem": np.ascontiguousarray(mem[b]), "ident": ident, "ropec": ropec}
        for n, shp in EVEN_W:
            m["e_" + n] = np.ascontiguousarray(inputs["e_" + n][0].reshape(shp))
        maps.append(m)
    return maps


def run_even(x, mem, positions, inputs):
    nc = build_even_nc()
    res = run_bass_kernel_spmd(nc, even_in_maps(x, mem, positions, inputs), core_ids=list(range(8)))
    out = np.zeros((BATCH, SEQ, 1024), np.float32)
    for c in range(8):
        b, half = c // 2, c % 2
        out[b, half * NT:(half + 1) * NT] = res.results[c]["y"]
    return out


ODD_W = [("o_w_in", [1024, 3072]), ("o_d", [1024]), ("o_w_glu", [1024, 2048]), ("o_mem_kv", [1024, 1024]),
         ("o_w_out", [1536, 1024]), ("o_ln_g", [1024]), ("o_ln_b", [1024])]
S5_P = [("a_re", [64, 64]), ("a_im", [64, 64]), ("log_dt", [64]), ("b_re", [64, 64, 16]), ("b_im", [64, 64, 16]),
        ("c_re", [64, 16, 64]), ("c_im", [64, 16, 64])]


def odd_dram(nc, d):
    d["mem"] = d.get("mem") or nc.dram_tensor("mem", [256, 1024], F32, kind="ExternalInput").ap()
    for n, shp in ODD_W:
        d[n] = nc.dram_tensor(n, shp, F32, kind="ExternalInput").ap()
    for n, shp in S5_P:
        for dr in ("A", "B"):
            d["s_%s_%s" % (n, dr)] = nc.dram_tensor("s_%s_%s" % (n, dr), shp, F32, kind="ExternalInput").ap()
    d["s5e"] = nc.dram_tensor("s5e", [128, 28], F32, kind="ExternalInput").ap()
    d["tmask"] = nc.dram_tensor("tmask", [128, 2, 128], F32, kind="ExternalInput").ap()


def build_odd_nc():
    nc = bass.Bass("TRN2", target_bir_lowering=False)
    d = {}
    d["h_seq"] = nc.dram_tensor("h_seq", [SEQ, 1024], F32, kind="ExternalInput").ap()
    d["ident"] = nc.dram_tensor("ident", [128, 128], F32, kind="ExternalInput").ap()
    odd_dram(nc, d)
    d["y"] = nc.dram_tensor("y", [NT, 1024], F32, kind="ExternalOutput").ap()
    st = contextlib.ExitStack()
    cx = Ctx(nc, st)
    consts = load_consts(cx, st, d)
    build_odd(nc, cx, d, consts)
    cx.S.emit()
    try:
        st.close()
    except AssertionError:
        pass
    return nc


def odd_consts():
    s5e = np.zeros((128, 28), np.float32)
    i = np.arange(8, dtype=np.float32)
    s5e[0:64, 0:8] = 7 - i
    s5e[64:128, 0:8] = i
    s5e[0:64, 8:16] = i - 7
    s5e[64:128, 8:16] = -i
    s5e[0:64, 16:24] = i + 1
    s5e[64:128, 16:24] = 8 - i
    s5e[0:64, 24] = 1.0
    s5e[64:128, 25] = 1.0
    s5e[0:64, 26] = -1.0
    s5e[64:128, 27] = -1.0
    s_idx = np.arange(128) // 16
    tmask = np.zeros((128, 2, 128), np.float32)
    tmask[:, 0, :] = (s_idx[None, :] >= s_idx[:, None])
    tmask[:, 1, :] = (s_idx[:, None] >= s_idx[None, :])
    return s5e, tmask


def odd_param_maps(inputs, half):
    s5e, tmask = odd_consts()
    m = {"s5e": s5e, "tmask": tmask}
    for n, shp in ODD_W:
        m[n] = np.ascontiguousarray(inputs[n][0].reshape(shp))
    dirs = ("f", "b") if half == 0 else ("b", "f")
    for n, shp in S5_P:
        for dr, src in zip(("A", "B"), dirs):
            m["s_%s_%s" % (n, dr)] = np.ascontiguousarray(inputs["o_%s_%s" % (n, src)][0].reshape(shp))
    return m


def run_odd(h, mem, inputs):
    nc = build_odd_nc()
    ident, _ = host_consts()
    maps = []
    for c in range(8):
        b, half = c // 2, c % 2
        hs = h[b] if half == 0 else h[b][::-1]
        m = {"h_seq": np.ascontiguousarray(hs), "mem": np.ascontiguousarray(mem[b]), "ident": ident}
        m.update(odd_param_maps(inputs, half))
        maps.append(m)
    res = run_bass_kernel_spmd(nc, maps, core_ids=list(range(8)))
    out = np.zeros((BATCH, SEQ, 1024), np.float32)
    for c in range(8):
        b, half = c // 2, c % 2
        y = res.results[c]["y"]
        if half == 0:
            out[b, 0:NT] = y
        else:
            out[b, NT:SEQ] = y[::-1]
    return out


PAIRS = [[0, 1], [2, 3], [4, 5], [6, 7]]
ALL_INPUT_NAMES = ["x", "mem", "positions", "e_w_in", "e_conv_w", "e_conv_b", "e_conv_ln_g", "e_conv_ln_b", "e_q_norm",
                   "e_w_uq", "e_kv_norm", "e_w_ukv", "e_mem_kv", "e_w_out", "e_ln_g", "e_ln_b", "o_w_in",
                   "o_a_re_f", "o_a_im_f", "o_log_dt_f", "o_b_re_f", "o_b_im_f", "o_c_re_f", "o_c_im_f",
                   "o_a_re_b", "o_a_im_b", "o_log_dt_b", "o_b_re_b", "o_b_im_b", "o_c_re_b", "o_c_im_b",
                   "o_d", "o_w_glu", "o_mem_kv", "o_w_out", "o_ln_g", "o_ln_b"]


def build_fused_nc():
    nc = bass.Bass("TRN2", target_bir_lowering=False)
    d = {}
    d["x_kv"] = nc.dram_tensor("x_kv", [SEQ, 1024], F32, kind="ExternalInput").ap()
    d["x_halo"] = nc.dram_tensor("x_halo", [NT + 30, 1024], F32, kind="ExternalInput").ap()
    d["pos_kv"] = nc.dram_tensor("pos_kv", [SEQ], I32, kind="ExternalInput").ap()
    d["mem"] = nc.dram_tensor("mem", [256, 1024], F32, kind="ExternalInput").ap()
    d["ident"] = nc.dram_tensor("ident", [128, 128], F32, kind="ExternalInput").ap()
    d["rident"] = nc.dram_tensor("rident", [128, 128], F32, kind="ExternalInput").ap()
    d["ropec"] = nc.dram_tensor("ropec", [64, 2], F32, kind="ExternalInput").ap()
    for n, shp in EVEN_W:
        d[n] = nc.dram_tensor("e_" + n, shp, F32, kind="ExternalInput").ap()
    odd_dram(nc, d)
    h1 = nc.dram_tensor("h1_bounce", [NT, 1024], F32).ap()
    hsum = nc.dram_tensor("hsum_bounce", [NT, 1024], F32).ap()
    yout = nc.dram_tensor("y", [NT, 1024], F32, kind="ExternalOutput").ap()
    h1B = Buf("h1")
    hsumB = Buf("hsum")
    st = contextlib.ExitStack()
    cx = Ctx(nc, st)
    consts = load_consts(cx, st, d)
    d["y"] = h1
    d["yB"] = h1B
    d["w_in_bf"] = nc.dram_tensor("wbf_in", [1024, 6208], BF16, kind="Internal").ap()
    d["w_out_bf"] = nc.dram_tensor("wbf_out", [2560, 1024], BF16, kind="Internal").ap()
    d["wbfB"] = Buf("wbf")
    a8 = cx.sb(st, "a8", [128, 2, NG], F32)
    d["a8"] = (a8, Buf("a8"))
    d["scr"] = make_scr(nc)
    preB = Buf("dram_in_pre")

    def hook_after_mla(stA):
        d["s5pre"] = s5_prefetch(cx, stA, d, consts, preB, persist=True)

    def hook_end():
        s5_tables(cx, d, consts, preB, d["a8"][0], d["a8"][1], d["scr"], pre=d["s5pre"])
    d["hook_after_mla"] = hook_after_mla
    d["hook_end"] = hook_end
    build_even(nc, cx, d, consts)
    d["y"] = yout
    d["h1"] = h1
    d["h1B"] = h1B
    d["hsum"] = hsum
    d["hsumB"] = hsumB
    build_odd(nc, cx, d, consts)
    cx.S.emit()
    try:
        st.close()
    except AssertionError:
        pass
    return nc


def fused_in_maps(x, mem, positions, inputs):
    ident, ropec = host_consts()
    rident = np.ascontiguousarray(ident[::-1])
    maps = []
    for c in range(8):
        b, half = c // 2, c % 2
        own = slice(half * NT, (half + 1) * NT)
        oth = slice((1 - half) * NT, (2 - half) * NT)
        xp = np.zeros((SEQ + 30, 1024), np.float32)
        xp[15:15 + SEQ] = x[b]
        x_halo = xp[half * NT: half * NT + NT + 30]
        xo, xt = x[b, own], x[b, oth]
        po, pt = positions[b, own], positions[b, oth]
        cw = inputs["e_conv_w"][0].reshape(31, 1024)
        if half == 1:
            x_halo, xo, xt, po, pt, cw = x_halo[::-1], xo[::-1], xt[::-1], po[::-1], pt[::-1], cw[::-1]
        m = {"x_kv": np.ascontiguousarray(np.concatenate([xo, xt], axis=0)), "x_halo": np.ascontiguousarray(x_halo),
             "pos_kv": np.ascontiguousarray(np.concatenate([po, pt], axis=0)).astype(np.int32),
             "mem": np.ascontiguousarray(mem[b]), "ident": ident, "rident": rident, "ropec": ropec}
        for n, shp in EVEN_W:
            m["e_" + n] = np.ascontiguousarray(inputs["e_" + n][0].reshape(shp))
        m["e_conv_w"] = np.ascontiguousarray(cw)
        m.update(odd_param_maps(inputs, half))
        maps.append(m)
    return maps


def kernel(**inputs):
    inputs = {k: np.asarray(inputs[k]) for k in ALL_INPUT_NAMES}
    x = inputs["x"].astype(np.float32)
    mem = inputs["mem"].astype(np.float32)
    pos = inputs["positions"]
    nc = build_fused_nc()
    res = run_bass_kernel_spmd(nc, fused_in_maps(x, mem, pos, inputs), core_ids=list(range(8)))
    out = np.zeros((BATCH, SEQ, 1024), np.float32)
    for c in range(8):
        b, half = c // 2, c % 2
        y = res.results[c]["y"]
        if half == 0:
            out[b, 0:NT] = y
        else:
            out[b, NT:SEQ] = y[::-1]
    return out
```
